# Optimizing a Trainium2 kernel written in Bass

```python
import jax, jax.numpy as jnp
from jax import lax
import numpy as np

D_MODEL = 2048
BATCH = 2
SEQ = 8192
DEPTH = 2

GRID_W = 64
Q_BLOCK = 128
NORM_EPS = 1e-6
ROPE_THETA = 500000.0
AXIAL_THETA = 10000.0
MLA_HEADS = 8
MLA_Q_LORA = 512
MLA_KV_LORA = 256
MLA_NOPE_DIM = 128
MLA_ROPE_DIM = 64
MLA_QK_DIM = MLA_NOPE_DIM + MLA_ROPE_DIM
MLA_V_DIM = 128
GQA_HEADS = 8
GQA_KV_HEADS = 2
GQA_HEAD_DIM = 128
SWA_HEADS = 32
SWA_KV_HEADS = 4
SWA_HEAD_DIM = 64
SWA_WINDOW = 128
SWA_ROT_DIM = SWA_HEAD_DIM // 4
D_FF = 4 * D_MODEL
EVEN_IN = MLA_Q_LORA + MLA_KV_LORA + MLA_ROPE_DIM + (GQA_HEADS + 2 * GQA_KV_HEADS) * GQA_HEAD_DIM
EVEN_OUT = MLA_HEADS * MLA_V_DIM + GQA_HEADS * GQA_HEAD_DIM
ODD_IN = (SWA_HEADS + 2 * SWA_KV_HEADS) * SWA_HEAD_DIM
ODD_OUT = SWA_HEADS * SWA_HEAD_DIM
N_EVEN = (DEPTH + 1) // 2
N_ODD = DEPTH // 2

kernel_name = 'hybrid_mla_gridgqa_swa_sqrelu_encoder'


def rms_norm(x, gain):
    xf = x.astype(jnp.float32)
    y = xf * lax.rsqrt(jnp.mean(xf * xf, axis=-1, keepdims=True) + NORM_EPS)
    return (y * gain.astype(jnp.float32)).astype(x.dtype)


def rope_table(pos, dim, theta):
    inv = jnp.float32(theta) ** (-jnp.arange(0, dim, 2, dtype=jnp.float32) / dim)
    ang = pos.astype(jnp.float32)[:, None] * inv[None, :]
    return jnp.cos(ang), jnp.sin(ang)


def apply_rope(x, cos, sin):
    half = x.shape[-1] // 2
    c = cos[None, :, None, :].astype(x.dtype)
    s = sin[None, :, None, :].astype(x.dtype)
    x1 = x[..., :half]
    x2 = x[..., half:]
    return jnp.concatenate([x1 * c - x2 * s, x2 * c + x1 * s], axis=-1)


def dense_block_attention(q, k, v, scale):
    B, S, Hq, dk = q.shape
    Hkv = k.shape[2]
    G = Hq // Hkv
    dv = v.shape[-1]
    nb = S // Q_BLOCK
    qb = jnp.swapaxes(q.reshape(B, nb, Q_BLOCK, Hkv, G, dk), 0, 1)

    def one_block(qblk):
        s = jnp.einsum('bqhgd,bkhd->bhgqk', qblk, k, preferred_element_type=jnp.float32) * scale
        p = jax.nn.softmax(s, axis=-1).astype(v.dtype)
        return jnp.einsum('bhgqk,bkhd->bqhgd', p, v)

    o = lax.map(one_block, qb)
    return jnp.swapaxes(o, 0, 1).reshape(B, S, Hq, dv)


def banded_window_attention(q, k, v, sink, scale):
    B, S, Hq, d = q.shape
    Hkv = k.shape[2]
    G = Hq // Hkv
    nb = S // Q_BLOCK
    span = Q_BLOCK + 2 * SWA_WINDOW
    pad = ((0, 0), (SWA_WINDOW, SWA_WINDOW), (0, 0), (0, 0))
    kp = jnp.pad(k, pad)
    vp = jnp.pad(v, pad)
    qb = jnp.swapaxes(q.reshape(B, nb, Q_BLOCK, Hkv, G, d), 0, 1)
    sink_b = sink.astype(jnp.float32).reshape(1, Hkv, G, 1, 1)
    offs_q = jnp.arange(Q_BLOCK)
    offs_k = jnp.arange(span) - SWA_WINDOW

    def one_block(args):
        i, qblk = args
        start = i * Q_BLOCK
        kblk = lax.dynamic_slice_in_dim(kp, start, span, axis=1)
        vblk = lax.dynamic_slice_in_dim(vp, start, span, axis=1)
        q_pos = start + offs_q
        k_pos = start + offs_k
        valid = (jnp.abs(q_pos[:, None] - k_pos[None, :]) <= SWA_WINDOW) & ((k_pos >= 0) & (k_pos < S))[None, :]
        s = jnp.einsum('bqhgd,bkhd->bhgqk', qblk, kblk, preferred_element_type=jnp.float32) * scale
        s = jnp.where(valid, s, -jnp.inf)
        m = jnp.maximum(jnp.max(s, axis=-1, keepdims=True), sink_b)
        p = jnp.exp(s - m)
        denom = jnp.sum(p, axis=-1, keepdims=True) + jnp.exp(sink_b - m)
        p = (p / denom).astype(v.dtype)
        return jnp.einsum('bhgqk,bkhd->bqhgd', p, vblk)

    o = lax.map(one_block, (jnp.arange(nb), qb))
    return jnp.swapaxes(o, 0, 1).reshape(B, S, Hq, d)


def even_mixer(h, w_in, q_lat_norm, kv_lat_norm, w_uq, w_ukv, q_norm, k_nope_norm, k_rope_norm,
               g_q_norm, g_k_norm, w_out, mla_cos, mla_sin, row_cos, row_sin, col_cos, col_sin):
    B, S, _ = h.shape
    proj = h @ w_in
    o1 = MLA_Q_LORA
    o2 = o1 + MLA_KV_LORA
    o3 = o2 + MLA_ROPE_DIM
    o4 = o3 + GQA_HEADS * GQA_HEAD_DIM
    o5 = o4 + GQA_KV_HEADS * GQA_HEAD_DIM
    c_q = proj[..., :o1]
    c_kv = proj[..., o1:o2]
    k_rope = proj[..., o2:o3]
    q_g = proj[..., o3:o4].reshape(B, S, GQA_HEADS, GQA_HEAD_DIM)
    k_g = proj[..., o4:o5].reshape(B, S, GQA_KV_HEADS, GQA_HEAD_DIM)
    v_g = proj[..., o5:].reshape(B, S, GQA_KV_HEADS, GQA_HEAD_DIM)

    q_a = (rms_norm(c_q, q_lat_norm) @ w_uq).reshape(B, S, MLA_HEADS, MLA_QK_DIM)
    q_a = rms_norm(q_a, q_norm)
    q_a = jnp.concatenate([q_a[..., :MLA_NOPE_DIM], apply_rope(q_a[..., MLA_NOPE_DIM:], mla_cos, mla_sin)], axis=-1)
    kv = (rms_norm(c_kv, kv_lat_norm) @ w_ukv).reshape(B, S, MLA_HEADS, MLA_NOPE_DIM + MLA_V_DIM)
    k_nope = rms_norm(kv[..., :MLA_NOPE_DIM], k_nope_norm)
    v_a = kv[..., MLA_NOPE_DIM:]
    k_r = apply_rope(rms_norm(k_rope, k_rope_norm)[:, :, None, :], mla_cos, mla_sin)
    k_a = jnp.concatenate([k_nope, jnp.broadcast_to(k_r, (B, S, MLA_HEADS, MLA_ROPE_DIM))], axis=-1)
    o_a = dense_block_attention(q_a, k_a, v_a, MLA_QK_DIM ** -0.5)

    half = GQA_HEAD_DIM // 2
    def axial(t):
        return jnp.concatenate([apply_rope(t[..., :half], row_cos, row_sin), apply_rope(t[..., half:], col_cos, col_sin)], axis=-1)
    q_g = axial(rms_norm(q_g, g_q_norm))
    k_g = axial(rms_norm(k_g, g_k_norm))
    o_g = dense_block_attention(q_g, k_g, v_g, GQA_HEAD_DIM ** -0.5)

    merged = jnp.concatenate([o_a.reshape(B, S, -1), o_g.reshape(B, S, -1)], axis=-1)
    return merged @ w_out


def odd_mixer(h, w_qkv, q_norm, k_norm, sink, w_out, swa_cos, swa_sin):
    B, S, _ = h.shape
    qkv = h @ w_qkv
    nq = SWA_HEADS * SWA_HEAD_DIM
    nk = SWA_KV_HEADS * SWA_HEAD_DIM
    q = rms_norm(qkv[..., :nq].reshape(B, S, SWA_HEADS, SWA_HEAD_DIM), q_norm)
    k = rms_norm(qkv[..., nq:nq + nk].reshape(B, S, SWA_KV_HEADS, SWA_HEAD_DIM), k_norm)
    v = qkv[..., nq + nk:].reshape(B, S, SWA_KV_HEADS, SWA_HEAD_DIM)
    q = jnp.concatenate([apply_rope(q[..., :SWA_ROT_DIM], swa_cos, swa_sin), q[..., SWA_ROT_DIM:]], axis=-1)
    k = jnp.concatenate([apply_rope(k[..., :SWA_ROT_DIM], swa_cos, swa_sin), k[..., SWA_ROT_DIM:]], axis=-1)
    o = banded_window_attention(q, k, v, sink, SWA_HEAD_DIM ** -0.5)
    return o.reshape(B, S, -1) @ w_out


def squared_relu_mlp(x, gain, w_up, w_down):
    h = rms_norm(x, gain) @ w_up
    return jnp.square(jax.nn.relu(h)) @ w_down


def _dense(k, shape):
    return jax.random.normal(k, shape, jnp.float32) * (shape[-2] ** -0.5)


def _gain(k, shape):
    return 1.0 + 0.02 * jax.random.normal(k, shape, jnp.float32)


def setup_inputs(seed: int = 0) -> dict:
    key = jax.random.key(seed)
    ks = jax.random.split(key, 22)
    return {
        'x': jax.random.normal(ks[0], (BATCH, SEQ, D_MODEL), jnp.float32),
        'even_norm': _gain(ks[1], (N_EVEN, D_MODEL)),
        'even_w_in': _dense(ks[2], (N_EVEN, D_MODEL, EVEN_IN)),
        'mla_q_lat_norm': _gain(ks[3], (N_EVEN, MLA_Q_LORA)),
        'mla_kv_lat_norm': _gain(ks[4], (N_EVEN, MLA_KV_LORA)),
        'mla_w_uq': _dense(ks[5], (N_EVEN, MLA_Q_LORA, MLA_HEADS * MLA_QK_DIM)),
        'mla_w_ukv': _dense(ks[6], (N_EVEN, MLA_KV_LORA, MLA_HEADS * (MLA_NOPE_DIM + MLA_V_DIM))),
        'mla_q_norm': _gain(ks[7], (N_EVEN, MLA_QK_DIM)),
        'mla_k_nope_norm': _gain(ks[8], (N_EVEN, MLA_NOPE_DIM)),
        'mla_k_rope_norm': _gain(ks[9], (N_EVEN, MLA_ROPE_DIM)),
        'gqa_q_norm': _gain(ks[10], (N_EVEN, GQA_HEAD_DIM)),
        'gqa_k_norm': _gain(ks[11], (N_EVEN, GQA_HEAD_DIM)),
        'even_w_out': _dense(ks[12], (N_EVEN, EVEN_OUT, D_MODEL)),
        'odd_norm': _gain(ks[13], (N_ODD, D_MODEL)),
        'odd_w_qkv': _dense(ks[14], (N_ODD, D_MODEL, ODD_IN)),
        'swa_q_norm': _gain(ks[15], (N_ODD, SWA_HEAD_DIM)),
        'swa_k_norm': _gain(ks[16], (N_ODD, SWA_HEAD_DIM)),
        'swa_sink': jax.random.normal(ks[17], (N_ODD, SWA_HEADS), jnp.float32),
        'odd_w_out': _dense(ks[18], (N_ODD, ODD_OUT, D_MODEL)),
        'mlp_norm': _gain(ks[19], (DEPTH, D_MODEL)),
        'mlp_w_up': _dense(ks[20], (DEPTH, D_MODEL, D_FF)),
        'mlp_w_down': _dense(ks[21], (DEPTH, D_FF, D_MODEL)),
    }


def reference(x, even_norm, even_w_in, mla_q_lat_norm, mla_kv_lat_norm, mla_w_uq, mla_w_ukv,
              mla_q_norm, mla_k_nope_norm, mla_k_rope_norm, gqa_q_norm, gqa_k_norm, even_w_out,
              odd_norm, odd_w_qkv, swa_q_norm, swa_k_norm, swa_sink, odd_w_out,
              mlp_norm, mlp_w_up, mlp_w_down):
    B, S, _ = x.shape
    rows = S // GRID_W
    pos = jnp.arange(S)
    row_pos = jnp.repeat(jnp.arange(rows), GRID_W)
    col_pos = jnp.tile(jnp.arange(GRID_W), rows)
    mla_cos, mla_sin = rope_table(pos, MLA_ROPE_DIM, ROPE_THETA)
    row_cos, row_sin = rope_table(row_pos, GQA_HEAD_DIM // 2, AXIAL_THETA)
    col_cos, col_sin = rope_table(col_pos, GQA_HEAD_DIM // 2, AXIAL_THETA)
    swa_cos, swa_sin = rope_table(pos, SWA_ROT_DIM, ROPE_THETA)
    for layer in range(DEPTH):
        i = layer // 2
        if layer % 2 == 0:
            x = x + even_mixer(rms_norm(x, even_norm[i]), even_w_in[i], mla_q_lat_norm[i], mla_kv_lat_norm[i],
                               mla_w_uq[i], mla_w_ukv[i], mla_q_norm[i], mla_k_nope_norm[i], mla_k_rope_norm[i],
                               gqa_q_norm[i], gqa_k_norm[i], even_w_out[i],
                               mla_cos, mla_sin, row_cos, row_sin, col_cos, col_sin)
        else:
            x = x + odd_mixer(rms_norm(x, odd_norm[i]), odd_w_qkv[i], swa_q_norm[i], swa_k_norm[i],
                              swa_sink[i], odd_w_out[i], swa_cos, swa_sin)
        x = x + squared_relu_mlp(x, mlp_norm[layer], mlp_w_up[layer], mlp_w_down[layer])
    return x
```

```python
import numpy as np
from contextlib import ExitStack
import concourse.bass as bass
import concourse.mybir as mybir
from concourse.bass_utils import run_bass_kernel_spmd

F32 = mybir.dt.float32
BF16 = mybir.dt.bfloat16
AF = mybir.ActivationFunctionType
ALU = mybir.AluOpType

D = 2048
S = 8192
EXT = 2304
OWN = 2048
EPS = 1e-6
ENGS = ("pe", "act", "dve", "pool", "sp")
ENGATTR = {"pe": "tensor", "act": "scalar", "dve": "vector", "pool": "gpsimd", "sp": "sync"}


class Buf:
    __slots__ = ("name", "last_w", "readers", "ndma", "sem", "id")
    _n = 0

    def __init__(self, name="b"):
        self.name = name
        self.last_w = None
        self.readers = []
        self.ndma = 0
        self.sem = None
        Buf._n += 1
        self.id = Buf._n


class TB:
    __slots__ = ("t", "b")

    def __init__(self, t, name="b"):
        self.t = t
        self.b = Buf(name)


class Rot:
    def __init__(self, items):
        self.items = items
        self.i = 0

    def next(self):
        it = self.items[self.i % len(self.items)]
        self.i += 1
        return it


class Sems:
    def __init__(self, nc, es):
        self.nc = nc
        self.es = es
        self.eng = {e: [es.enter_context(nc.semaphore("prog_" + e)), 0] for e in ENGS}
        self.free = []
        self.n = 0

    def get_dma(self):
        if self.free:
            return self.free.pop()
        self.n += 1
        return [self.es.enter_context(self.nc.semaphore("dsem%d" % self.n)), 0]


class Op:
    __slots__ = ("eng", "fn", "waits", "signal", "dma_buf", "idx")

    def __init__(self, eng, fn):
        self.eng = eng
        self.fn = fn
        self.waits = []
        self.signal = False
        self.dma_buf = None
        self.idx = None


class Prog:
    def __init__(self, nc, sems):
        self.nc = nc
        self.sems = sems
        self.ops = {e: [] for e in ENGS}
        self.waited = {e: {} for e in ENGS}
        self.dma_bufs = []

    def _need(self, op, tok):
        if tok is None:
            return
        w = self.waited[op.eng]
        if tok[0] == "c":
            if tok[1] == op.eng and op.dma_buf is None:
                return
            key = tok[1]
            if w.get(key, -1) >= tok[2]:
                return
            w[key] = tok[2]
            op.waits.append(tok)
            self.ops[tok[1]][tok[2]].signal = True
        else:
            key = ("d", tok[1].id)
            if w.get(key, -1) >= tok[2]:
                return
            w[key] = tok[2]
            op.waits.append(tok)

    def op(self, eng, fn, reads=(), writes=(), dma=None):
        o = Op(eng, fn)
        o.idx = len(self.ops[eng])
        o.dma_buf = dma
        for b in reads:
            self._need(o, b.last_w)
        for b in writes:
            self._need(o, b.last_w)
            for t in b.readers:
                self._need(o, t)
        if dma is not None:
            if dma.ndma == 0 and dma.sem is None:
                self.dma_bufs.append(dma)
            dma.ndma += 1
            tok = ("d", dma, dma.ndma)
        else:
            tok = ("c", eng, o.idx)
        for b in reads:
            b.readers.append(tok)
        for b in writes:
            b.last_w = tok
            b.readers = []
        self.ops[eng].append(o)
        return tok

    def emit(self):
        nc = self.nc
        sems = self.sems
        for b in self.dma_bufs:
            b.sem = sems.get_dma()
        for e in ENGS:
            for o in reversed(self.ops[e]):
                if o.dma_buf is None:
                    o.signal = True
                    break
        sigval = {}
        final = {}
        for e in ENGS:
            c = sems.eng[e][1]
            for o in self.ops[e]:
                if o.signal and o.dma_buf is None:
                    c += 1
                    sigval[(e, o.idx)] = c
            final[e] = c

        def lower(tok):
            if tok[0] == "c":
                return sems.eng[tok[1]][0], sigval[(tok[1], tok[2])]
            b = tok[1]
            return b.sem[0], 16 * (b.sem[1] + tok[2])

        sp_final = final["sp"] + 1
        with nc.Block() as block:
            def make(e):
                def body(eng):
                    for o in self.ops[e]:
                        for t in o.waits:
                            s, v = lower(t)
                            eng.wait_ge(s, v)
                        ins = o.fn(eng)
                        if o.dma_buf is not None:
                            ins.then_inc(o.dma_buf.sem[0], 16)
                        elif o.signal:
                            ins.then_inc(sems.eng[e][0], 1)
                    if e == "sp":
                        for e2 in ENGS:
                            if e2 != "sp" and final[e2] > 0:
                                eng.wait_ge(sems.eng[e2][0], final[e2])
                        for b in self.dma_bufs:
                            eng.wait_ge(b.sem[0], 16 * (b.sem[1] + b.ndma))
                        eng.sem_inc(sems.eng["sp"][0], 1)
                    else:
                        eng.wait_ge(sems.eng["sp"][0], sp_final)
                return body
            for e in ENGS:
                getattr(block, ENGATTR[e])(make(e))
        for e in ENGS:
            sems.eng[e][1] = final[e]
        sems.eng["sp"][1] = sp_final
        for b in self.dma_bufs:
            b.sem[1] += b.ndma
            sems.free.append(b.sem)
            b.sem = None


class Ctx:
    def __init__(self, nc, sems, tag):
        self.nc = nc
        self.es = ExitStack()
        self.P = Prog(nc, sems)
        self.tag = tag
        self.k = 0

    def sb(self, shape, dt, name="t"):
        self.k += 1
        t = self.es.enter_context(self.nc.sbuf_tensor("%s_%s%d" % (self.tag, name, self.k), list(shape), dt))
        return TB(t, name)

    def ps(self, shape, name="p"):
        self.k += 1
        t = self.es.enter_context(self.nc.psum_tensor("%s_%s%d" % (self.tag, name, self.k), list(shape), F32))
        return TB(t, name)

    def close(self):
        self.P.emit()
        self.es.close()


def dma_in(P, q, dst, src_ap, dst_ap=None):
    d = dst.t[:] if dst_ap is None else dst_ap
    P.op(q, lambda e, d=d, s=src_ap: e.dma_start(out=d, in_=s), writes=[dst.b], dma=dst.b)


def dma_out(P, src, dst_ap, src_ap, dram_bufs=(), q="pool"):
    return P.op(q, lambda e, d=dst_ap, s=src_ap: e.dma_start(out=d, in_=s),
                reads=[src.b], writes=list(dram_bufs), dma=src.b)


def mm_group(P, out_ap, out_b, pairs, reads):
    n = len(pairs)

    def fn(e, pairs=pairs, out_ap=out_ap, n=n):
        ins = None
        for i, (l, r) in enumerate(pairs):
            ins = e.matmul(out_ap, lhsT=l, rhs=r, start=(i == 0), stop=(i == n - 1))
        return ins
    P.op("pe", fn, reads=reads, writes=[out_b])


class Common:
    def __init__(self, C, gcols_d, with_bd=False):
        P = C.P
        self.C = C
        self.ones = C.sb([128, 128], BF16, "ones")
        P.op("dve", lambda e: e.memset(self.ones.t[:], 1.0), writes=[self.ones.b])
        self.eps = C.sb([128, 1], F32, "eps")
        P.op("dve", lambda e: e.memset(self.eps.t[:], EPS), writes=[self.eps.b])
        self.g = C.sb([128, 84], F32, "g")
        dma_in(P, "sp", self.g, gcols_d)
        self.sq = Rot([C.sb([128, 512], BF16, "sq") for _ in range(3)])
        self.f32 = Rot([C.sb([128, 512], F32, "f") for _ in range(6)])
        self.rawb = Rot([C.sb([128, 512], BF16, "rawb") for _ in range(2)])
        if with_bd:
            self.bd = C.sb([128, 128], BF16, "bd")
            P.op("dve", lambda e: e.memset(self.bd.t[:], 0.0), writes=[self.bd.b])
            P.op("dve", lambda e: e.memset(self.bd.t[0:64, 0:64], 1.0), writes=[self.bd.b])
            P.op("dve", lambda e: e.memset(self.bd.t[64:128, 64:128], 1.0), writes=[self.bd.b])


def sq_accum(P, K, srcs, ssqs, n, mrows=128, lhs=None):
    ssq = ssqs.next()
    lhs = K.ones if lhs is None else lhs
    m = len(srcs)
    for i, (ap, b, rows) in enumerate(srcs):
        sq = K.sq.next()
        P.op("act", lambda e, o=sq.t[0:rows, 0:n], a=ap: e.activation(out=o, in_=a, func=AF.Square),
             reads=[b], writes=[sq.b])
        P.op("pe", lambda e, o=ssq.t[0:mrows, 0:n], l=lhs.t[0:rows, 0:mrows], r=sq.t[0:rows, 0:n], i=i:
             e.matmul(o, lhsT=l, rhs=r, start=(i == 0), stop=(i == m - 1)),
             reads=[sq.b, lhs.b], writes=[ssq.b])
    return ssq


def rstd_of(P, K, ssq, d, n, rows=128):
    r = K.f32.next()
    P.op("act", lambda e, o=r.t[0:rows, 0:n], a=ssq.t[0:rows, 0:n]:
         e.activation(out=o, in_=a, func=AF.Sqrt, bias=K.eps.t[0:rows, 0:1], scale=1.0 / d),
         reads=[ssq.b, K.eps.b], writes=[r.b])
    P.op("dve", lambda e, o=r.t[0:rows, 0:n]: e.reciprocal(out=o, in_=o), reads=[r.b], writes=[r.b])
    return r


def apply_plain(P, K, src_ap, src_b, gcol, r, out_ap, out_b, n, rows=128, eng="dve"):
    P.op(eng, lambda e: e.scalar_tensor_tensor(out=out_ap, in0=src_ap, scalar=K.g.t[0:rows, gcol:gcol + 1],
                                               in1=r.t[0:rows, 0:n], op0=ALU.mult, op1=ALU.mult),
         reads=[src_b, K.g.b, r.b], writes=[out_b])


def apply_rope(P, K, src, n, rows, perm, gcol, Ctab, Stab, r, out_ap, out_b, psw, eng2="pool"):
    raw = K.rawb.next()
    P.op("act", lambda e: e.activation(out=raw.t[0:rows, 0:n], in_=src.t[0:rows, 0:n], func=AF.Copy),
         reads=[src.b], writes=[raw.b])
    P.op("pe", lambda e: e.matmul(psw.t[0:rows, 0:n], lhsT=perm.t[0:rows, 0:rows], rhs=raw.t[0:rows, 0:n],
                                  start=True, stop=True), reads=[raw.b, perm.b], writes=[psw.b])
    t1 = K.f32.next()
    P.op("dve", lambda e: e.scalar_tensor_tensor(out=t1.t[0:rows, 0:n], in0=src.t[0:rows, 0:n],
                                                 scalar=K.g.t[0:rows, gcol:gcol + 1], in1=Ctab[0],
                                                 op0=ALU.mult, op1=ALU.mult),
         reads=[src.b, K.g.b, Ctab[1]], writes=[t1.b])
    t2 = K.f32.next()
    P.op("dve", lambda e: e.scalar_tensor_tensor(out=t2.t[0:rows, 0:n], in0=psw.t[0:rows, 0:n],
                                                 scalar=K.g.t[0:rows, gcol + 1:gcol + 2], in1=Stab[0],
                                                 op0=ALU.mult, op1=ALU.mult),
         reads=[psw.b, K.g.b, Stab[1]], writes=[t2.b])
    P.op(eng2, lambda e: e.tensor_tensor(out=t1.t[0:rows, 0:n], in0=t1.t[0:rows, 0:n], in1=t2.t[0:rows, 0:n],
                                         op=ALU.add), reads=[t1.b, t2.b], writes=[t1.b])
    P.op(eng2, lambda e: e.tensor_tensor(out=out_ap, in0=t1.t[0:rows, 0:n], in1=r.t[0:rows, 0:n],
                                         op=ALU.mult), reads=[t1.b, r.b], writes=[out_b])


def xnorm(P, K, xt, hT, gbase, n, ssq):
    ssq_ = sq_accum(P, K, [(xt.t[:, k, 0:n], xt.b, 128) for k in range(16)], ssq, n)
    r = rstd_of(P, K, ssq_, float(D), n)
    for k in range(16):
        apply_plain(P, K, xt.t[:, k, 0:n], xt.b, gbase + k, r, hT.t[:, k, 0:n], hT.b, n,
                    eng="dve")


def build_program():
    nc = bass.Bass("TRN2", target_bir_lowering=False)

    def din(name, shape):
        return nc.dram_tensor(name, list(shape), F32, kind="ExternalInput").ap()

    def dscr(name, shape, dt):
        return nc.dram_tensor(name, list(shape), dt, kind="Internal").ap()

    xT = din("xT", [D, S])
    gcols = din("gcols", [128, 84])
    w_kv = din("w_kv", [128, 16 * 832])
    w_q0 = din("w_q0", [128, 16 * 1536])
    w_uq = din("w_uq", [128, 4 * 1536])
    w_uk = din("w_uk", [128, 2 * 1024])
    w_uv = din("w_uv", [128, 2 * 1024])
    w_o0 = din("w_o0", [4, 128, 8192])
    w_up = din("w_up", [2, 16, 128, 8192])
    w_dn = din("w_dn", [2, 2, 8, 128, 8192])
    w_qkv1 = din("w_qkv1", [5, 128, 8192])
    w_o1 = din("w_o1", [4, 128, 8192])
    t_mla = din("t_mla", [2, 64, S])
    t_gqa = din("t_gqa", [2, 128, S])
    t_swa = din("t_swa", [2, 128, EXT])
    perms = din("perms", [3, 128, 128])
    masks = din("masks", [4, 128, 512])
    sinkrep = din("sinkrep", [64, 32])
    yT = nc.dram_tensor("yT", [D, OWN], F32, kind="ExternalOutput").ap()

    kn_d = dscr("kn_d", [8, 128, S], BF16)
    kr_d = dscr("kr_d", [64, S], BF16)
    kg_d = dscr("kg_d", [2, 128, S], BF16)
    va_d = dscr("va_d", [8, 128, 64, 128], BF16)
    vg_d = dscr("vg_d", [2, 128, 64, 128], BF16)
    qn_d = dscr("qn_d", [8, 128, EXT], BF16)
    qr_d = dscr("qr_d", [8, 64, EXT], BF16)
    qg_d = dscr("qg_d", [8, 128, EXT], BF16)
    mg_d = dscr("mg_d", [D, EXT], BF16)
    x1_d = dscr("x1_d", [D, EXT], F32)
    q1_d = dscr("q1_d", [4, 64, 18, 8, 128], BF16)
    k1_d = dscr("k1_d", [4, 64, EXT], BF16)
    v1_d = dscr("v1_d", [128, 18, 256], BF16)
    m1_d = dscr("m1_d", [D, OWN], BF16)
    wb_o0 = dscr("wb_o0", [4, 128, 8192], BF16)
    wb_up = dscr("wb_up", [2, 16, 128, 8192], BF16)
    wb_dn = dscr("wb_dn", [2, 2, 8, 128, 8192], BF16)
    wb_qkv1 = dscr("wb_qkv1", [5, 128, 8192], BF16)
    wb_o1 = dscr("wb_o1", [4, 128, 8192], BF16)

    es = ExitStack()
    sems = Sems(nc, es)
    xT_v = xT.rearrange("(k p) t -> p k t", p=128)

    C = Ctx(nc, sems, "p0")
    P = C.P
    K = Common(C, gcols)
    wkv = C.sb([128, 16, 832], BF16, "wkv")
    dma_in(P, "pool", wkv, w_kv.rearrange("p (k c) -> p k c", k=16))
    wuk = C.sb([128, 2, 1024], BF16, "wuk")
    dma_in(P, "pool", wuk, w_uk.rearrange("p (k c) -> p k c", k=2))
    wuv = C.sb([128, 2, 1024], BF16, "wuv")
    dma_in(P, "pool", wuv, w_uv.rearrange("p (k c) -> p k c", k=2))
    pm = C.sb([128, 3, 128], BF16, "pm")
    dma_in(P, "pool", pm, perms.rearrange("a p m -> p a m"))
    pm_mla = TB(pm.t[:, 0, :]); pm_mla.b = pm.b
    pm_gqa = TB(pm.t[:, 1, :]); pm_gqa.b = pm.b
    xts = Rot([C.sb([128, 16, 512], F32, "xt") for _ in range(2)])
    hT = C.sb([128, 16, 512], BF16, "hT")
    tabs = Rot([(C.sb([64, 2, 512], F32, "tm"), C.sb([128, 2, 512], F32, "tg")) for _ in range(2)])
    ckvn = C.sb([128, 2, 512], BF16, "ckvn")
    kst = Rot([(C.sb([128, 8, 512], BF16, "knst"), C.sb([64, 512], BF16, "krst"),
                C.sb([128, 2, 512], BF16, "kgst"), C.sb([128, 4, 1024], BF16, "vast"),
                C.sb([128, 4, 256], BF16, "vgst")) for _ in range(2)])
    ssq = Rot([C.ps([128, 512], "ssq") for _ in range(2)])
    prot = Rot([C.ps([128, 512], "pr") for _ in range(6)])
    for t in range(16):
        c0 = t * 512
        xt = xts.next()
        dma_in(P, "sp", xt, xT_v[:, :, c0:c0 + 512])
        tm, tg = tabs.next()
        dma_in(P, "sp", tm, t_mla[:, :, c0:c0 + 512].rearrange("a p t -> p a t"))
        dma_in(P, "sp", tg, t_gqa[:, :, c0:c0 + 512].rearrange("a p t -> p a t"))
        xnorm(P, K, xt, hT, 0, 512, ssq)
        knst, krst, kgst, vast, vgst = kst.next()
        pck = [prot.next(), prot.next()]
        for j in range(2):
            mm_group(P, pck[j].t[:, :], pck[j].b,
                     [(wkv.t[:, kc, j * 128:(j + 1) * 128], hT.t[:, kc, :]) for kc in range(16)], [wkv.b, hT.b])
        ssq_ = sq_accum(P, K, [(pck[j].t[:, :], pck[j].b, 128) for j in range(2)], ssq, 512)
        r = rstd_of(P, K, ssq_, 256.0, 512)
        for j in range(2):
            apply_plain(P, K, pck[j].t[:, :], pck[j].b, 68 + j, r, ckvn.t[:, j, :], ckvn.b, 512)
        for h in range(8):
            pk = prot.next()
            mm_group(P, pk.t[:, :], pk.b,
                     [(wuk.t[:, kc, h * 128:(h + 1) * 128], ckvn.t[:, kc, :]) for kc in range(2)], [wuk.b, ckvn.b])
            ssq_ = sq_accum(P, K, [(pk.t[:, :], pk.b, 128)], ssq, 512)
            r = rstd_of(P, K, ssq_, 128.0, 512)
            apply_plain(P, K, pk.t[:, :], pk.b, 73, r, knst.t[:, h, :], knst.b, 512)
        for s in range(4):
            for hf in range(2):
                pv = prot.next()
                mm_group(P, pv.t[:, :], pv.b,
                         [(ckvn.t[:, kc, s * 128:(s + 1) * 128], wuv.t[:, kc, hf * 512:(hf + 1) * 512]) for kc in range(2)],
                         [wuv.b, ckvn.b])
                eng = "act" if (s * 2 + hf) % 2 == 0 else "dve"
                if eng == "act":
                    P.op("act", lambda e, o=vast.t[:, s, hf * 512:(hf + 1) * 512], a=pv.t[:, :]:
                         e.activation(out=o, in_=a, func=AF.Copy), reads=[pv.b], writes=[vast.b])
                else:
                    P.op("dve", lambda e, o=vast.t[:, s, hf * 512:(hf + 1) * 512], a=pv.t[:, :]:
                         e.tensor_copy(out=o, in_=a), reads=[pv.b], writes=[vast.b])
        pk = prot.next()
        mm_group(P, pk.t[0:64, :], pk.b, [(wkv.t[:, kc, 256:320], hT.t[:, kc, :]) for kc in range(16)], [wkv.b, hT.b])
        ssq_ = sq_accum(P, K, [(pk.t[0:64, :], pk.b, 64)], ssq, 512, mrows=64)
        r = rstd_of(P, K, ssq_, 64.0, 512, rows=64)
        psw = prot.next()
        apply_rope(P, K, pk, 512, 64, pm_mla, 74, (tm.t[:, 0, :], tm.b), (tm.t[:, 1, :], tm.b), r,
                   krst.t[:, :], krst.b, psw)
        for j in range(2):
            pk = prot.next()
            mm_group(P, pk.t[:, :], pk.b,
                     [(wkv.t[:, kc, 320 + j * 128:320 + (j + 1) * 128], hT.t[:, kc, :]) for kc in range(16)],
                     [wkv.b, hT.b])
            ssq_ = sq_accum(P, K, [(pk.t[:, :], pk.b, 128)], ssq, 512)
            r = rstd_of(P, K, ssq_, 128.0, 512)
            psw = prot.next()
            apply_rope(P, K, pk, 512, 128, pm_gqa, 78, (tg.t[:, 0, :], tg.b), (tg.t[:, 1, :], tg.b), r,
                       kgst.t[:, j, :], kgst.b, psw)
        for s in range(4):
            pv = prot.next()
            mm_group(P, pv.t[:, 0:256], pv.b,
                     [(hT.t[:, kc, s * 128:(s + 1) * 128], wkv.t[:, kc, 576:832]) for kc in range(16)], [wkv.b, hT.b])
            P.op("act", lambda e, o=vgst.t[:, s, :], a=pv.t[:, 0:256]: e.activation(out=o, in_=a, func=AF.Copy),
                 reads=[pv.b], writes=[vgst.b])
        dma_out(P, knst, kn_d[:, :, c0:c0 + 512].rearrange("h p t -> p h t"), knst.t[:])
        dma_out(P, krst, kr_d[:, c0:c0 + 512], krst.t[:])
        dma_out(P, kgst, kg_d[:, :, c0:c0 + 512].rearrange("h p t -> p h t"), kgst.t[:])
        for s in range(4):
            dma_out(P, vast, va_d[:, :, 4 * t + s, :].rearrange("h p d -> p h d"),
                    vast.t[:, s, :].rearrange("p (h d) -> p h d", h=8))
        for s in range(4):
            dma_out(P, vgst, vg_d[:, :, 4 * t + s, :].rearrange("h p d -> p h d"),
                    vgst.t[:, s, :].rearrange("p (h d) -> p h d", h=2))
    C.close()

    C = Ctx(nc, sems, "p1")
    P = C.P
    K = Common(C, gcols)
    wq = C.sb([128, 16, 1536], BF16, "wq")
    dma_in(P, "pool", wq, w_q0.rearrange("p (k c) -> p k c", k=16))
    wuq = C.sb([128, 4, 1536], BF16, "wuq")
    dma_in(P, "pool", wuq, w_uq.rearrange("p (k c) -> p k c", k=4))
    pm = C.sb([128, 3, 128], BF16, "pm")
    dma_in(P, "pool", pm, perms.rearrange("a p m -> p a m"))
    pm_mla = TB(pm.t[:, 0, :]); pm_mla.b = pm.b
    pm_gqa = TB(pm.t[:, 1, :]); pm_gqa.b = pm.b
    xts = Rot([C.sb([128, 16, 512], F32, "xt")])
    hT = C.sb([128, 16, 512], BF16, "hT")
    tabs = Rot([(C.sb([64, 2, 512], F32, "tm"), C.sb([128, 2, 512], F32, "tg")) for _ in range(2)])
    cqn = C.sb([128, 4, 512], BF16, "cqn")
    qst = Rot([(C.sb([128, 8, 512], BF16, "qnst"), C.sb([64, 8, 512], BF16, "qrst"),
                C.sb([128, 8, 512], BF16, "qgst")) for _ in range(1)])
    ssq = Rot([C.ps([128, 512], "ssq") for _ in range(2)])
    prot = Rot([C.ps([128, 512], "pr") for _ in range(6)])
    for t in range(5):
        c0 = t * 512
        n = min(512, EXT - c0)
        xt = xts.next()
        dma_in(P, "sp", xt, xT_v[:, :, c0:c0 + n], xt.t[:, :, 0:n])
        tm, tg = tabs.next()
        dma_in(P, "sp", tm, t_mla[:, :, c0:c0 + n].rearrange("a p t -> p a t"), tm.t[:, :, 0:n])
        dma_in(P, "sp", tg, t_gqa[:, :, c0:c0 + n].rearrange("a p t -> p a t"), tg.t[:, :, 0:n])
        xnorm(P, K, xt, hT, 0, n, ssq)
        qnst, qrst, qgst = qst.next()
        pcq = [prot.next() for _ in range(4)]
        for j in range(4):
            mm_group(P, pcq[j].t[:, 0:n], pcq[j].b,
                     [(wq.t[:, kc, j * 128:(j + 1) * 128], hT.t[:, kc, 0:n]) for kc in range(16)], [wq.b, hT.b])
        ssq_ = sq_accum(P, K, [(pcq[j].t[:, 0:n], pcq[j].b, 128) for j in range(4)], ssq, n)
        r = rstd_of(P, K, ssq_, 512.0, n)
        for j in range(4):
            apply_plain(P, K, pcq[j].t[:, 0:n], pcq[j].b, 64 + j, r, cqn.t[:, j, 0:n], cqn.b, n)
        for h in range(8):
            pn = prot.next()
            mm_group(P, pn.t[:, 0:n], pn.b,
                     [(wuq.t[:, kc, h * 192:h * 192 + 128], cqn.t[:, kc, 0:n]) for kc in range(4)], [wuq.b, cqn.b])
            pr = prot.next()
            mm_group(P, pr.t[0:64, 0:n], pr.b,
                     [(wuq.t[:, kc, h * 192 + 128:h * 192 + 192], cqn.t[:, kc, 0:n]) for kc in range(4)], [wuq.b, cqn.b])
            ssq_ = sq_accum(P, K, [(pn.t[:, 0:n], pn.b, 128), (pr.t[0:64, 0:n], pr.b, 64)], ssq, n)
            r = rstd_of(P, K, ssq_, 192.0, n)
            apply_plain(P, K, pn.t[:, 0:n], pn.b, 70, r, qnst.t[:, h, 0:n], qnst.b, n)
            psw = prot.next()
            apply_rope(P, K, pr, n, 64, pm_mla, 71, (tm.t[:, 0, 0:n], tm.b), (tm.t[:, 1, 0:n], tm.b), r,
                       qrst.t[:, h, 0:n], qrst.b, psw)
        for h in range(8):
            pg = prot.next()
            mm_group(P, pg.t[:, 0:n], pg.b,
                     [(wq.t[:, kc, 512 + h * 128:512 + (h + 1) * 128], hT.t[:, kc, 0:n]) for kc in range(16)],
                     [wq.b, hT.b])
            ssq_ = sq_accum(P, K, [(pg.t[:, 0:n], pg.b, 128)], ssq, n)
            r = rstd_of(P, K, ssq_, 128.0, n)
            psw = prot.next()
            apply_rope(P, K, pg, n, 128, pm_gqa, 76, (tg.t[:, 0, 0:n], tg.b), (tg.t[:, 1, 0:n], tg.b), r,
                       qgst.t[:, h, 0:n], qgst.b, psw)
        dma_out(P, qnst, qn_d[:, :, c0:c0 + n].rearrange("h p t -> p h t"), qnst.t[:, :, 0:n])
        dma_out(P, qrst, qr_d[:, :, c0:c0 + n].rearrange("h p t -> p h t"), qrst.t[:, :, 0:n])
        dma_out(P, qgst, qg_d[:, :, c0:c0 + n].rearrange("h p t -> p h t"), qgst.t[:, :, 0:n])
    C.close()

    C = Ctx(nc, sems, "p2")
    P = C.P
    ones = C.sb([128, 128], BF16, "ones")
    P.op("dve", lambda e: e.memset(ones.t[:], 1.0), writes=[ones.b])
    krs = C.sb([128, S], BF16, "krs")
    P.op("dve", lambda e: e.memset(krs.t[64:128, :], 0.0), writes=[krs.b])
    dma_in(P, "sp", krs, kr_d, krs.t[0:64, :])
    cvb = Buf("cv")
    cvl = [(wb_o0[i], w_o0[i]) for i in range(4)]
    cvl += [(wb_up[0, i], w_up[0, i]) for i in range(16)]
    cvl += [(wb_dn[0, hf, i], w_dn[0, hf, i]) for hf in range(2) for i in range(8)]
    cvl += [(wb_qkv1[i], w_qkv1[i]) for i in range(5)]
    cvl += [(wb_o1[i], w_o1[i]) for i in range(4)]
    cvl += [(wb_up[1, i], w_up[1, i]) for i in range(16)]
    cvl += [(wb_dn[1, hf, i], w_dn[1, hf, i]) for hf in range(2) for i in range(8)]
    for (dst_, src_) in cvl:
        P.op("pool", lambda e, d=dst_, s_=src_: e.dma_start(out=d, in_=s_), writes=[cvb], dma=cvb)
    kts = Rot([C.sb([128, S], BF16, "kt") for _ in range(2)])
    vts = Rot([C.sb([128, 64, 128], BF16, "vt") for _ in range(2)])
    qns = Rot([C.sb([128, EXT], BF16, "qn") for _ in range(2)])
    qrs = Rot([C.sb([128, EXT], BF16, "qr") for _ in range(2)])
    for _q in qrs.items:
        P.op("dve", lambda e, _q=_q: e.memset(_q.t[64:128, :], 0.0), writes=[_q.b])
    pts = Rot([C.sb([128, 1024], BF16, "pt") for _ in range(4)])
    ones32 = C.sb([128, 128], F32, "ones32")
    P.op("dve", lambda e: e.memset(ones32.t[:], 1.0), writes=[ones32.b])
    accAs = Rot([C.sb([128, 2, 512], F32, "accA") for _ in range(2)])
    accBs = Rot([C.sb([128, 2, 512], F32, "accB") for _ in range(1)])
    hls = Rot([C.sb([128, 2, 512], BF16, "hl") for _ in range(2)])
    rds = Rot([C.sb([128, 512], F32, "rd") for _ in range(2)])
    ost = Rot([C.sb([128, 512], BF16, "ost") for _ in range(2)])
    p2s = Rot([C.sb([128, 512], BF16, "p2") for _ in range(3)])
    sps = Rot([C.ps([128, 1024], "S") for _ in range(2)])
    ops_ = Rot([C.ps([128, 512], "O") for _ in range(2)])
    dps = Rot([C.ps([128, 512], "Dn") for _ in range(2)])
    qtiles = [(0, 512), (512, 512), (1024, 512), (1536, 512), (2048, 256)]
    kt = vt = None
    pend = []

    def flush_pend():
        while pend:
            o_, dst_, n_ = pend.pop(0)
            dma_out(P, o_, dst_, o_.t[:, 0:n_], q="act")
    for job in range(16):
        mla = job < 8
        if mla:
            kt = kts.next()
            dma_in(P, "sp", kt, kn_d[job])
            vt = vts.next()
            dma_in(P, "sp", vt, va_d[job])
            qn = qns.next()
            dma_in(P, "sp", qn, qn_d[job])
            qr = qrs.next()
            dma_in(P, "sp", qr, qr_d[job], qr.t[0:64, :])
            scale = 192.0 ** -0.5
        else:
            j = job - 8
            if j % 4 == 0:
                kt = kts.next()
                dma_in(P, "sp", kt, kg_d[j // 4])
                vt = vts.next()
                dma_in(P, "sp", vt, vg_d[j // 4])
            qn = qns.next()
            dma_in(P, "sp", qn, qg_d[j])
            scale = 128.0 ** -0.5
        for (q0, n) in qtiles:
            Ops = ops_.next()
            Dps = dps.next()

            def s_pair(kp, mla=mla, kt=kt, qn=qn, qr=(qr if mla else None), q0=q0, n=n):
                Sp = sps.next()

                def fn(e, Sp=Sp):
                    ins = None
                    for i in range(2):
                        kb = 2 * kp + i
                        o = Sp.t[:, i * n:(i + 1) * n] if n == 256 else Sp.t[:, i * 512:(i + 1) * 512]
                        ins = e.matmul(o, lhsT=kt.t[:, kb * 128:(kb + 1) * 128], rhs=qn.t[:, q0:q0 + n],
                                       start=True, stop=(not mla))
                        if mla:
                            ins = e.matmul(o, lhsT=krs.t[:, kb * 128:(kb + 1) * 128], rhs=qr.t[:, q0:q0 + n],
                                           start=False, stop=True)
                    return ins
                rd = [kt.b, qn.b] + ([krs.b, qr.b] if mla else [])
                P.op("pe", fn, reads=rd, writes=[Sp.b])
                return Sp

            def pv_pair(kp, Sp, vt=vt, n=n, Ops=Ops, Dps=Dps, scale=scale):
                pt = pts.next()
                w = 2 * n
                P.op("act", lambda e: e.activation(out=pt.t[:, 0:w], in_=Sp.t[:, 0:w], func=AF.Exp, scale=scale),
                     reads=[Sp.b], writes=[pt.b])
                p2 = p2s.next()
                P.op("dve", lambda e: e.tensor_tensor(out=p2.t[:, 0:n], in0=pt.t[:, 0:n], in1=pt.t[:, n:w], op=ALU.add),
                     reads=[pt.b], writes=[p2.b])

                def fn(e):
                    ins = None
                    for i in range(2):
                        kb = 2 * kp + i
                        ins = e.matmul(Ops.t[:, 0:n], lhsT=vt.t[:, kb, :], rhs=pt.t[:, i * n:(i + 1) * n],
                                       start=(kb == 0), stop=(kb == 63))
                    return ins
                P.op("pe", fn, reads=[vt.b, pt.b], writes=[Ops.b])
                P.op("pe", lambda e: e.matmul(Dps.t[:, 0:n], lhsT=ones.t[:, :], rhs=p2.t[:, 0:n],
                                              start=(kp == 0), stop=(kp == 31)),
                     reads=[p2.b, ones.b], writes=[Dps.b])

            prev = s_pair(0)
            for kp in range(32):
                nxt = s_pair(kp + 1) if kp + 1 < 32 else None
                pv_pair(kp, prev)
                prev = nxt
                if kp == 4:
                    flush_pend()
            rd = rds.next()
            P.op("dve", lambda e, rd=rd, Dps=Dps, n=n: e.reciprocal(out=rd.t[:, 0:n], in_=Dps.t[:, 0:n]),
                 reads=[Dps.b], writes=[rd.b])
            o = ost.next()
            P.op("dve", lambda e, o=o, rd=rd, Ops=Ops, n=n: e.tensor_tensor(out=o.t[:, 0:n], in0=Ops.t[:, 0:n],
                                                                           in1=rd.t[:, 0:n], op=ALU.mult),
                 reads=[Ops.b, rd.b], writes=[o.b])
            pend.append((o, mg_d[job * 128:(job + 1) * 128, q0:q0 + n], n))
    flush_pend()
    C.close()

    def tail_phase(tag, layer):
        C = Ctx(nc, sems, tag)
        P = C.P
        K = Common(C, gcols, with_bd=(layer == 0))
        slabs = Rot([C.sb([128, 8192], BF16, "slab") for _ in range(4)])
        xt = C.sb([128, 16, 512], F32, "xt")
        hT = C.sb([128, 16, 512], BF16, "hT")
        hid = C.sb([128, 32, 512], BF16, "hid")
        rls = Rot([C.sb([128, 512], F32, "rl") for _ in range(3)])
        ssq = Rot([C.ps([128, 512], "ssq") for _ in range(2)])
        prot = Rot([C.ps([128, 512], "pr") for _ in range(6)])
        if layer == 0:
            pm = C.sb([128, 128], BF16, "pm")
            dma_in(P, "pool", pm, perms[2])
            tsw = Rot([C.sb([128, 2, 512], F32, "tsw") for _ in range(2)])
            qst = Rot([C.sb([128, 512], BF16, "qst") for _ in range(3)])
            vst = Rot([C.sb([128, 4, 256], BF16, "vst") for _ in range(1)])
            ntiles = 5
            wo, mgsrc, xsrc = wb_o0, mg_d, None
        else:
            ntiles = 4
            wo, mgsrc = wb_o1, m1_d
        outtoks = []

        def load_slab(src_ap):
            sl = slabs.next()
            dma_in(P, "sp", sl, src_ap)
            return sl

        for t in range(ntiles):
            if layer == 0:
                c0 = t * 512
                n = min(512, EXT - c0)
                dma_in(P, "pool", xt, xT_v[:, :, c0:c0 + n], xt.t[:, :, 0:n])
                dma_in(P, "pool", hT, mgsrc.rearrange("(k p) t -> p k t", p=128)[:, :, c0:c0 + n], hT.t[:, :, 0:n])
            else:
                c0 = t * 512
                n = 512
                dma_in(P, "pool", xt, x1_d.rearrange("(k p) t -> p k t", p=128)[:, :, 128 + c0:128 + c0 + n])
                dma_in(P, "pool", hT, mgsrc.rearrange("(k p) t -> p k t", p=128)[:, :, c0:c0 + n])
            for s in range(4):
                sl = load_slab(wo[s])
                slv = sl.t[:].rearrange("p (k c) -> p k c", k=16)
                for j in range(4):
                    c = 4 * s + j
                    pp = prot.next()
                    mm_group(P, pp.t[:, 0:n], pp.b,
                             [(slv[:, kc, j * 128:(j + 1) * 128], hT.t[:, kc, 0:n]) for kc in range(16)], [sl.b, hT.b])
                    P.op("dve", lambda e, c=c, pp=pp, n=n: e.tensor_tensor(out=xt.t[:, c, 0:n], in0=pp.t[:, 0:n],
                                                                           in1=xt.t[:, c, 0:n], op=ALU.add),
                         reads=[pp.b, xt.b], writes=[xt.b])
            xnorm(P, K, xt, hT, 16 + 32 * layer, n, ssq)
            for hf in range(2):
                for s in range(8):
                    sl = load_slab(wb_up[layer, hf * 8 + s])
                    slv = sl.t[:].rearrange("p (k c) -> p k c", k=16)
                    for j in range(4):
                        hc = s * 4 + j
                        pp = prot.next()
                        mm_group(P, pp.t[:, 0:n], pp.b,
                                 [(slv[:, kc, j * 128:(j + 1) * 128], hT.t[:, kc, 0:n]) for kc in range(16)],
                                 [sl.b, hT.b])
                        rl = rls.next()
                        P.op("act", lambda e, rl=rl, pp=pp, n=n: e.activation(out=rl.t[:, 0:n], in_=pp.t[:, 0:n],
                                                                              func=AF.Relu),
                             reads=[pp.b], writes=[rl.b])
                        P.op("dve", lambda e, rl=rl, hc=hc, n=n: e.tensor_tensor(out=hid.t[:, hc, 0:n], in0=rl.t[:, 0:n],
                                                                              in1=rl.t[:, 0:n], op=ALU.mult),
                             reads=[rl.b], writes=[hid.b])
                for s in range(8):
                    sl = load_slab(wb_dn[layer, hf, s])
                    slv = sl.t[:].rearrange("p (k c) -> p k c", k=32)
                    for j in range(2):
                        c = 2 * s + j
                        pp = prot.next()
                        mm_group(P, pp.t[:, 0:n], pp.b,
                                 [(slv[:, kc, j * 128:(j + 1) * 128], hid.t[:, kc, 0:n]) for kc in range(32)],
                                 [sl.b, hid.b])
                        P.op("dve", lambda e, c=c, pp=pp, n=n: e.tensor_tensor(out=xt.t[:, c, 0:n], in0=pp.t[:, 0:n],
                                                                               in1=xt.t[:, c, 0:n], op=ALU.add),
                             reads=[pp.b, xt.b], writes=[xt.b])
            if layer == 1:
                outtoks.append(dma_out(P, xt, yT.rearrange("(k p) t -> p k t", p=128)[:, :, c0:c0 + n], xt.t[:, :, 0:n]))
                continue
            dma_out(P, xt, x1_d.rearrange("(k p) t -> p k t", p=128)[:, :, c0:c0 + n], xt.t[:, :, 0:n])
            ts_ = tsw.next()
            dma_in(P, "pool", ts_, t_swa[:, :, c0:c0 + n].rearrange("a p t -> p a t"), ts_.t[:, :, 0:n])
            xnorm(P, K, xt, hT, 32, n, ssq)
            nb = n // 128
            b0 = c0 // 128
            for s in range(5):
                sl = load_slab(wb_qkv1[s])
                slv = sl.t[:].rearrange("p (k c) -> p k c", k=16)
                nch = 4 if s < 4 else 2
                for j in range(nch):
                    pq = prot.next()
                    mm_group(P, pq.t[:, 0:n], pq.b,
                             [(slv[:, kc, j * 128:(j + 1) * 128], hT.t[:, kc, 0:n]) for kc in range(16)], [sl.b, hT.b])
                    ssq_ = sq_accum(P, K, [(pq.t[:, 0:n], pq.b, 128)], ssq, n, lhs=K.bd)
                    r = rstd_of(P, K, ssq_, 64.0, n)
                    psw = prot.next()
                    qs = qst.next()
                    pmtb = pm
                    apply_rope(P, K, pq, n, 128, pmtb, 80 if s < 4 else 82,
                               (ts_.t[:, 0, 0:n], ts_.b), (ts_.t[:, 1, 0:n], ts_.b), r, qs.t[:, 0:n], qs.b, psw, eng2="dve")
                    if s < 4:
                        c = 4 * s + j
                        for u in range(2):
                            h = 2 * c + u
                            kvh, hh = h // 8, h % 8
                            dma_out(P, qs, q1_d[kvh, :, b0:b0 + nb, hh, :],
                                    qs.t[u * 64:(u + 1) * 64, 0:n].rearrange("p (b q) -> p b q", q=128))
                    else:
                        for u in range(2):
                            dma_out(P, qs, k1_d[2 * j + u, :, c0:c0 + n], qs.t[u * 64:(u + 1) * 64, 0:n])
                if s == 4:
                    vs = vst.next()
                    for sb_ in range(nb):
                        pv = prot.next()
                        mm_group(P, pv.t[:, 0:256], pv.b,
                                 [(hT.t[:, kc, sb_ * 128:(sb_ + 1) * 128], slv[:, kc, 256:512]) for kc in range(16)],
                                 [sl.b, hT.b])
                        P.op("act", lambda e, vs=vs, sb_=sb_, pv=pv: e.activation(out=vs.t[:, sb_, :], in_=pv.t[:, 0:256],
                                                                                  func=AF.Copy),
                             reads=[pv.b], writes=[vs.b])
                    dma_out(P, vs, v1_d[:, b0:b0 + nb, :], vs.t[:, 0:nb, :])
        C.close()
        return outtoks

    tail_phase("p3", 0)

    C = Ctx(nc, sems, "p4")
    P = C.P
    ones = C.sb([128, 128], BF16, "ones")
    P.op("dve", lambda e: e.memset(ones.t[:], 1.0), writes=[ones.b])
    k1s = C.sb([64, 4, EXT], BF16, "k1s")
    dma_in(P, "sp", k1s, k1_d.rearrange("k p t -> p k t"))
    v1s = C.sb([128, 18, 256], BF16, "v1s")
    dma_in(P, "sp", v1s, v1_d)
    mks = C.sb([128, 4, 512], BF16, "mks")
    dma_in(P, "pool", mks, masks.rearrange("a p t -> p a t"))
    esk = C.sb([64, 32], F32, "esk")
    dma_in(P, "sp", esk, sinkrep)
    P.op("act", lambda e: e.activation(out=esk.t[:], in_=esk.t[:], func=AF.Exp), reads=[esk.b], writes=[esk.b])
    eskx = C.sb([64, 32, 128], F32, "eskx")
    P.op("dve", lambda e: e.memset(eskx.t[:], 0.0), writes=[eskx.b])
    for h in range(32):
        P.op("dve", lambda e, h=h: e.tensor_scalar(out=eskx.t[:, h, :], in0=eskx.t[:, h, :], scalar1=esk.t[:, h:h + 1],
                                                   scalar2=None, op0=ALU.add), reads=[esk.b, eskx.b], writes=[eskx.b])
    q1s = Rot([C.sb([64, 18, 8, 128], BF16, "q1s") for _ in range(2)])
    pts = Rot([C.sb([128, 1536], BF16, "pt") for _ in range(3)])
    dns = Rot([C.sb([64, 512], F32, "dn") for _ in range(2)])
    dls = Rot([C.sb([64, 512], F32, "dl") for _ in range(2)])
    ost = Rot([C.sb([64, 512], BF16, "ost") for _ in range(2)])
    Sp = C.ps([128, 1536], "S")
    opsr = Rot([C.ps([128, 512], "O") for _ in range(2)])
    dpsr = Rot([C.ps([128, 512], "Dn") for _ in range(2)])
    sc1 = 64.0 ** -0.5
    q1_of = {}
    its = []
    for kv in range(4):
        for jq in range(16):
            for hg in range(2):
                its.append((kv, jq, hg))

    def s_mm(it):
        kv, jq, hg = it
        if kv not in q1_of:
            q1 = q1s.next()
            dma_in(P, "sp", q1, q1_d[kv])
            q1_of[kv] = q1
        q1 = q1_of[kv]

        def fnS(e):
            ins = None
            for kb in range(3):
                ins = e.matmul(Sp.t[:, kb * 512:(kb + 1) * 512],
                               lhsT=k1s.t[:, kv, (jq + kb) * 128:(jq + kb + 1) * 128],
                               rhs=q1.t[:, jq + 1, hg * 4:(hg + 1) * 4, :].rearrange("p h q -> p (h q)"),
                               start=True, stop=True)
            return ins
        P.op("pe", fnS, reads=[k1s.b, q1.b], writes=[Sp.b])

    def exp_mask(it):
        kv, jq, hg = it
        pt = pts.next()
        P.op("act", lambda e: e.activation(out=pt.t[:, :], in_=Sp.t[:, :], func=AF.Exp, scale=sc1),
             reads=[Sp.b], writes=[pt.b])
        mP = 2 if jq == 0 else 0
        mN = 3 if jq == 15 else 1
        P.op("dve", lambda e: e.tensor_tensor(out=pt.t[:, 0:512], in0=pt.t[:, 0:512], in1=mks.t[:, mP, :],
                                              op=ALU.mult), reads=[pt.b, mks.b], writes=[pt.b])
        P.op("pool", lambda e: e.tensor_tensor(out=pt.t[:, 1024:1536], in0=pt.t[:, 1024:1536], in1=mks.t[:, mN, :],
                                               op=ALU.mult), reads=[pt.b, mks.b], writes=[pt.b])
        return pt

    def pv_norm(it, pt):
        kv, jq, hg = it
        Ops = opsr.next()
        Dps = dpsr.next()

        def fnP(e):
            ins = None
            for kb in range(3):
                e.matmul(Ops.t[0:64, :], lhsT=v1s.t[:, jq + kb, kv * 64:(kv + 1) * 64],
                         rhs=pt.t[:, kb * 512:(kb + 1) * 512], start=(kb == 0), stop=(kb == 2))
            for kb in range(3):
                ins = e.matmul(Dps.t[0:64, :], lhsT=ones.t[:, 0:64], rhs=pt.t[:, kb * 512:(kb + 1) * 512],
                               start=(kb == 0), stop=(kb == 2))
            return ins
        P.op("pe", fnP, reads=[v1s.b, pt.b, ones.b], writes=[Ops.b, Dps.b])
        dn = dns.next()
        h0 = kv * 8 + hg * 4
        P.op("dve", lambda e: e.tensor_tensor(
            out=dn.t[:, :], in0=Dps.t[0:64, :], in1=eskx.t[:, h0:h0 + 4, :].rearrange("p h q -> p (h q)"),
            op=ALU.add), reads=[Dps.b, eskx.b], writes=[dn.b])
        dl = dls.next()
        P.op("act", lambda e: e.activation(out=dl.t[:, :], in_=dn.t[:, :], func=AF.Ln),
             reads=[dn.b], writes=[dl.b])
        P.op("act", lambda e: e.activation(out=dn.t[:, :], in_=dl.t[:, :], func=AF.Exp, scale=-1.0),
             reads=[dl.b], writes=[dn.b])
        o = ost.next()
        P.op("dve", lambda e: e.tensor_tensor(out=o.t[:, :], in0=Ops.t[0:64, :], in1=dn.t[:, :], op=ALU.mult),
             reads=[Ops.b, dn.b], writes=[o.b])
        r0 = (kv * 8 + hg * 4) * 64
        dma_out(P, o, m1_d[r0:r0 + 256, jq * 128:(jq + 1) * 128].rearrange("(h d) q -> d h q", d=64),
                o.t[:, :].rearrange("p (h q) -> p h q", q=128), q="sp")

    s_mm(its[0])
    pt_cur = exp_mask(its[0])
    for i, it in enumerate(its):
        if i + 1 < len(its):
            s_mm(its[i + 1])
            pt_nxt = exp_mask(its[i + 1])
        else:
            pt_nxt = None
        pv_norm(it, pt_cur)
        pt_cur = pt_nxt
    C.close()

    tail_phase("p5", 1)
    es.close()
    return nc


def _slab(W, cols=None):
    W = np.asarray(W, np.float32)
    if cols is not None:
        W = W[:, cols]
    Kd, n = W.shape
    return np.ascontiguousarray(W.reshape(Kd // 128, 128, n).transpose(1, 0, 2).reshape(128, (Kd // 128) * n))


def _rope_tab(pos, dim, theta):
    inv = np.float32(theta) ** (-np.arange(0, dim, 2, dtype=np.float32) / np.float32(dim))
    ang = pos.astype(np.float32)[:, None] * inv[None, :].astype(np.float32)
    ang = ang.astype(np.float32).astype(np.float64)
    return np.cos(ang).astype(np.float32).T, np.sin(ang).astype(np.float32).T


def _perm_lhsT(perm):
    n = len(perm)
    m = np.zeros((128, 128), np.float32)
    for o in range(n):
        m[perm[o], o] = 1.0
    return m


_NC_CACHE = {}


def kernel(x, even_norm, even_w_in, mla_q_lat_norm, mla_kv_lat_norm, mla_w_uq, mla_w_ukv,
           mla_q_norm, mla_k_nope_norm, mla_k_rope_norm, gqa_q_norm, gqa_k_norm, even_w_out,
           odd_norm, odd_w_qkv, swa_q_norm, swa_k_norm, swa_sink, odd_w_out,
           mlp_norm, mlp_w_up, mlp_w_down):
    f = lambda a: np.asarray(a, np.float32)
    x = f(x)
    p_mla = np.concatenate([np.arange(32, 64), np.arange(0, 32)])
    p_gqa = np.concatenate([p_mla, 64 + p_mla])
    p_sw64 = np.concatenate([np.arange(8, 16), np.arange(0, 8), np.arange(16, 64)])
    p_swa = np.concatenate([p_sw64, 64 + p_sw64])
    perms = np.stack([_perm_lhsT(p_mla), _perm_lhsT(p_gqa), _perm_lhsT(p_swa)])
    g = np.ones((128, 84), np.float32)
    g[:, 0:16] = f(even_norm)[0].reshape(16, 128).T
    g[:, 16:32] = f(mlp_norm)[0].reshape(16, 128).T
    g[:, 32:48] = f(odd_norm)[0].reshape(16, 128).T
    g[:, 48:64] = f(mlp_norm)[1].reshape(16, 128).T
    g[:, 64:68] = f(mla_q_lat_norm)[0].reshape(4, 128).T
    g[:, 68:70] = f(mla_kv_lat_norm)[0].reshape(2, 128).T
    qn_ = f(mla_q_norm)[0]
    g[:, 70] = qn_[0:128]
    g[0:64, 71] = qn_[128:192]
    g[0:64, 72] = qn_[128:192][p_mla]
    g[:, 73] = f(mla_k_nope_norm)[0]
    kr_ = f(mla_k_rope_norm)[0]
    g[0:64, 74] = kr_
    g[0:64, 75] = kr_[p_mla]
    gq_ = f(gqa_q_norm)[0]
    g[:, 76] = gq_
    g[:, 77] = gq_[p_gqa]
    gk_ = f(gqa_k_norm)[0]
    g[:, 78] = gk_
    g[:, 79] = gk_[p_gqa]
    sq_ = np.tile(f(swa_q_norm)[0], 2)
    g[:, 80] = sq_
    g[:, 81] = sq_[p_swa]
    sk_ = np.tile(f(swa_k_norm)[0], 2)
    g[:, 82] = sk_
    g[:, 83] = sk_[p_swa]
    w_in = f(even_w_in)[0]
    cols_kv = np.concatenate([np.arange(512, 768), np.arange(768, 832), np.arange(1856, 2112), np.arange(2112, 2368)])
    cols_q = np.concatenate([np.arange(0, 512), np.arange(832, 1856)])
    wkv = _slab(w_in, cols_kv)
    wq0 = _slab(w_in, cols_q)
    wuq = _slab(f(mla_w_uq)[0])
    ukv = f(mla_w_ukv)[0]
    ck = np.concatenate([np.arange(h * 256, h * 256 + 128) for h in range(8)])
    cv = np.concatenate([np.arange(h * 256 + 128, h * 256 + 256) for h in range(8)])
    wuk = _slab(ukv, ck)
    wuv = _slab(ukv, cv)
    wo0 = np.stack([_slab(f(even_w_out)[0][:, s * 512:(s + 1) * 512]) for s in range(4)])
    wo1 = np.stack([_slab(f(odd_w_out)[0][:, s * 512:(s + 1) * 512]) for s in range(4)])
    wqkv1 = np.stack([_slab(f(odd_w_qkv)[0][:, s * 512:(s + 1) * 512]) for s in range(5)])
    up = f(mlp_w_up)
    wup = np.stack([np.stack([_slab(up[l][:, s * 512:(s + 1) * 512]) for s in range(16)]) for l in range(2)])
    dn = f(mlp_w_down)
    wdn = np.stack([np.stack([np.stack([_slab(dn[l][hf * 4096:(hf + 1) * 4096, s * 256:(s + 1) * 256])
                                        for s in range(8)]) for hf in range(2)]) for l in range(2)])
    sinkrep = np.ascontiguousarray(np.broadcast_to(f(swa_sink)[0][None, :], (64, 32)))
    kk = np.arange(128)[:, None]
    qq = np.arange(128)[None, :]
    mP = np.tile((kk >= qq).astype(np.float32), (1, 4))
    mN = np.tile((kk <= qq).astype(np.float32), (1, 4))

    in_maps = []
    for c in range(8):
        b, ch = c // 4, c % 4
        ext = list(range(16 * ch - 1, 16 * ch + 17))
        if ch == 0:
            ext[0] = 17
        if ch == 3:
            ext[17] = 46
        rest = [i for i in range(64) if i not in ext]
        order = np.array(ext + rest)
        tok = (order[:, None] * 128 + np.arange(128)[None, :]).reshape(-1)
        xTc = np.ascontiguousarray(x[b][tok].T)
        cm, sm = _rope_tab(tok, 64, 500000.0)
        t_mla = np.stack([np.concatenate([cm, cm]), np.concatenate([-sm, sm])])
        cr, sr = _rope_tab(tok // 64, 64, 10000.0)
        cc, sc = _rope_tab(tok % 64, 64, 10000.0)
        t_gqa = np.stack([np.concatenate([cr, cr, cc, cc]), np.concatenate([-sr, sr, -sc, sc])])
        te = tok[:EXT]
        cs, ss = _rope_tab(te, 16, 500000.0)
        one = np.ones((48, EXT), np.float32)
        zero = np.zeros((48, EXT), np.float32)
        c64 = np.concatenate([cs, cs, one])
        s64 = np.concatenate([-ss, ss, zero])
        t_swa = np.stack([np.concatenate([c64, c64]), np.concatenate([s64, s64])])
        masks = np.stack([mP, mN, mP * (0.0 if ch == 0 else 1.0), mN * (0.0 if ch == 3 else 1.0)])
        in_maps.append({
            "xT": xTc, "gcols": g, "w_kv": wkv, "w_q0": wq0, "w_uq": wuq, "w_uk": wuk, "w_uv": wuv,
            "w_o0": wo0, "w_up": wup, "w_dn": wdn, "w_qkv1": wqkv1, "w_o1": wo1,
            "t_mla": np.ascontiguousarray(t_mla, np.float32), "t_gqa": np.ascontiguousarray(t_gqa, np.float32),
            "t_swa": np.ascontiguousarray(t_swa, np.float32), "perms": perms,
            "masks": np.ascontiguousarray(masks, np.float32), "sinkrep": sinkrep,
        })
    if "nc" not in _NC_CACHE:
        _NC_CACHE["nc"] = build_program()
    nc = _NC_CACHE["nc"]
    res = run_bass_kernel_spmd(nc, in_maps, core_ids=list(range(8)))
    out = np.empty((2, S, D), np.float32)
    for c in range(8):
        b, ch = c // 4, c % 4
        out[b, ch * 2048:(ch + 1) * 2048, :] = res.results[c]["yT"].T
    return out
```

```python
import numpy as np
from contextlib import ExitStack
import concourse.bass as bass
import concourse.mybir as mybir
from concourse.bass_utils import run_bass_kernel_spmd

F32 = mybir.dt.float32
BF16 = mybir.dt.bfloat16
AF = mybir.ActivationFunctionType
ALU = mybir.AluOpType

D = 2048
S = 8192
EXT = 2304
OWN = 2048
EPS = 1e-6
ENGS = ("pe", "act", "dve", "pool", "sp")
ENGATTR = {"pe": "tensor", "act": "scalar", "dve": "vector", "pool": "gpsimd", "sp": "sync"}


class Buf:
    __slots__ = ("name", "last_w", "readers", "ndma", "sem", "id")
    _n = 0

    def __init__(self, name="b"):
        self.name = name
        self.last_w = None
        self.readers = []
        self.ndma = 0
        self.sem = None
        Buf._n += 1
        self.id = Buf._n


class TB:
    __slots__ = ("t", "b")

    def __init__(self, t, name="b"):
        self.t = t
        self.b = Buf(name)


class Rot:
    def __init__(self, items):
        self.items = items
        self.i = 0

    def next(self):
        it = self.items[self.i % len(self.items)]
        self.i += 1
        return it


class Sems:
    def __init__(self, nc, es):
        self.nc = nc
        self.es = es
        self.eng = {e: [es.enter_context(nc.semaphore("prog_" + e)), 0] for e in ENGS}
        self.free = []
        self.n = 0

    def get_dma(self):
        if self.free:
            return self.free.pop()
        self.n += 1
        return [self.es.enter_context(self.nc.semaphore("dsem%d" % self.n)), 0]


class Op:
    __slots__ = ("eng", "fn", "waits", "signal", "dma_buf", "idx")

    def __init__(self, eng, fn):
        self.eng = eng
        self.fn = fn
        self.waits = []
        self.signal = False
        self.dma_buf = None
        self.idx = None


class Prog:
    def __init__(self, nc, sems):
        self.nc = nc
        self.sems = sems
        self.ops = {e: [] for e in ENGS}
        self.waited = {e: {} for e in ENGS}
        self.dma_bufs = []

    def _need(self, op, tok):
        if tok is None:
            return
        w = self.waited[op.eng]
        if tok[0] == "c":
            if tok[1] == op.eng and op.dma_buf is None:
                return
            key = tok[1]
            if w.get(key, -1) >= tok[2]:
                return
            w[key] = tok[2]
            op.waits.append(tok)
            self.ops[tok[1]][tok[2]].signal = True
        else:
            key = ("d", tok[1].id)
            if w.get(key, -1) >= tok[2]:
                return
            w[key] = tok[2]
            op.waits.append(tok)

    def op(self, eng, fn, reads=(), writes=(), dma=None):
        o = Op(eng, fn)
        o.idx = len(self.ops[eng])
        o.dma_buf = dma
        for b in reads:
            self._need(o, b.last_w)
        for b in writes:
            self._need(o, b.last_w)
            for t in b.readers:
                self._need(o, t)
        if dma is not None:
            if dma.ndma == 0 and dma.sem is None:
                self.dma_bufs.append(dma)
            dma.ndma += 1
            tok = ("d", dma, dma.ndma)
        else:
            tok = ("c", eng, o.idx)
        for b in reads:
            b.readers.append(tok)
        for b in writes:
            b.last_w = tok
            b.readers = []
        self.ops[eng].append(o)
        return tok

    def emit(self):
        nc = self.nc
        sems = self.sems
        for b in self.dma_bufs:
            b.sem = sems.get_dma()
        for e in ENGS:
            for o in reversed(self.ops[e]):
                if o.dma_buf is None:
                    o.signal = True
                    break
        sigval = {}
        final = {}
        for e in ENGS:
            c = sems.eng[e][1]
            for o in self.ops[e]:
                if o.signal and o.dma_buf is None:
                    c += 1
                    sigval[(e, o.idx)] = c
            final[e] = c

        def lower(tok):
            if tok[0] == "c":
                return sems.eng[tok[1]][0], sigval[(tok[1], tok[2])]
            b = tok[1]
            return b.sem[0], 16 * (b.sem[1] + tok[2])

        sp_final = final["sp"] + 1
        with nc.Block() as block:
            def make(e):
                def body(eng):
                    for o in self.ops[e]:
                        for t in o.waits:
                            s, v = lower(t)
                            eng.wait_ge(s, v)
                        ins = o.fn(eng)
                        if o.dma_buf is not None:
                            ins.then_inc(o.dma_buf.sem[0], 16)
                        elif o.signal:
                            ins.then_inc(sems.eng[e][0], 1)
                    if e == "sp":
                        for e2 in ENGS:
                            if e2 != "sp" and final[e2] > 0:
                                eng.wait_ge(sems.eng[e2][0], final[e2])
                        for b in self.dma_bufs:
                            eng.wait_ge(b.sem[0], 16 * (b.sem[1] + b.ndma))
                        eng.sem_inc(sems.eng["sp"][0], 1)
                    else:
                        eng.wait_ge(sems.eng["sp"][0], sp_final)
                return body
            for e in ENGS:
                getattr(block, ENGATTR[e])(make(e))
        for e in ENGS:
            sems.eng[e][1] = final[e]
        sems.eng["sp"][1] = sp_final
        for b in self.dma_bufs:
            b.sem[1] += b.ndma
            sems.free.append(b.sem)
            b.sem = None


class Ctx:
    def __init__(self, nc, sems, tag):
        self.nc = nc
        self.es = ExitStack()
        self.P = Prog(nc, sems)
        self.tag = tag
        self.k = 0

    def sb(self, shape, dt, name="t"):
        self.k += 1
        t = self.es.enter_context(self.nc.sbuf_tensor("%s_%s%d" % (self.tag, name, self.k), list(shape), dt))
        return TB(t, name)

    def ps(self, shape, name="p"):
        self.k += 1
        t = self.es.enter_context(self.nc.psum_tensor("%s_%s%d" % (self.tag, name, self.k), list(shape), F32))
        return TB(t, name)

    def close(self):
        self.P.emit()
        self.es.close()


def dma_in(P, q, dst, src_ap, dst_ap=None):
    d = dst.t[:] if dst_ap is None else dst_ap
    P.op(q, lambda e, d=d, s=src_ap: e.dma_start(out=d, in_=s), writes=[dst.b], dma=dst.b)


def dma_out(P, src, dst_ap, src_ap, dram_bufs=(), q="pool"):
    return P.op(q, lambda e, d=dst_ap, s=src_ap: e.dma_start(out=d, in_=s),
                reads=[src.b], writes=list(dram_bufs), dma=src.b)


def mm_group(P, out_ap, out_b, pairs, reads):
    n = len(pairs)

    def fn(e, pairs=pairs, out_ap=out_ap, n=n):
        ins = None
        for i, (l, r) in enumerate(pairs):
            ins = e.matmul(out_ap, lhsT=l, rhs=r, start=(i == 0), stop=(i == n - 1))
        return ins
    P.op("pe", fn, reads=reads, writes=[out_b])


class Common:
    def __init__(self, C, gcols_d, with_bd=False):
        P = C.P
        self.C = C
        self.ones = C.sb([128, 128], BF16, "ones")
        P.op("dve", lambda e: e.memset(self.ones.t[:], 1.0), writes=[self.ones.b])
        self.eps = C.sb([128, 1], F32, "eps")
        P.op("dve", lambda e: e.memset(self.eps.t[:], EPS), writes=[self.eps.b])
        self.g = C.sb([128, 84], F32, "g")
        dma_in(P, "sp", self.g, gcols_d)
        self.sq = Rot([C.sb([128, 512], BF16, "sq") for _ in range(3)])
        self.f32 = Rot([C.sb([128, 512], F32, "f") for _ in range(6)])
        self.rawb = Rot([C.sb([128, 512], BF16, "rawb") for _ in range(2)])
        if with_bd:
            self.bd = C.sb([128, 128], BF16, "bd")
            P.op("dve", lambda e: e.memset(self.bd.t[:], 0.0), writes=[self.bd.b])
            P.op("dve", lambda e: e.memset(self.bd.t[0:64, 0:64], 1.0), writes=[self.bd.b])
            P.op("dve", lambda e: e.memset(self.bd.t[64:128, 64:128], 1.0), writes=[self.bd.b])


def sq_accum(P, K, srcs, ssqs, n, mrows=128, lhs=None):
    ssq = ssqs.next()
    lhs = K.ones if lhs is None else lhs
    m = len(srcs)
    for i, (ap, b, rows) in enumerate(srcs):
        sq = K.sq.next()
        P.op("act", lambda e, o=sq.t[0:rows, 0:n], a=ap: e.activation(out=o, in_=a, func=AF.Square),
             reads=[b], writes=[sq.b])
        P.op("pe", lambda e, o=ssq.t[0:mrows, 0:n], l=lhs.t[0:rows, 0:mrows], r=sq.t[0:rows, 0:n], i=i:
             e.matmul(o, lhsT=l, rhs=r, start=(i == 0), stop=(i == m - 1)),
             reads=[sq.b, lhs.b], writes=[ssq.b])
    return ssq


def rstd_of(P, K, ssq, d, n, rows=128):
    r = K.f32.next()
    P.op("act", lambda e, o=r.t[0:rows, 0:n], a=ssq.t[0:rows, 0:n]:
         e.activation(out=o, in_=a, func=AF.Sqrt, bias=K.eps.t[0:rows, 0:1], scale=1.0 / d),
         reads=[ssq.b, K.eps.b], writes=[r.b])
    P.op("dve", lambda e, o=r.t[0:rows, 0:n]: e.reciprocal(out=o, in_=o), reads=[r.b], writes=[r.b])
    return r


def apply_plain(P, K, src_ap, src_b, gcol, r, out_ap, out_b, n, rows=128, eng="dve"):
    P.op(eng, lambda e: e.scalar_tensor_tensor(out=out_ap, in0=src_ap, scalar=K.g.t[0:rows, gcol:gcol + 1],
                                               in1=r.t[0:rows, 0:n], op0=ALU.mult, op1=ALU.mult),
         reads=[src_b, K.g.b, r.b], writes=[out_b])


def apply_rope(P, K, src, n, rows, perm, gcol, Ctab, Stab, r, out_ap, out_b, psw, eng2="pool"):
    raw = K.rawb.next()
    P.op("act", lambda e: e.activation(out=raw.t[0:rows, 0:n], in_=src.t[0:rows, 0:n], func=AF.Copy),
         reads=[src.b], writes=[raw.b])
    P.op("pe", lambda e: e.matmul(psw.t[0:rows, 0:n], lhsT=perm.t[0:rows, 0:rows], rhs=raw.t[0:rows, 0:n],
                                  start=True, stop=True), reads=[raw.b, perm.b], writes=[psw.b])
    t1 = K.f32.next()
    P.op("dve", lambda e: e.scalar_tensor_tensor(out=t1.t[0:rows, 0:n], in0=src.t[0:rows, 0:n],
                                                 scalar=K.g.t[0:rows, gcol:gcol + 1], in1=Ctab[0],
                                                 op0=ALU.mult, op1=ALU.mult),
         reads=[src.b, K.g.b, Ctab[1]], writes=[t1.b])
    t2 = K.f32.next()
    P.op("dve", lambda e: e.scalar_tensor_tensor(out=t2.t[0:rows, 0:n], in0=psw.t[0:rows, 0:n],
                                                 scalar=K.g.t[0:rows, gcol + 1:gcol + 2], in1=Stab[0],
                                                 op0=ALU.mult, op1=ALU.mult),
         reads=[psw.b, K.g.b, Stab[1]], writes=[t2.b])
    P.op(eng2, lambda e: e.tensor_tensor(out=t1.t[0:rows, 0:n], in0=t1.t[0:rows, 0:n], in1=t2.t[0:rows, 0:n],
                                         op=ALU.add), reads=[t1.b, t2.b], writes=[t1.b])
    P.op(eng2, lambda e: e.tensor_tensor(out=out_ap, in0=t1.t[0:rows, 0:n], in1=r.t[0:rows, 0:n],
                                         op=ALU.mult), reads=[t1.b, r.b], writes=[out_b])


def xnorm(P, K, xt, hT, gbase, n, ssq):
    ssq_ = sq_accum(P, K, [(xt.t[:, k, 0:n], xt.b, 128) for k in range(16)], ssq, n)
    r = rstd_of(P, K, ssq_, float(D), n)
    for k in range(16):
        apply_plain(P, K, xt.t[:, k, 0:n], xt.b, gbase + k, r, hT.t[:, k, 0:n], hT.b, n,
                    eng="dve")


def build_program():
    nc = bass.Bass("TRN2", target_bir_lowering=False)

    def din(name, shape):
        return nc.dram_tensor(name, list(shape), F32, kind="ExternalInput").ap()

    def dscr(name, shape, dt):
        return nc.dram_tensor(name, list(shape), dt, kind="Internal").ap()

    xT = din("xT", [D, S])
    gcols = din("gcols", [128, 84])
    w_kv = din("w_kv", [128, 16 * 832])
    w_q0 = din("w_q0", [128, 16 * 1536])
    w_uq = din("w_uq", [128, 4 * 1536])
    w_uk = din("w_uk", [128, 2 * 1024])
    w_uv = din("w_uv", [128, 2 * 1024])
    w_o0 = din("w_o0", [4, 128, 8192])
    w_up = din("w_up", [2, 16, 128, 8192])
    w_dn = din("w_dn", [2, 2, 8, 128, 8192])
    w_qkv1 = din("w_qkv1", [5, 128, 8192])
    w_o1 = din("w_o1", [4, 128, 8192])
    t_mla = din("t_mla", [2, 64, S])
    t_gqa = din("t_gqa", [2, 128, S])
    t_swa = din("t_swa", [2, 128, EXT])
    perms = din("perms", [3, 128, 128])
    masks = din("masks", [4, 128, 512])
    sinkrep = din("sinkrep", [64, 32])
    yT = nc.dram_tensor("yT", [D, OWN], F32, kind="ExternalOutput").ap()

    kn_d = dscr("kn_d", [8, 128, S], BF16)
    kr_d = dscr("kr_d", [64, S], BF16)
    kg_d = dscr("kg_d", [2, 128, S], BF16)
    va_d = dscr("va_d", [8, 128, 64, 128], BF16)
    vg_d = dscr("vg_d", [2, 128, 64, 128], BF16)
    qn_d = dscr("qn_d", [8, 128, EXT], BF16)
    qr_d = dscr("qr_d", [8, 64, EXT], BF16)
    qg_d = dscr("qg_d", [8, 128, EXT], BF16)
    mg_d = dscr("mg_d", [D, EXT], BF16)
    x1_d = dscr("x1_d", [D, EXT], F32)
    q1_d = dscr("q1_d", [4, 64, 18, 8, 128], BF16)
    k1_d = dscr("k1_d", [4, 64, EXT], BF16)
    v1_d = dscr("v1_d", [128, 18, 256], BF16)
    m1_d = dscr("m1_d", [D, OWN], BF16)
    wb_o0 = dscr("wb_o0", [4, 128, 8192], BF16)
    wb_up = dscr("wb_up", [2, 16, 128, 8192], BF16)
    wb_dn = dscr("wb_dn", [2, 2, 8, 128, 8192], BF16)
    wb_qkv1 = dscr("wb_qkv1", [5, 128, 8192], BF16)
    wb_o1 = dscr("wb_o1", [4, 128, 8192], BF16)

    es = ExitStack()
    sems = Sems(nc, es)
    xT_v = xT.rearrange("(k p) t -> p k t", p=128)

    C = Ctx(nc, sems, "p0")
    P = C.P
    K = Common(C, gcols)
    wkv = C.sb([128, 16, 832], BF16, "wkv")
    dma_in(P, "pool", wkv, w_kv.rearrange("p (k c) -> p k c", k=16))
    wuk = C.sb([128, 2, 1024], BF16, "wuk")
    dma_in(P, "pool", wuk, w_uk.rearrange("p (k c) -> p k c", k=2))
    wuv = C.sb([128, 2, 1024], BF16, "wuv")
    dma_in(P, "pool", wuv, w_uv.rearrange("p (k c) -> p k c", k=2))
    pm = C.sb([128, 3, 128], BF16, "pm")
    dma_in(P, "pool", pm, perms.rearrange("a p m -> p a m"))
    pm_mla = TB(pm.t[:, 0, :]); pm_mla.b = pm.b
    pm_gqa = TB(pm.t[:, 1, :]); pm_gqa.b = pm.b
    xts = Rot([C.sb([128, 16, 512], F32, "xt") for _ in range(2)])
    hT = C.sb([128, 16, 512], BF16, "hT")
    tabs = Rot([(C.sb([64, 2, 512], F32, "tm"), C.sb([128, 2, 512], F32, "tg")) for _ in range(2)])
    ckvn = C.sb([128, 2, 512], BF16, "ckvn")
    kst = Rot([(C.sb([128, 8, 512], BF16, "knst"), C.sb([64, 512], BF16, "krst"),
                C.sb([128, 2, 512], BF16, "kgst"), C.sb([128, 4, 1024], BF16, "vast"),
                C.sb([128, 4, 256], BF16, "vgst")) for _ in range(2)])
    ssq = Rot([C.ps([128, 512], "ssq") for _ in range(2)])
    prot = Rot([C.ps([128, 512], "pr") for _ in range(6)])
    for t in range(16):
        c0 = t * 512
        xt = xts.next()
        dma_in(P, "sp", xt, xT_v[:, :, c0:c0 + 512])
        tm, tg = tabs.next()
        dma_in(P, "sp", tm, t_mla[:, :, c0:c0 + 512].rearrange("a p t -> p a t"))
        dma_in(P, "sp", tg, t_gqa[:, :, c0:c0 + 512].rearrange("a p t -> p a t"))
        xnorm(P, K, xt, hT, 0, 512, ssq)
        knst, krst, kgst, vast, vgst = kst.next()
        pck = [prot.next(), prot.next()]
        for j in range(2):
            mm_group(P, pck[j].t[:, :], pck[j].b,
                     [(wkv.t[:, kc, j * 128:(j + 1) * 128], hT.t[:, kc, :]) for kc in range(16)], [wkv.b, hT.b])
        ssq_ = sq_accum(P, K, [(pck[j].t[:, :], pck[j].b, 128) for j in range(2)], ssq, 512)
        r = rstd_of(P, K, ssq_, 256.0, 512)
        for j in range(2):
            apply_plain(P, K, pck[j].t[:, :], pck[j].b, 68 + j, r, ckvn.t[:, j, :], ckvn.b, 512)
        for h in range(8):
            pk = prot.next()
            mm_group(P, pk.t[:, :], pk.b,
                     [(wuk.t[:, kc, h * 128:(h + 1) * 128], ckvn.t[:, kc, :]) for kc in range(2)], [wuk.b, ckvn.b])
            ssq_ = sq_accum(P, K, [(pk.t[:, :], pk.b, 128)], ssq, 512)
            r = rstd_of(P, K, ssq_, 128.0, 512)
            apply_plain(P, K, pk.t[:, :], pk.b, 73, r, knst.t[:, h, :], knst.b, 512)
        for s in range(4):
            for hf in range(2):
                pv = prot.next()
                mm_group(P, pv.t[:, :], pv.b,
                         [(ckvn.t[:, kc, s * 128:(s + 1) * 128], wuv.t[:, kc, hf * 512:(hf + 1) * 512]) for kc in range(2)],
                         [wuv.b, ckvn.b])
                eng = "act" if (s * 2 + hf) % 2 == 0 else "dve"
                if eng == "act":
                    P.op("act", lambda e, o=vast.t[:, s, hf * 512:(hf + 1) * 512], a=pv.t[:, :]:
                         e.activation(out=o, in_=a, func=AF.Copy), reads=[pv.b], writes=[vast.b])
                else:
                    P.op("dve", lambda e, o=vast.t[:, s, hf * 512:(hf + 1) * 512], a=pv.t[:, :]:
                         e.tensor_copy(out=o, in_=a), reads=[pv.b], writes=[vast.b])
        pk = prot.next()
        mm_group(P, pk.t[0:64, :], pk.b, [(wkv.t[:, kc, 256:320], hT.t[:, kc, :]) for kc in range(16)], [wkv.b, hT.b])
        ssq_ = sq_accum(P, K, [(pk.t[0:64, :], pk.b, 64)], ssq, 512, mrows=64)
        r = rstd_of(P, K, ssq_, 64.0, 512, rows=64)
        psw = prot.next()
        apply_rope(P, K, pk, 512, 64, pm_mla, 74, (tm.t[:, 0, :], tm.b), (tm.t[:, 1, :], tm.b), r,
                   krst.t[:, :], krst.b, psw)
        for j in range(2):
            pk = prot.next()
            mm_group(P, pk.t[:, :], pk.b,
                     [(wkv.t[:, kc, 320 + j * 128:320 + (j + 1) * 128], hT.t[:, kc, :]) for kc in range(16)],
                     [wkv.b, hT.b])
            ssq_ = sq_accum(P, K, [(pk.t[:, :], pk.b, 128)], ssq, 512)
            r = rstd_of(P, K, ssq_, 128.0, 512)
            psw = prot.next()
            apply_rope(P, K, pk, 512, 128, pm_gqa, 78, (tg.t[:, 0, :], tg.b), (tg.t[:, 1, :], tg.b), r,
                       kgst.t[:, j, :], kgst.b, psw)
        for s in range(4):
            pv = prot.next()
            mm_group(P, pv.t[:, 0:256], pv.b,
                     [(hT.t[:, kc, s * 128:(s + 1) * 128], wkv.t[:, kc, 576:832]) for kc in range(16)], [wkv.b, hT.b])
            P.op("act", lambda e, o=vgst.t[:, s, :], a=pv.t[:, 0:256]: e.activation(out=o, in_=a, func=AF.Copy),
                 reads=[pv.b], writes=[vgst.b])
        dma_out(P, knst, kn_d[:, :, c0:c0 + 512].rearrange("h p t -> p h t"), knst.t[:])
        dma_out(P, krst, kr_d[:, c0:c0 + 512], krst.t[:])
        dma_out(P, kgst, kg_d[:, :, c0:c0 + 512].rearrange("h p t -> p h t"), kgst.t[:])
        for s in range(4):
            dma_out(P, vast, va_d[:, :, 4 * t + s, :].rearrange("h p d -> p h d"),
                    vast.t[:, s, :].rearrange("p (h d) -> p h d", h=8))
        for s in range(4):
            dma_out(P, vgst, vg_d[:, :, 4 * t + s, :].rearrange("h p d -> p h d"),
                    vgst.t[:, s, :].rearrange("p (h d) -> p h d", h=2))
    C.close()

    C = Ctx(nc, sems, "p1")
    P = C.P
    K = Common(C, gcols)
    wq = C.sb([128, 16, 1536], BF16, "wq")
    dma_in(P, "pool", wq, w_q0.rearrange("p (k c) -> p k c", k=16))
    wuq = C.sb([128, 4, 1536], BF16, "wuq")
    dma_in(P, "pool", wuq, w_uq.rearrange("p (k c) -> p k c", k=4))
    pm = C.sb([128, 3, 128], BF16, "pm")
    dma_in(P, "pool", pm, perms.rearrange("a p m -> p a m"))
    pm_mla = TB(pm.t[:, 0, :]); pm_mla.b = pm.b
    pm_gqa = TB(pm.t[:, 1, :]); pm_gqa.b = pm.b
    xts = Rot([C.sb([128, 16, 512], F32, "xt")])
    hT = C.sb([128, 16, 512], BF16, "hT")
    tabs = Rot([(C.sb([64, 2, 512], F32, "tm"), C.sb([128, 2, 512], F32, "tg")) for _ in range(2)])
    cqn = C.sb([128, 4, 512], BF16, "cqn")
    qst = Rot([(C.sb([128, 8, 512], BF16, "qnst"), C.sb([64, 8, 512], BF16, "qrst"),
                C.sb([128, 8, 512], BF16, "qgst")) for _ in range(1)])
    ssq = Rot([C.ps([128, 512], "ssq") for _ in range(2)])
    prot = Rot([C.ps([128, 512], "pr") for _ in range(6)])
    for t in range(5):
        c0 = t * 512
        n = min(512, EXT - c0)
        xt = xts.next()
        dma_in(P, "sp", xt, xT_v[:, :, c0:c0 + n], xt.t[:, :, 0:n])
        tm, tg = tabs.next()
        dma_in(P, "sp", tm, t_mla[:, :, c0:c0 + n].rearrange("a p t -> p a t"), tm.t[:, :, 0:n])
        dma_in(P, "sp", tg, t_gqa[:, :, c0:c0 + n].rearrange("a p t -> p a t"), tg.t[:, :, 0:n])
        xnorm(P, K, xt, hT, 0, n, ssq)
        qnst, qrst, qgst = qst.next()
        pcq = [prot.next() for _ in range(4)]
        for j in range(4):
            mm_group(P, pcq[j].t[:, 0:n], pcq[j].b,
                     [(wq.t[:, kc, j * 128:(j + 1) * 128], hT.t[:, kc, 0:n]) for kc in range(16)], [wq.b, hT.b])
        ssq_ = sq_accum(P, K, [(pcq[j].t[:, 0:n], pcq[j].b, 128) for j in range(4)], ssq, n)
        r = rstd_of(P, K, ssq_, 512.0, n)
        for j in range(4):
            apply_plain(P, K, pcq[j].t[:, 0:n], pcq[j].b, 64 + j, r, cqn.t[:, j, 0:n], cqn.b, n)
        for h in range(8):
            pn = prot.next()
            mm_group(P, pn.t[:, 0:n], pn.b,
                     [(wuq.t[:, kc, h * 192:h * 192 + 128], cqn.t[:, kc, 0:n]) for kc in range(4)], [wuq.b, cqn.b])
            pr = prot.next()
            mm_group(P, pr.t[0:64, 0:n], pr.b,
                     [(wuq.t[:, kc, h * 192 + 128:h * 192 + 192], cqn.t[:, kc, 0:n]) for kc in range(4)], [wuq.b, cqn.b])
            ssq_ = sq_accum(P, K, [(pn.t[:, 0:n], pn.b, 128), (pr.t[0:64, 0:n], pr.b, 64)], ssq, n)
            r = rstd_of(P, K, ssq_, 192.0, n)
            apply_plain(P, K, pn.t[:, 0:n], pn.b, 70, r, qnst.t[:, h, 0:n], qnst.b, n)
            psw = prot.next()
            apply_rope(P, K, pr, n, 64, pm_mla, 71, (tm.t[:, 0, 0:n], tm.b), (tm.t[:, 1, 0:n], tm.b), r,
                       qrst.t[:, h, 0:n], qrst.b, psw)
        for h in range(8):
            pg = prot.next()
            mm_group(P, pg.t[:, 0:n], pg.b,
                     [(wq.t[:, kc, 512 + h * 128:512 + (h + 1) * 128], hT.t[:, kc, 0:n]) for kc in range(16)],
                     [wq.b, hT.b])
            ssq_ = sq_accum(P, K, [(pg.t[:, 0:n], pg.b, 128)], ssq, n)
            r = rstd_of(P, K, ssq_, 128.0, n)
            psw = prot.next()
            apply_rope(P, K, pg, n, 128, pm_gqa, 76, (tg.t[:, 0, 0:n], tg.b), (tg.t[:, 1, 0:n], tg.b), r,
                       qgst.t[:, h, 0:n], qgst.b, psw)
        dma_out(P, qnst, qn_d[:, :, c0:c0 + n].rearrange("h p t -> p h t"), qnst.t[:, :, 0:n])
        dma_out(P, qrst, qr_d[:, :, c0:c0 + n].rearrange("h p t -> p h t"), qrst.t[:, :, 0:n])
        dma_out(P, qgst, qg_d[:, :, c0:c0 + n].rearrange("h p t -> p h t"), qgst.t[:, :, 0:n])
    C.close()

    C = Ctx(nc, sems, "p2")
    P = C.P
    ones = C.sb([128, 128], BF16, "ones")
    P.op("dve", lambda e: e.memset(ones.t[:], 1.0), writes=[ones.b])
    krs = C.sb([128, S], BF16, "krs")
    P.op("dve", lambda e: e.memset(krs.t[64:128, :], 0.0), writes=[krs.b])
    dma_in(P, "sp", krs, kr_d, krs.t[0:64, :])
    cvb = Buf("cv")
    cvl = [(wb_o0[i], w_o0[i]) for i in range(4)]
    cvl += [(wb_up[0, i], w_up[0, i]) for i in range(16)]
    cvl += [(wb_dn[0, hf, i], w_dn[0, hf, i]) for hf in range(2) for i in range(8)]
    cvl += [(wb_qkv1[i], w_qkv1[i]) for i in range(5)]
    cvl += [(wb_o1[i], w_o1[i]) for i in range(4)]
    cvl += [(wb_up[1, i], w_up[1, i]) for i in range(16)]
    cvl += [(wb_dn[1, hf, i], w_dn[1, hf, i]) for hf in range(2) for i in range(8)]
    for (dst_, src_) in cvl:
        P.op("pool", lambda e, d=dst_, s_=src_: e.dma_start(out=d, in_=s_), writes=[cvb], dma=cvb)
    kts = Rot([C.sb([128, S], BF16, "kt") for _ in range(2)])
    vts = Rot([C.sb([128, 64, 128], BF16, "vt") for _ in range(2)])
    qns = Rot([C.sb([128, EXT], BF16, "qn") for _ in range(2)])
    qrs = Rot([C.sb([128, EXT], BF16, "qr") for _ in range(2)])
    for _q in qrs.items:
        P.op("dve", lambda e, _q=_q: e.memset(_q.t[64:128, :], 0.0), writes=[_q.b])
    pts = Rot([C.sb([128, 1024], BF16, "pt") for _ in range(4)])
    ones32 = C.sb([128, 128], F32, "ones32")
    P.op("dve", lambda e: e.memset(ones32.t[:], 1.0), writes=[ones32.b])
    accAs = Rot([C.sb([128, 2, 512], F32, "accA") for _ in range(2)])
    accBs = Rot([C.sb([128, 2, 512], F32, "accB") for _ in range(1)])
    hls = Rot([C.sb([128, 2, 512], BF16, "hl") for _ in range(2)])
    rds = Rot([C.sb([128, 512], F32, "rd") for _ in range(2)])
    ost = Rot([C.sb([128, 512], BF16, "ost") for _ in range(2)])
    sps = Rot([C.ps([128, 1024], "S") for _ in range(2)])
    ops_ = Rot([C.ps([128, 512], "O") for _ in range(2)])
    dps = Rot([C.ps([128, 512], "Dn") for _ in range(2)])
    qtiles = [(0, 512), (512, 512), (1024, 512), (1536, 512), (2048, 256)]
    kt = vt = None
    pend = []

    def flush_pend():
        while pend:
            o_, dst_, n_ = pend.pop(0)
            dma_out(P, o_, dst_, o_.t[:, 0:n_], q="act")
    for job in range(16):
        mla = job < 8
        if mla:
            kt = kts.next()
            dma_in(P, "sp", kt, kn_d[job])
            vt = vts.next()
            dma_in(P, "sp", vt, va_d[job])
            qn = qns.next()
            dma_in(P, "sp", qn, qn_d[job])
            qr = qrs.next()
            dma_in(P, "sp", qr, qr_d[job], qr.t[0:64, :])
            scale = 192.0 ** -0.5
        else:
            j = job - 8
            if j % 4 == 0:
                kt = kts.next()
                dma_in(P, "sp", kt, kg_d[j // 4])
                vt = vts.next()
                dma_in(P, "sp", vt, vg_d[j // 4])
            qn = qns.next()
            dma_in(P, "sp", qn, qg_d[j])
            scale = 128.0 ** -0.5
        for (q0, n) in qtiles:
            Ops = ops_.next()
            Dps = dps.next()

            def s_pair(kp, mla=mla, kt=kt, qn=qn, qr=(qr if mla else None), q0=q0, n=n):
                Sp = sps.next()

                def fn(e, Sp=Sp):
                    ins = None
                    for i in range(2):
                        kb = 2 * kp + i
                        o = Sp.t[:, i * n:(i + 1) * n] if n == 256 else Sp.t[:, i * 512:(i + 1) * 512]
                        ins = e.matmul(o, lhsT=kt.t[:, kb * 128:(kb + 1) * 128], rhs=qn.t[:, q0:q0 + n],
                                       start=True, stop=(not mla))
                        if mla:
                            ins = e.matmul(o, lhsT=krs.t[:, kb * 128:(kb + 1) * 128], rhs=qr.t[:, q0:q0 + n],
                                           start=False, stop=True)
                    return ins
                rd = [kt.b, qn.b] + ([krs.b, qr.b] if mla else [])
                P.op("pe", fn, reads=rd, writes=[Sp.b])
                return Sp

            def pv_pair(kp, Sp, vt=vt, n=n, Ops=Ops, Dps=Dps, scale=scale):
                pt = pts.next()
                w = 2 * n
                P.op("act", lambda e: e.activation(out=pt.t[:, 0:w], in_=Sp.t[:, 0:w], func=AF.Exp, scale=scale),
                     reads=[Sp.b], writes=[pt.b])

                def fn(e):
                    ins = None
                    for i in range(2):
                        kb = 2 * kp + i
                        first = (kb == 0)
                        last = (kb == 63)
                        e.matmul(Ops.t[:, 0:n], lhsT=vt.t[:, kb, :], rhs=pt.t[:, i * n:(i + 1) * n],
                                 start=first, stop=last)
                        ins = e.matmul(Dps.t[:, 0:n], lhsT=ones.t[:, :], rhs=pt.t[:, i * n:(i + 1) * n],
                                       start=first, stop=last)
                    return ins
                P.op("pe", fn, reads=[vt.b, pt.b, ones.b], writes=[Ops.b, Dps.b])

            prev = s_pair(0)
            for kp in range(32):
                nxt = s_pair(kp + 1) if kp + 1 < 32 else None
                pv_pair(kp, prev)
                prev = nxt
                if kp == 4:
                    flush_pend()
            rd = rds.next()
            P.op("dve", lambda e, rd=rd, Dps=Dps, n=n: e.reciprocal(out=rd.t[:, 0:n], in_=Dps.t[:, 0:n]),
                 reads=[Dps.b], writes=[rd.b])
            o = ost.next()
            P.op("dve", lambda e, o=o, rd=rd, Ops=Ops, n=n: e.tensor_tensor(out=o.t[:, 0:n], in0=Ops.t[:, 0:n],
                                                                           in1=rd.t[:, 0:n], op=ALU.mult),
                 reads=[Ops.b, rd.b], writes=[o.b])
            pend.append((o, mg_d[job * 128:(job + 1) * 128, q0:q0 + n], n))
    flush_pend()
    C.close()

    def tail_phase(tag, layer):
        C = Ctx(nc, sems, tag)
        P = C.P
        K = Common(C, gcols, with_bd=(layer == 0))
        slabs = Rot([C.sb([128, 8192], BF16, "slab") for _ in range(4)])
        xt = C.sb([128, 16, 512], F32, "xt")
        hT = C.sb([128, 16, 512], BF16, "hT")
        hid = C.sb([128, 32, 512], BF16, "hid")
        rls = Rot([C.sb([128, 512], F32, "rl") for _ in range(3)])
        ssq = Rot([C.ps([128, 512], "ssq") for _ in range(2)])
        prot = Rot([C.ps([128, 512], "pr") for _ in range(6)])
        if layer == 0:
            pm = C.sb([128, 128], BF16, "pm")
            dma_in(P, "pool", pm, perms[2])
            tsw = Rot([C.sb([128, 2, 512], F32, "tsw") for _ in range(2)])
            qst = Rot([C.sb([128, 512], BF16, "qst") for _ in range(3)])
            vst = Rot([C.sb([128, 4, 256], BF16, "vst") for _ in range(1)])
            ntiles = 5
            wo, mgsrc, xsrc = wb_o0, mg_d, None
        else:
            ntiles = 4
            wo, mgsrc = wb_o1, m1_d
        outtoks = []

        def load_slab(src_ap):
            sl = slabs.next()
            dma_in(P, "sp", sl, src_ap)
            return sl

        mg_v = mgsrc.rearrange("(k p) t -> p k t", p=128)
        for t in range(ntiles):
            if layer == 0:
                c0 = t * 512
                n = min(512, EXT - c0)
                dma_in(P, "pool", hT, mg_v[:, :, c0:c0 + n], hT.t[:, :, 0:n])
                if t == 0:
                    dma_in(P, "pool", xt, xT_v[:, :, c0:c0 + n], xt.t[:, :, 0:n])
            else:
                c0 = t * 512
                n = 512
                if t == 0:
                    dma_in(P, "pool", hT, mg_v[:, :, c0:c0 + n])
                dma_in(P, "pool", xt, x1_d.rearrange("(k p) t -> p k t", p=128)[:, :, 128 + c0:128 + c0 + n])
            for s in range(4):
                sl = load_slab(wo[s])
                slv = sl.t[:].rearrange("p (k c) -> p k c", k=16)
                for j in range(4):
                    c = 4 * s + j
                    pp = prot.next()
                    mm_group(P, pp.t[:, 0:n], pp.b,
                             [(slv[:, kc, j * 128:(j + 1) * 128], hT.t[:, kc, 0:n]) for kc in range(16)], [sl.b, hT.b])
                    P.op("dve", lambda e, c=c, pp=pp, n=n: e.tensor_tensor(out=xt.t[:, c, 0:n], in0=pp.t[:, 0:n],
                                                                           in1=xt.t[:, c, 0:n], op=ALU.add),
                         reads=[pp.b, xt.b], writes=[xt.b])
            xnorm(P, K, xt, hT, 16 + 32 * layer, n, ssq)
            for hf in range(2):
                for s in range(8):
                    sl = load_slab(wb_up[layer, hf * 8 + s])
                    slv = sl.t[:].rearrange("p (k c) -> p k c", k=16)
                    for j in range(4):
                        hc = s * 4 + j
                        pp = prot.next()
                        mm_group(P, pp.t[:, 0:n], pp.b,
                                 [(slv[:, kc, j * 128:(j + 1) * 128], hT.t[:, kc, 0:n]) for kc in range(16)],
                                 [sl.b, hT.b])
                        rl = rls.next()
                        P.op("act", lambda e, rl=rl, pp=pp, n=n: e.activation(out=rl.t[:, 0:n], in_=pp.t[:, 0:n],
                                                                              func=AF.Relu),
                             reads=[pp.b], writes=[rl.b])
                        P.op("dve", lambda e, rl=rl, hc=hc, n=n: e.tensor_tensor(out=hid.t[:, hc, 0:n], in0=rl.t[:, 0:n],
                                                                              in1=rl.t[:, 0:n], op=ALU.mult),
                             reads=[rl.b], writes=[hid.b])
                if layer == 1 and hf == 1 and t + 1 < ntiles:
                    dma_in(P, "pool", hT, mg_v[:, :, c0 + 512:c0 + 1024])
                for s in range(8):
                    sl = load_slab(wb_dn[layer, hf, s])
                    slv = sl.t[:].rearrange("p (k c) -> p k c", k=32)
                    for j in range(2):
                        c = 2 * s + j
                        pp = prot.next()
                        mm_group(P, pp.t[:, 0:n], pp.b,
                                 [(slv[:, kc, j * 128:(j + 1) * 128], hid.t[:, kc, 0:n]) for kc in range(32)],
                                 [sl.b, hid.b])
                        P.op("dve", lambda e, c=c, pp=pp, n=n: e.tensor_tensor(out=xt.t[:, c, 0:n], in0=pp.t[:, 0:n],
                                                                               in1=xt.t[:, c, 0:n], op=ALU.add),
                             reads=[pp.b, xt.b], writes=[xt.b])
            if layer == 1:
                outtoks.append(dma_out(P, xt, yT.rearrange("(k p) t -> p k t", p=128)[:, :, c0:c0 + n], xt.t[:, :, 0:n]))
                continue
            dma_out(P, xt, x1_d.rearrange("(k p) t -> p k t", p=128)[:, :, c0:c0 + n], xt.t[:, :, 0:n])
            ts_ = tsw.next()
            dma_in(P, "pool", ts_, t_swa[:, :, c0:c0 + n].rearrange("a p t -> p a t"), ts_.t[:, :, 0:n])
            xnorm(P, K, xt, hT, 32, n, ssq)
            if t + 1 < ntiles:
                c1 = (t + 1) * 512
                n1 = min(512, EXT - c1)
                dma_in(P, "pool", xt, xT_v[:, :, c1:c1 + n1], xt.t[:, :, 0:n1])
            nb = n // 128
            b0 = c0 // 128
            for s in range(5):
                sl = load_slab(wb_qkv1[s])
                slv = sl.t[:].rearrange("p (k c) -> p k c", k=16)
                nch = 4 if s < 4 else 2
                for j in range(nch):
                    pq = prot.next()
                    mm_group(P, pq.t[:, 0:n], pq.b,
                             [(slv[:, kc, j * 128:(j + 1) * 128], hT.t[:, kc, 0:n]) for kc in range(16)], [sl.b, hT.b])
                    ssq_ = sq_accum(P, K, [(pq.t[:, 0:n], pq.b, 128)], ssq, n, lhs=K.bd)
                    r = rstd_of(P, K, ssq_, 64.0, n)
                    psw = prot.next()
                    qs = qst.next()
                    pmtb = pm
                    apply_rope(P, K, pq, n, 128, pmtb, 80 if s < 4 else 82,
                               (ts_.t[:, 0, 0:n], ts_.b), (ts_.t[:, 1, 0:n], ts_.b), r, qs.t[:, 0:n], qs.b, psw, eng2="dve")
                    if s < 4:
                        c = 4 * s + j
                        for u in range(2):
                            h = 2 * c + u
                            kvh, hh = h // 8, h % 8
                            dma_out(P, qs, q1_d[kvh, :, b0:b0 + nb, hh, :],
                                    qs.t[u * 64:(u + 1) * 64, 0:n].rearrange("p (b q) -> p b q", q=128))
                    else:
                        for u in range(2):
                            dma_out(P, qs, k1_d[2 * j + u, :, c0:c0 + n], qs.t[u * 64:(u + 1) * 64, 0:n])
                if s == 4:
                    vs = vst.next()
                    for sb_ in range(nb):
                        pv = prot.next()
                        mm_group(P, pv.t[:, 0:256], pv.b,
                                 [(hT.t[:, kc, sb_ * 128:(sb_ + 1) * 128], slv[:, kc, 256:512]) for kc in range(16)],
                                 [sl.b, hT.b])
                        P.op("act", lambda e, vs=vs, sb_=sb_, pv=pv: e.activation(out=vs.t[:, sb_, :], in_=pv.t[:, 0:256],
                                                                                  func=AF.Copy),
                             reads=[pv.b], writes=[vs.b])
                    dma_out(P, vs, v1_d[:, b0:b0 + nb, :], vs.t[:, 0:nb, :])
        C.close()
        return outtoks

    tail_phase("p3", 0)

    C = Ctx(nc, sems, "p4")
    P = C.P
    ones = C.sb([128, 128], BF16, "ones")
    P.op("dve", lambda e: e.memset(ones.t[:], 1.0), writes=[ones.b])
    k1s = C.sb([64, 4, EXT], BF16, "k1s")
    dma_in(P, "sp", k1s, k1_d.rearrange("k p t -> p k t"))
    v1s = C.sb([128, 18, 256], BF16, "v1s")
    dma_in(P, "sp", v1s, v1_d)
    mks = C.sb([128, 4, 512], BF16, "mks")
    dma_in(P, "pool", mks, masks.rearrange("a p t -> p a t"))
    esk = C.sb([64, 32], F32, "esk")
    dma_in(P, "sp", esk, sinkrep)
    P.op("act", lambda e: e.activation(out=esk.t[:], in_=esk.t[:], func=AF.Exp), reads=[esk.b], writes=[esk.b])
    eskx = C.sb([64, 32, 128], F32, "eskx")
    P.op("dve", lambda e: e.memset(eskx.t[:], 0.0), writes=[eskx.b])
    for h in range(32):
        P.op("dve", lambda e, h=h: e.tensor_scalar(out=eskx.t[:, h, :], in0=eskx.t[:, h, :], scalar1=esk.t[:, h:h + 1],
                                                   scalar2=None, op0=ALU.add), reads=[esk.b, eskx.b], writes=[eskx.b])
    q1s = Rot([C.sb([64, 18, 8, 128], BF16, "q1s") for _ in range(2)])
    pts = Rot([C.sb([128, 1536], BF16, "pt") for _ in range(3)])
    dns = Rot([C.sb([64, 512], F32, "dn") for _ in range(2)])
    dls = Rot([C.sb([64, 512], F32, "dl") for _ in range(2)])
    ost = Rot([C.sb([64, 512], BF16, "ost") for _ in range(2)])
    Sp = C.ps([128, 1536], "S")
    opsr = Rot([C.ps([128, 512], "O") for _ in range(2)])
    dpsr = Rot([C.ps([128, 512], "Dn") for _ in range(2)])
    sc1 = 64.0 ** -0.5
    q1_of = {}
    its = []
    for kv in range(4):
        for jq in range(16):
            for hg in range(2):
                its.append((kv, jq, hg))

    def s_mm(it):
        kv, jq, hg = it
        if kv not in q1_of:
            q1 = q1s.next()
            dma_in(P, "sp", q1, q1_d[kv])
            q1_of[kv] = q1
        q1 = q1_of[kv]

        def fnS(e):
            ins = None
            for kb in range(3):
                ins = e.matmul(Sp.t[:, kb * 512:(kb + 1) * 512],
                               lhsT=k1s.t[:, kv, (jq + kb) * 128:(jq + kb + 1) * 128],
                               rhs=q1.t[:, jq + 1, hg * 4:(hg + 1) * 4, :].rearrange("p h q -> p (h q)"),
                               start=True, stop=True)
            return ins
        P.op("pe", fnS, reads=[k1s.b, q1.b], writes=[Sp.b])

    def exp_mask(it):
        kv, jq, hg = it
        pt = pts.next()
        P.op("act", lambda e: e.activation(out=pt.t[:, :], in_=Sp.t[:, :], func=AF.Exp, scale=sc1),
             reads=[Sp.b], writes=[pt.b])
        mP = 2 if jq == 0 else 0
        mN = 3 if jq == 15 else 1
        P.op("dve", lambda e: e.tensor_tensor(out=pt.t[:, 0:512], in0=pt.t[:, 0:512], in1=mks.t[:, mP, :],
                                              op=ALU.mult), reads=[pt.b, mks.b], writes=[pt.b])
        P.op("pool", lambda e: e.tensor_tensor(out=pt.t[:, 1024:1536], in0=pt.t[:, 1024:1536], in1=mks.t[:, mN, :],
                                               op=ALU.mult), reads=[pt.b, mks.b], writes=[pt.b])
        return pt

    def pv_norm(it, pt):
        kv, jq, hg = it
        Ops = opsr.next()
        Dps = dpsr.next()

        def fnP(e):
            ins = None
            for kb in range(3):
                e.matmul(Ops.t[0:64, :], lhsT=v1s.t[:, jq + kb, kv * 64:(kv + 1) * 64],
                         rhs=pt.t[:, kb * 512:(kb + 1) * 512], start=(kb == 0), stop=(kb == 2))
            for kb in range(3):
                ins = e.matmul(Dps.t[0:64, :], lhsT=ones.t[:, 0:64], rhs=pt.t[:, kb * 512:(kb + 1) * 512],
                               start=(kb == 0), stop=(kb == 2))
            return ins
        P.op("pe", fnP, reads=[v1s.b, pt.b, ones.b], writes=[Ops.b, Dps.b])
        dn = dns.next()
        h0 = kv * 8 + hg * 4
        P.op("dve", lambda e: e.tensor_tensor(
            out=dn.t[:, :], in0=Dps.t[0:64, :], in1=eskx.t[:, h0:h0 + 4, :].rearrange("p h q -> p (h q)"),
            op=ALU.add), reads=[Dps.b, eskx.b], writes=[dn.b])
        dl = dls.next()
        P.op("act", lambda e: e.activation(out=dl.t[:, :], in_=dn.t[:, :], func=AF.Ln),
             reads=[dn.b], writes=[dl.b])
        P.op("act", lambda e: e.activation(out=dn.t[:, :], in_=dl.t[:, :], func=AF.Exp, scale=-1.0),
             reads=[dl.b], writes=[dn.b])
        o = ost.next()
        P.op("dve", lambda e: e.tensor_tensor(out=o.t[:, :], in0=Ops.t[0:64, :], in1=dn.t[:, :], op=ALU.mult),
             reads=[Ops.b, dn.b], writes=[o.b])
        r0 = (kv * 8 + hg * 4) * 64
        dma_out(P, o, m1_d[r0:r0 + 256, jq * 128:(jq + 1) * 128].rearrange("(h d) q -> d h q", d=64),
                o.t[:, :].rearrange("p (h q) -> p h q", q=128), q="sp")

    s_mm(its[0])
    pt_cur = exp_mask(its[0])
    for i, it in enumerate(its):
        if i + 1 < len(its):
            s_mm(its[i + 1])
            pt_nxt = exp_mask(its[i + 1])
        else:
            pt_nxt = None
        pv_norm(it, pt_cur)
        pt_cur = pt_nxt
    C.close()

    tail_phase("p5", 1)
    es.close()
    return nc


def _slab(W, cols=None):
    W = np.asarray(W, np.float32)
    if cols is not None:
        W = W[:, cols]
    Kd, n = W.shape
    return np.ascontiguousarray(W.reshape(Kd // 128, 128, n).transpose(1, 0, 2).reshape(128, (Kd // 128) * n))


def _rope_tab(pos, dim, theta):
    inv = np.float32(theta) ** (-np.arange(0, dim, 2, dtype=np.float32) / np.float32(dim))
    ang = pos.astype(np.float32)[:, None] * inv[None, :].astype(np.float32)
    ang = ang.astype(np.float32).astype(np.float64)
    return np.cos(ang).astype(np.float32).T, np.sin(ang).astype(np.float32).T


def _perm_lhsT(perm):
    n = len(perm)
    m = np.zeros((128, 128), np.float32)
    for o in range(n):
        m[perm[o], o] = 1.0
    return m


_NC_CACHE = {}


def kernel(x, even_norm, even_w_in, mla_q_lat_norm, mla_kv_lat_norm, mla_w_uq, mla_w_ukv,
           mla_q_norm, mla_k_nope_norm, mla_k_rope_norm, gqa_q_norm, gqa_k_norm, even_w_out,
           odd_norm, odd_w_qkv, swa_q_norm, swa_k_norm, swa_sink, odd_w_out,
           mlp_norm, mlp_w_up, mlp_w_down):
    f = lambda a: np.asarray(a, np.float32)
    x = f(x)
    p_mla = np.concatenate([np.arange(32, 64), np.arange(0, 32)])
    p_gqa = np.concatenate([p_mla, 64 + p_mla])
    p_sw64 = np.concatenate([np.arange(8, 16), np.arange(0, 8), np.arange(16, 64)])
    p_swa = np.concatenate([p_sw64, 64 + p_sw64])
    perms = np.stack([_perm_lhsT(p_mla), _perm_lhsT(p_gqa), _perm_lhsT(p_swa)])
    g = np.ones((128, 84), np.float32)
    g[:, 0:16] = f(even_norm)[0].reshape(16, 128).T
    g[:, 16:32] = f(mlp_norm)[0].reshape(16, 128).T
    g[:, 32:48] = f(odd_norm)[0].reshape(16, 128).T
    g[:, 48:64] = f(mlp_norm)[1].reshape(16, 128).T
    g[:, 64:68] = f(mla_q_lat_norm)[0].reshape(4, 128).T
    g[:, 68:70] = f(mla_kv_lat_norm)[0].reshape(2, 128).T
    qn_ = f(mla_q_norm)[0]
    g[:, 70] = qn_[0:128]
    g[0:64, 71] = qn_[128:192]
    g[0:64, 72] = qn_[128:192][p_mla]
    g[:, 73] = f(mla_k_nope_norm)[0]
    kr_ = f(mla_k_rope_norm)[0]
    g[0:64, 74] = kr_
    g[0:64, 75] = kr_[p_mla]
    gq_ = f(gqa_q_norm)[0]
    g[:, 76] = gq_
    g[:, 77] = gq_[p_gqa]
    gk_ = f(gqa_k_norm)[0]
    g[:, 78] = gk_
    g[:, 79] = gk_[p_gqa]
    sq_ = np.tile(f(swa_q_norm)[0], 2)
    g[:, 80] = sq_
    g[:, 81] = sq_[p_swa]
    sk_ = np.tile(f(swa_k_norm)[0], 2)
    g[:, 82] = sk_
    g[:, 83] = sk_[p_swa]
    w_in = f(even_w_in)[0]
    cols_kv = np.concatenate([np.arange(512, 768), np.arange(768, 832), np.arange(1856, 2112), np.arange(2112, 2368)])
    cols_q = np.concatenate([np.arange(0, 512), np.arange(832, 1856)])
    wkv = _slab(w_in, cols_kv)
    wq0 = _slab(w_in, cols_q)
    wuq = _slab(f(mla_w_uq)[0])
    ukv = f(mla_w_ukv)[0]
    ck = np.concatenate([np.arange(h * 256, h * 256 + 128) for h in range(8)])
    cv = np.concatenate([np.arange(h * 256 + 128, h * 256 + 256) for h in range(8)])
    wuk = _slab(ukv, ck)
    wuv = _slab(ukv, cv)
    wo0 = np.stack([_slab(f(even_w_out)[0][:, s * 512:(s + 1) * 512]) for s in range(4)])
    wo1 = np.stack([_slab(f(odd_w_out)[0][:, s * 512:(s + 1) * 512]) for s in range(4)])
    wqkv1 = np.stack([_slab(f(odd_w_qkv)[0][:, s * 512:(s + 1) * 512]) for s in range(5)])
    up = f(mlp_w_up)
    wup = np.stack([np.stack([_slab(up[l][:, s * 512:(s + 1) * 512]) for s in range(16)]) for l in range(2)])
    dn = f(mlp_w_down)
    wdn = np.stack([np.stack([np.stack([_slab(dn[l][hf * 4096:(hf + 1) * 4096, s * 256:(s + 1) * 256])
                                        for s in range(8)]) for hf in range(2)]) for l in range(2)])
    sinkrep = np.ascontiguousarray(np.broadcast_to(f(swa_sink)[0][None, :], (64, 32)))
    kk = np.arange(128)[:, None]
    qq = np.arange(128)[None, :]
    mP = np.tile((kk >= qq).astype(np.float32), (1, 4))
    mN = np.tile((kk <= qq).astype(np.float32), (1, 4))

    in_maps = []
    for c in range(8):
        b, ch = c // 4, c % 4
        ext = list(range(16 * ch - 1, 16 * ch + 17))
        if ch == 0:
            ext[0] = 17
        if ch == 3:
            ext[17] = 46
        rest = [i for i in range(64) if i not in ext]
        order = np.array(ext + rest)
        tok = (order[:, None] * 128 + np.arange(128)[None, :]).reshape(-1)
        xTc = np.ascontiguousarray(x[b][tok].T)
        cm, sm = _rope_tab(tok, 64, 500000.0)
        t_mla = np.stack([np.concatenate([cm, cm]), np.concatenate([-sm, sm])])
        cr, sr = _rope_tab(tok // 64, 64, 10000.0)
        cc, sc = _rope_tab(tok % 64, 64, 10000.0)
        t_gqa = np.stack([np.concatenate([cr, cr, cc, cc]), np.concatenate([-sr, sr, -sc, sc])])
        te = tok[:EXT]
        cs, ss = _rope_tab(te, 16, 500000.0)
        one = np.ones((48, EXT), np.float32)
        zero = np.zeros((48, EXT), np.float32)
        c64 = np.concatenate([cs, cs, one])
        s64 = np.concatenate([-ss, ss, zero])
        t_swa = np.stack([np.concatenate([c64, c64]), np.concatenate([s64, s64])])
        masks = np.stack([mP, mN, mP * (0.0 if ch == 0 else 1.0), mN * (0.0 if ch == 3 else 1.0)])
        in_maps.append({
            "xT": xTc, "gcols": g, "w_kv": wkv, "w_q0": wq0, "w_uq": wuq, "w_uk": wuk, "w_uv": wuv,
            "w_o0": wo0, "w_up": wup, "w_dn": wdn, "w_qkv1": wqkv1, "w_o1": wo1,
            "t_mla": np.ascontiguousarray(t_mla, np.float32), "t_gqa": np.ascontiguousarray(t_gqa, np.float32),
            "t_swa": np.ascontiguousarray(t_swa, np.float32), "perms": perms,
            "masks": np.ascontiguousarray(masks, np.float32), "sinkrep": sinkrep,
        })
    if "nc" not in _NC_CACHE:
        _NC_CACHE["nc"] = build_program()
    nc = _NC_CACHE["nc"]
    res = run_bass_kernel_spmd(nc, in_maps, core_ids=list(range(8)))
    out = np.empty((2, S, D), np.float32)
    for c in range(8):
        b, ch = c // 4, c % 4
        out[b, ch * 2048:(ch + 1) * 2048, :] = res.results[c]["yT"].T
    return out
```

```python
import numpy as np
from contextlib import ExitStack
import concourse.bass as bass
import concourse.mybir as mybir
from concourse.bass_utils import run_bass_kernel_spmd

F32 = mybir.dt.float32
BF16 = mybir.dt.bfloat16
AF = mybir.ActivationFunctionType
ALU = mybir.AluOpType

D = 2048
S = 8192
EXT = 2304
OWN = 2048
EPS = 1e-6
ENGS = ("pe", "act", "dve", "pool", "sp")
ENGATTR = {"pe": "tensor", "act": "scalar", "dve": "vector", "pool": "gpsimd", "sp": "sync"}


class Buf:
    __slots__ = ("name", "last_w", "readers", "ndma", "sem", "id")
    _n = 0

    def __init__(self, name="b"):
        self.name = name
        self.last_w = None
        self.readers = []
        self.ndma = 0
        self.sem = None
        Buf._n += 1
        self.id = Buf._n


class TB:
    __slots__ = ("t", "b")

    def __init__(self, t, name="b"):
        self.t = t
        self.b = Buf(name)


class Rot:
    def __init__(self, items):
        self.items = items
        self.i = 0

    def next(self):
        it = self.items[self.i % len(self.items)]
        self.i += 1
        return it


class Sems:
    def __init__(self, nc, es):
        self.nc = nc
        self.es = es
        self.eng = {e: [es.enter_context(nc.semaphore("prog_" + e)), 0] for e in ENGS}
        self.free = []
        self.n = 0

    def get_dma(self):
        if self.free:
            return self.free.pop()
        self.n += 1
        return [self.es.enter_context(self.nc.semaphore("dsem%d" % self.n)), 0]


class Op:
    __slots__ = ("eng", "fn", "waits", "signal", "dma_buf", "idx")

    def __init__(self, eng, fn):
        self.eng = eng
        self.fn = fn
        self.waits = []
        self.signal = False
        self.dma_buf = None
        self.idx = None


class Prog:
    def __init__(self, nc, sems):
        self.nc = nc
        self.sems = sems
        self.ops = {e: [] for e in ENGS}
        self.waited = {e: {} for e in ENGS}
        self.dma_bufs = []

    def _need(self, op, tok):
        if tok is None:
            return
        w = self.waited[op.eng]
        if tok[0] == "c":
            if tok[1] == op.eng and op.dma_buf is None:
                return
            key = tok[1]
            if w.get(key, -1) >= tok[2]:
                return
            w[key] = tok[2]
            op.waits.append(tok)
            self.ops[tok[1]][tok[2]].signal = True
        else:
            key = ("d", tok[1].id)
            if w.get(key, -1) >= tok[2]:
                return
            w[key] = tok[2]
            op.waits.append(tok)

    def op(self, eng, fn, reads=(), writes=(), dma=None):
        o = Op(eng, fn)
        o.idx = len(self.ops[eng])
        o.dma_buf = dma
        for b in reads:
            self._need(o, b.last_w)
        for b in writes:
            self._need(o, b.last_w)
            for t in b.readers:
                self._need(o, t)
        if dma is not None:
            if dma.ndma == 0 and dma.sem is None:
                self.dma_bufs.append(dma)
            dma.ndma += 1
            tok = ("d", dma, dma.ndma)
        else:
            tok = ("c", eng, o.idx)
        for b in reads:
            b.readers.append(tok)
        for b in writes:
            b.last_w = tok
            b.readers = []
        self.ops[eng].append(o)
        return tok

    def emit(self):
        nc = self.nc
        sems = self.sems
        for b in self.dma_bufs:
            b.sem = sems.get_dma()
        for e in ENGS:
            for o in reversed(self.ops[e]):
                if o.dma_buf is None:
                    o.signal = True
                    break
        sigval = {}
        final = {}
        for e in ENGS:
            c = sems.eng[e][1]
            for o in self.ops[e]:
                if o.signal and o.dma_buf is None:
                    c += 1
                    sigval[(e, o.idx)] = c
            final[e] = c

        def lower(tok):
            if tok[0] == "c":
                return sems.eng[tok[1]][0], sigval[(tok[1], tok[2])]
            b = tok[1]
            return b.sem[0], 16 * (b.sem[1] + tok[2])

        sp_final = final["sp"] + 1
        with nc.Block() as block:
            def make(e):
                def body(eng):
                    for o in self.ops[e]:
                        for t in o.waits:
                            s, v = lower(t)
                            eng.wait_ge(s, v)
                        ins = o.fn(eng)
                        if o.dma_buf is not None:
                            ins.then_inc(o.dma_buf.sem[0], 16)
                        elif o.signal:
                            ins.then_inc(sems.eng[e][0], 1)
                    if e == "sp":
                        for e2 in ENGS:
                            if e2 != "sp" and final[e2] > 0:
                                eng.wait_ge(sems.eng[e2][0], final[e2])
                        for b in self.dma_bufs:
                            eng.wait_ge(b.sem[0], 16 * (b.sem[1] + b.ndma))
                        eng.sem_inc(sems.eng["sp"][0], 1)
                    else:
                        eng.wait_ge(sems.eng["sp"][0], sp_final)
                return body
            for e in ENGS:
                getattr(block, ENGATTR[e])(make(e))
        for e in ENGS:
            sems.eng[e][1] = final[e]
        sems.eng["sp"][1] = sp_final
        for b in self.dma_bufs:
            b.sem[1] += b.ndma
            sems.free.append(b.sem)
            b.sem = None


class Ctx:
    def __init__(self, nc, sems, tag):
        self.nc = nc
        self.es = ExitStack()
        self.P = Prog(nc, sems)
        self.tag = tag
        self.k = 0

    def sb(self, shape, dt, name="t"):
        self.k += 1
        t = self.es.enter_context(self.nc.sbuf_tensor("%s_%s%d" % (self.tag, name, self.k), list(shape), dt))
        return TB(t, name)

    def ps(self, shape, name="p"):
        self.k += 1
        t = self.es.enter_context(self.nc.psum_tensor("%s_%s%d" % (self.tag, name, self.k), list(shape), F32))
        return TB(t, name)

    def close(self):
        self.P.emit()
        self.es.close()


def dma_in(P, q, dst, src_ap, dst_ap=None):
    d = dst.t[:] if dst_ap is None else dst_ap
    P.op(q, lambda e, d=d, s=src_ap: e.dma_start(out=d, in_=s), writes=[dst.b], dma=dst.b)


def dma_out(P, src, dst_ap, src_ap, dram_bufs=(), q="pool"):
    return P.op(q, lambda e, d=dst_ap, s=src_ap: e.dma_start(out=d, in_=s),
                reads=[src.b], writes=list(dram_bufs), dma=src.b)


def mm_group(P, out_ap, out_b, pairs, reads):
    n = len(pairs)

    def fn(e, pairs=pairs, out_ap=out_ap, n=n):
        ins = None
        for i, (l, r) in enumerate(pairs):
            ins = e.matmul(out_ap, lhsT=l, rhs=r, start=(i == 0), stop=(i == n - 1))
        return ins
    P.op("pe", fn, reads=reads, writes=[out_b])


class Common:
    def __init__(self, C, gcols_d, with_bd=False):
        P = C.P
        self.C = C
        self.ones = C.sb([128, 128], BF16, "ones")
        P.op("dve", lambda e: e.memset(self.ones.t[:], 1.0), writes=[self.ones.b])
        self.eps = C.sb([128, 1], F32, "eps")
        P.op("dve", lambda e: e.memset(self.eps.t[:], EPS), writes=[self.eps.b])
        self.g = C.sb([128, 84], F32, "g")
        dma_in(P, "sp", self.g, gcols_d)
        self.sq = Rot([C.sb([128, 512], BF16, "sq") for _ in range(3)])
        self.f32 = Rot([C.sb([128, 512], F32, "f") for _ in range(6)])
        self.rawb = Rot([C.sb([128, 512], BF16, "rawb") for _ in range(2)])
        if with_bd:
            self.bd = C.sb([128, 128], BF16, "bd")
            P.op("dve", lambda e: e.memset(self.bd.t[:], 0.0), writes=[self.bd.b])
            P.op("dve", lambda e: e.memset(self.bd.t[0:64, 0:64], 1.0), writes=[self.bd.b])
            P.op("dve", lambda e: e.memset(self.bd.t[64:128, 64:128], 1.0), writes=[self.bd.b])


def sq_accum(P, K, srcs, ssqs, n, mrows=128, lhs=None):
    ssq = ssqs.next()
    lhs = K.ones if lhs is None else lhs
    m = len(srcs)
    for i, (ap, b, rows) in enumerate(srcs):
        sq = K.sq.next()
        P.op("act", lambda e, o=sq.t[0:rows, 0:n], a=ap: e.activation(out=o, in_=a, func=AF.Square),
             reads=[b], writes=[sq.b])
        P.op("pe", lambda e, o=ssq.t[0:mrows, 0:n], l=lhs.t[0:rows, 0:mrows], r=sq.t[0:rows, 0:n], i=i:
             e.matmul(o, lhsT=l, rhs=r, start=(i == 0), stop=(i == m - 1)),
             reads=[sq.b, lhs.b], writes=[ssq.b])
    return ssq


def rstd_of(P, K, ssq, d, n, rows=128):
    r = K.f32.next()
    P.op("act", lambda e, o=r.t[0:rows, 0:n], a=ssq.t[0:rows, 0:n]:
         e.activation(out=o, in_=a, func=AF.Sqrt, bias=K.eps.t[0:rows, 0:1], scale=1.0 / d),
         reads=[ssq.b, K.eps.b], writes=[r.b])
    P.op("dve", lambda e, o=r.t[0:rows, 0:n]: e.reciprocal(out=o, in_=o), reads=[r.b], writes=[r.b])
    return r


def apply_plain(P, K, src_ap, src_b, gcol, r, out_ap, out_b, n, rows=128, eng="dve"):
    P.op(eng, lambda e: e.scalar_tensor_tensor(out=out_ap, in0=src_ap, scalar=K.g.t[0:rows, gcol:gcol + 1],
                                               in1=r.t[0:rows, 0:n], op0=ALU.mult, op1=ALU.mult),
         reads=[src_b, K.g.b, r.b], writes=[out_b])


def apply_rope(P, K, src, n, rows, perm, gcol, Ctab, Stab, r, out_ap, out_b, psw, eng2="pool"):
    raw = K.rawb.next()
    P.op("act", lambda e: e.activation(out=raw.t[0:rows, 0:n], in_=src.t[0:rows, 0:n], func=AF.Copy),
         reads=[src.b], writes=[raw.b])
    P.op("pe", lambda e: e.matmul(psw.t[0:rows, 0:n], lhsT=perm.t[0:rows, 0:rows], rhs=raw.t[0:rows, 0:n],
                                  start=True, stop=True), reads=[raw.b, perm.b], writes=[psw.b])
    t1 = K.f32.next()
    P.op("dve", lambda e: e.scalar_tensor_tensor(out=t1.t[0:rows, 0:n], in0=src.t[0:rows, 0:n],
                                                 scalar=K.g.t[0:rows, gcol:gcol + 1], in1=Ctab[0],
                                                 op0=ALU.mult, op1=ALU.mult),
         reads=[src.b, K.g.b, Ctab[1]], writes=[t1.b])
    t2 = K.f32.next()
    P.op("dve", lambda e: e.scalar_tensor_tensor(out=t2.t[0:rows, 0:n], in0=psw.t[0:rows, 0:n],
                                                 scalar=K.g.t[0:rows, gcol + 1:gcol + 2], in1=Stab[0],
                                                 op0=ALU.mult, op1=ALU.mult),
         reads=[psw.b, K.g.b, Stab[1]], writes=[t2.b])
    P.op(eng2, lambda e: e.tensor_tensor(out=t1.t[0:rows, 0:n], in0=t1.t[0:rows, 0:n], in1=t2.t[0:rows, 0:n],
                                         op=ALU.add), reads=[t1.b, t2.b], writes=[t1.b])
    P.op(eng2, lambda e: e.tensor_tensor(out=out_ap, in0=t1.t[0:rows, 0:n], in1=r.t[0:rows, 0:n],
                                         op=ALU.mult), reads=[t1.b, r.b], writes=[out_b])


def xnorm(P, K, xt, hT, gbase, n, ssq):
    ssq_ = sq_accum(P, K, [(xt.t[:, k, 0:n], xt.b, 128) for k in range(16)], ssq, n)
    r = rstd_of(P, K, ssq_, float(D), n)
    for k in range(16):
        apply_plain(P, K, xt.t[:, k, 0:n], xt.b, gbase + k, r, hT.t[:, k, 0:n], hT.b, n,
                    eng="dve")


def build_program():
    nc = bass.Bass("TRN2", target_bir_lowering=False)

    def din(name, shape):
        return nc.dram_tensor(name, list(shape), F32, kind="ExternalInput").ap()

    def dscr(name, shape, dt):
        return nc.dram_tensor(name, list(shape), dt, kind="Internal").ap()

    xT = din("xT", [D, S])
    gcols = din("gcols", [128, 84])
    w_kv = din("w_kv", [128, 16 * 832])
    w_q0 = din("w_q0", [128, 16 * 1536])
    w_uq = din("w_uq", [128, 4 * 1536])
    w_uk = din("w_uk", [128, 2 * 1024])
    w_uv = din("w_uv", [128, 2 * 1024])
    w_o0 = din("w_o0", [4, 128, 8192])
    w_up = din("w_up", [2, 16, 128, 8192])
    w_dn = din("w_dn", [2, 2, 8, 128, 8192])
    w_qkv1 = din("w_qkv1", [5, 128, 8192])
    w_o1 = din("w_o1", [4, 128, 8192])
    t_mla = din("t_mla", [2, 64, S])
    t_gqa = din("t_gqa", [2, 128, S])
    t_swa = din("t_swa", [2, 128, EXT])
    perms = din("perms", [3, 128, 128])
    masks = din("masks", [4, 128, 512])
    sinkrep = din("sinkrep", [64, 32])
    yT = nc.dram_tensor("yT", [D, OWN], F32, kind="ExternalOutput").ap()

    kn_d = dscr("kn_d", [8, 128, S], BF16)
    kr_d = dscr("kr_d", [64, S], BF16)
    kg_d = dscr("kg_d", [2, 128, S], BF16)
    va_d = dscr("va_d", [8, 128, 64, 128], BF16)
    vg_d = dscr("vg_d", [2, 128, 64, 128], BF16)
    qn_d = dscr("qn_d", [8, 128, EXT], BF16)
    qr_d = dscr("qr_d", [8, 64, EXT], BF16)
    qg_d = dscr("qg_d", [8, 128, EXT], BF16)
    mg_d = dscr("mg_d", [D, EXT], BF16)
    x1_d = dscr("x1_d", [D, EXT], F32)
    q1_d = dscr("q1_d", [4, 64, 18, 8, 128], BF16)
    k1_d = dscr("k1_d", [4, 64, EXT], BF16)
    v1_d = dscr("v1_d", [128, 18, 256], BF16)
    m1_d = dscr("m1_d", [D, OWN], BF16)
    h0_d = dscr("h0_d", [D, EXT], BF16)
    wb_o0 = dscr("wb_o0", [4, 128, 8192], BF16)
    wb_up = dscr("wb_up", [2, 16, 128, 8192], BF16)
    wb_dn = dscr("wb_dn", [2, 2, 8, 128, 8192], BF16)
    wb_qkv1 = dscr("wb_qkv1", [5, 128, 8192], BF16)
    wb_o1 = dscr("wb_o1", [4, 128, 8192], BF16)

    es = ExitStack()
    sems = Sems(nc, es)
    xT_v = xT.rearrange("(k p) t -> p k t", p=128)

    C = Ctx(nc, sems, "p0")
    P = C.P
    K = Common(C, gcols)
    wkv = C.sb([128, 16, 832], BF16, "wkv")
    dma_in(P, "pool", wkv, w_kv.rearrange("p (k c) -> p k c", k=16))
    wuk = C.sb([128, 2, 1024], BF16, "wuk")
    dma_in(P, "pool", wuk, w_uk.rearrange("p (k c) -> p k c", k=2))
    wuv = C.sb([128, 2, 1024], BF16, "wuv")
    dma_in(P, "pool", wuv, w_uv.rearrange("p (k c) -> p k c", k=2))
    pm = C.sb([128, 3, 128], BF16, "pm")
    dma_in(P, "pool", pm, perms.rearrange("a p m -> p a m"))
    pm_mla = TB(pm.t[:, 0, :]); pm_mla.b = pm.b
    pm_gqa = TB(pm.t[:, 1, :]); pm_gqa.b = pm.b
    xts = Rot([C.sb([128, 16, 512], F32, "xt") for _ in range(2)])
    hT = C.sb([128, 16, 512], BF16, "hT")
    tabs = Rot([(C.sb([64, 2, 512], F32, "tm"), C.sb([128, 2, 512], F32, "tg")) for _ in range(2)])
    ckvn = C.sb([128, 2, 512], BF16, "ckvn")
    kst = Rot([(C.sb([128, 8, 512], BF16, "knst"), C.sb([64, 512], BF16, "krst"),
                C.sb([128, 2, 512], BF16, "kgst"), C.sb([128, 4, 1024], BF16, "vast"),
                C.sb([128, 4, 256], BF16, "vgst")) for _ in range(2)])
    ssq = Rot([C.ps([128, 512], "ssq") for _ in range(2)])
    prot = Rot([C.ps([128, 512], "pr") for _ in range(6)])
    for t in range(16):
        c0 = t * 512
        xt = xts.next()
        dma_in(P, "sp", xt, xT_v[:, :, c0:c0 + 512])
        tm, tg = tabs.next()
        dma_in(P, "sp", tm, t_mla[:, :, c0:c0 + 512].rearrange("a p t -> p a t"))
        dma_in(P, "sp", tg, t_gqa[:, :, c0:c0 + 512].rearrange("a p t -> p a t"))
        xnorm(P, K, xt, hT, 0, 512, ssq)
        if c0 < EXT:
            n_ = min(512, EXT - c0)
            dma_out(P, hT, h0_d.rearrange("(k p) t -> p k t", p=128)[:, :, c0:c0 + n_], hT.t[:, :, 0:n_])
        knst, krst, kgst, vast, vgst = kst.next()
        pck = [prot.next(), prot.next()]
        for j in range(2):
            mm_group(P, pck[j].t[:, :], pck[j].b,
                     [(wkv.t[:, kc, j * 128:(j + 1) * 128], hT.t[:, kc, :]) for kc in range(16)], [wkv.b, hT.b])
        ssq_ = sq_accum(P, K, [(pck[j].t[:, :], pck[j].b, 128) for j in range(2)], ssq, 512)
        r = rstd_of(P, K, ssq_, 256.0, 512)
        for j in range(2):
            apply_plain(P, K, pck[j].t[:, :], pck[j].b, 68 + j, r, ckvn.t[:, j, :], ckvn.b, 512)
        for h in range(8):
            pk = prot.next()
            mm_group(P, pk.t[:, :], pk.b,
                     [(wuk.t[:, kc, h * 128:(h + 1) * 128], ckvn.t[:, kc, :]) for kc in range(2)], [wuk.b, ckvn.b])
            ssq_ = sq_accum(P, K, [(pk.t[:, :], pk.b, 128)], ssq, 512)
            r = rstd_of(P, K, ssq_, 128.0, 512)
            apply_plain(P, K, pk.t[:, :], pk.b, 73, r, knst.t[:, h, :], knst.b, 512)
        for s in range(4):
            for hf in range(2):
                pv = prot.next()
                mm_group(P, pv.t[:, :], pv.b,
                         [(ckvn.t[:, kc, s * 128:(s + 1) * 128], wuv.t[:, kc, hf * 512:(hf + 1) * 512]) for kc in range(2)],
                         [wuv.b, ckvn.b])
                eng = "act" if (s * 2 + hf) % 2 == 0 else "dve"
                if eng == "act":
                    P.op("act", lambda e, o=vast.t[:, s, hf * 512:(hf + 1) * 512], a=pv.t[:, :]:
                         e.activation(out=o, in_=a, func=AF.Copy), reads=[pv.b], writes=[vast.b])
                else:
                    P.op("dve", lambda e, o=vast.t[:, s, hf * 512:(hf + 1) * 512], a=pv.t[:, :]:
                         e.tensor_copy(out=o, in_=a), reads=[pv.b], writes=[vast.b])
        pk = prot.next()
        mm_group(P, pk.t[0:64, :], pk.b, [(wkv.t[:, kc, 256:320], hT.t[:, kc, :]) for kc in range(16)], [wkv.b, hT.b])
        ssq_ = sq_accum(P, K, [(pk.t[0:64, :], pk.b, 64)], ssq, 512, mrows=64)
        r = rstd_of(P, K, ssq_, 64.0, 512, rows=64)
        psw = prot.next()
        apply_rope(P, K, pk, 512, 64, pm_mla, 74, (tm.t[:, 0, :], tm.b), (tm.t[:, 1, :], tm.b), r,
                   krst.t[:, :], krst.b, psw)
        for j in range(2):
            pk = prot.next()
            mm_group(P, pk.t[:, :], pk.b,
                     [(wkv.t[:, kc, 320 + j * 128:320 + (j + 1) * 128], hT.t[:, kc, :]) for kc in range(16)],
                     [wkv.b, hT.b])
            ssq_ = sq_accum(P, K, [(pk.t[:, :], pk.b, 128)], ssq, 512)
            r = rstd_of(P, K, ssq_, 128.0, 512)
            psw = prot.next()
            apply_rope(P, K, pk, 512, 128, pm_gqa, 78, (tg.t[:, 0, :], tg.b), (tg.t[:, 1, :], tg.b), r,
                       kgst.t[:, j, :], kgst.b, psw)
        for s in range(4):
            pv = prot.next()
            mm_group(P, pv.t[:, 0:256], pv.b,
                     [(hT.t[:, kc, s * 128:(s + 1) * 128], wkv.t[:, kc, 576:832]) for kc in range(16)], [wkv.b, hT.b])
            P.op("act", lambda e, o=vgst.t[:, s, :], a=pv.t[:, 0:256]: e.activation(out=o, in_=a, func=AF.Copy),
                 reads=[pv.b], writes=[vgst.b])
        dma_out(P, knst, kn_d[:, :, c0:c0 + 512].rearrange("h p t -> p h t"), knst.t[:])
        dma_out(P, krst, kr_d[:, c0:c0 + 512], krst.t[:])
        dma_out(P, kgst, kg_d[:, :, c0:c0 + 512].rearrange("h p t -> p h t"), kgst.t[:])
        for s in range(4):
            dma_out(P, vast, va_d[:, :, 4 * t + s, :].rearrange("h p d -> p h d"),
                    vast.t[:, s, :].rearrange("p (h d) -> p h d", h=8))
        for s in range(4):
            dma_out(P, vgst, vg_d[:, :, 4 * t + s, :].rearrange("h p d -> p h d"),
                    vgst.t[:, s, :].rearrange("p (h d) -> p h d", h=2))
    C.close()

    C = Ctx(nc, sems, "p1")
    P = C.P
    K = Common(C, gcols)
    wq = C.sb([128, 16, 1536], BF16, "wq")
    dma_in(P, "pool", wq, w_q0.rearrange("p (k c) -> p k c", k=16))
    wuq = C.sb([128, 4, 1536], BF16, "wuq")
    dma_in(P, "pool", wuq, w_uq.rearrange("p (k c) -> p k c", k=4))
    pm = C.sb([128, 3, 128], BF16, "pm")
    dma_in(P, "pool", pm, perms.rearrange("a p m -> p a m"))
    pm_mla = TB(pm.t[:, 0, :]); pm_mla.b = pm.b
    pm_gqa = TB(pm.t[:, 1, :]); pm_gqa.b = pm.b
    hTs = Rot([C.sb([128, 16, 512], BF16, "hT") for _ in range(2)])
    tabs = Rot([(C.sb([64, 2, 512], F32, "tm"), C.sb([128, 2, 512], F32, "tg")) for _ in range(2)])
    cqn = C.sb([128, 4, 512], BF16, "cqn")
    qst = Rot([(C.sb([128, 8, 512], BF16, "qnst"), C.sb([64, 8, 512], BF16, "qrst"),
                C.sb([128, 8, 512], BF16, "qgst")) for _ in range(1)])
    ssq = Rot([C.ps([128, 512], "ssq") for _ in range(2)])
    prot = Rot([C.ps([128, 512], "pr") for _ in range(6)])
    for t in range(5):
        c0 = t * 512
        n = min(512, EXT - c0)
        hT = hTs.next()
        dma_in(P, "sp", hT, h0_d.rearrange("(k p) t -> p k t", p=128)[:, :, c0:c0 + n], hT.t[:, :, 0:n])
        tm, tg = tabs.next()
        dma_in(P, "sp", tm, t_mla[:, :, c0:c0 + n].rearrange("a p t -> p a t"), tm.t[:, :, 0:n])
        dma_in(P, "sp", tg, t_gqa[:, :, c0:c0 + n].rearrange("a p t -> p a t"), tg.t[:, :, 0:n])
        qnst, qrst, qgst = qst.next()
        pcq = [prot.next() for _ in range(4)]
        for j in range(4):
            mm_group(P, pcq[j].t[:, 0:n], pcq[j].b,
                     [(wq.t[:, kc, j * 128:(j + 1) * 128], hT.t[:, kc, 0:n]) for kc in range(16)], [wq.b, hT.b])
        ssq_ = sq_accum(P, K, [(pcq[j].t[:, 0:n], pcq[j].b, 128) for j in range(4)], ssq, n)
        r = rstd_of(P, K, ssq_, 512.0, n)
        for j in range(4):
            apply_plain(P, K, pcq[j].t[:, 0:n], pcq[j].b, 64 + j, r, cqn.t[:, j, 0:n], cqn.b, n)
        for h in range(8):
            pn = prot.next()
            mm_group(P, pn.t[:, 0:n], pn.b,
                     [(wuq.t[:, kc, h * 192:h * 192 + 128], cqn.t[:, kc, 0:n]) for kc in range(4)], [wuq.b, cqn.b])
            pr = prot.next()
            mm_group(P, pr.t[0:64, 0:n], pr.b,
                     [(wuq.t[:, kc, h * 192 + 128:h * 192 + 192], cqn.t[:, kc, 0:n]) for kc in range(4)], [wuq.b, cqn.b])
            ssq_ = sq_accum(P, K, [(pn.t[:, 0:n], pn.b, 128), (pr.t[0:64, 0:n], pr.b, 64)], ssq, n)
            r = rstd_of(P, K, ssq_, 192.0, n)
            apply_plain(P, K, pn.t[:, 0:n], pn.b, 70, r, qnst.t[:, h, 0:n], qnst.b, n)
            psw = prot.next()
            apply_rope(P, K, pr, n, 64, pm_mla, 71, (tm.t[:, 0, 0:n], tm.b), (tm.t[:, 1, 0:n], tm.b), r,
                       qrst.t[:, h, 0:n], qrst.b, psw)
        for h in range(8):
            pg = prot.next()
            mm_group(P, pg.t[:, 0:n], pg.b,
                     [(wq.t[:, kc, 512 + h * 128:512 + (h + 1) * 128], hT.t[:, kc, 0:n]) for kc in range(16)],
                     [wq.b, hT.b])
            ssq_ = sq_accum(P, K, [(pg.t[:, 0:n], pg.b, 128)], ssq, n)
            r = rstd_of(P, K, ssq_, 128.0, n)
            psw = prot.next()
            apply_rope(P, K, pg, n, 128, pm_gqa, 76, (tg.t[:, 0, 0:n], tg.b), (tg.t[:, 1, 0:n], tg.b), r,
                       qgst.t[:, h, 0:n], qgst.b, psw)
        dma_out(P, qnst, qn_d[:, :, c0:c0 + n].rearrange("h p t -> p h t"), qnst.t[:, :, 0:n])
        dma_out(P, qrst, qr_d[:, :, c0:c0 + n].rearrange("h p t -> p h t"), qrst.t[:, :, 0:n])
        dma_out(P, qgst, qg_d[:, :, c0:c0 + n].rearrange("h p t -> p h t"), qgst.t[:, :, 0:n])
    C.close()

    C = Ctx(nc, sems, "p2")
    P = C.P
    ones = C.sb([128, 128], BF16, "ones")
    P.op("dve", lambda e: e.memset(ones.t[:], 1.0), writes=[ones.b])
    krs = C.sb([128, S], BF16, "krs")
    P.op("dve", lambda e: e.memset(krs.t[64:128, :], 0.0), writes=[krs.b])
    dma_in(P, "sp", krs, kr_d, krs.t[0:64, :])
    cvb = Buf("cv")
    cvl = [(wb_o0[i], w_o0[i]) for i in range(4)]
    cvl += [(wb_up[0, i], w_up[0, i]) for i in range(16)]
    cvl += [(wb_dn[0, hf, i], w_dn[0, hf, i]) for hf in range(2) for i in range(8)]
    cvl += [(wb_qkv1[i], w_qkv1[i]) for i in range(5)]
    cvl += [(wb_o1[i], w_o1[i]) for i in range(4)]
    cvl += [(wb_up[1, i], w_up[1, i]) for i in range(16)]
    cvl += [(wb_dn[1, hf, i], w_dn[1, hf, i]) for hf in range(2) for i in range(8)]
    for (dst_, src_) in cvl:
        P.op("pool", lambda e, d=dst_, s_=src_: e.dma_start(out=d, in_=s_), writes=[cvb], dma=cvb)
    kts = Rot([C.sb([128, S], BF16, "kt") for _ in range(2)])
    vts = Rot([C.sb([128, 64, 128], BF16, "vt") for _ in range(2)])
    qns = Rot([C.sb([128, EXT], BF16, "qn") for _ in range(2)])
    qrs = Rot([C.sb([128, EXT], BF16, "qr") for _ in range(2)])
    for _q in qrs.items:
        P.op("dve", lambda e, _q=_q: e.memset(_q.t[64:128, :], 0.0), writes=[_q.b])
    pts = Rot([C.sb([128, 1024], BF16, "pt") for _ in range(4)])
    ones32 = C.sb([128, 128], F32, "ones32")
    P.op("dve", lambda e: e.memset(ones32.t[:], 1.0), writes=[ones32.b])
    accAs = Rot([C.sb([128, 2, 512], F32, "accA") for _ in range(2)])
    accBs = Rot([C.sb([128, 2, 512], F32, "accB") for _ in range(1)])
    hls = Rot([C.sb([128, 2, 512], BF16, "hl") for _ in range(2)])
    rds = Rot([C.sb([128, 512], F32, "rd") for _ in range(2)])
    ost = Rot([C.sb([128, 512], BF16, "ost") for _ in range(2)])
    sps = Rot([C.ps([128, 1024], "S") for _ in range(2)])
    ops_ = Rot([C.ps([128, 512], "O") for _ in range(2)])
    dps = Rot([C.ps([128, 512], "Dn") for _ in range(2)])
    qtiles = [(0, 512), (512, 512), (1024, 512), (1536, 512), (2048, 256)]
    kt = vt = None
    pend = []

    def flush_pend():
        while pend:
            o_, dst_, n_ = pend.pop(0)
            dma_out(P, o_, dst_, o_.t[:, 0:n_], q="act")
    for job in range(16):
        mla = job < 8
        if mla:
            kt = kts.next()
            dma_in(P, "sp", kt, kn_d[job])
            vt = vts.next()
            dma_in(P, "sp", vt, va_d[job])
            qn = qns.next()
            dma_in(P, "sp", qn, qn_d[job])
            qr = qrs.next()
            dma_in(P, "sp", qr, qr_d[job], qr.t[0:64, :])
            scale = 192.0 ** -0.5
        else:
            j = job - 8
            if j % 4 == 0:
                kt = kts.next()
                dma_in(P, "sp", kt, kg_d[j // 4])
                vt = vts.next()
                dma_in(P, "sp", vt, vg_d[j // 4])
            qn = qns.next()
            dma_in(P, "sp", qn, qg_d[j])
            scale = 128.0 ** -0.5
        for (q0, n) in qtiles:
            Ops = ops_.next()
            Dps = dps.next()

            def s_pair(kp, mla=mla, kt=kt, qn=qn, qr=(qr if mla else None), q0=q0, n=n):
                Sp = sps.next()

                def fn(e, Sp=Sp):
                    ins = None
                    for i in range(2):
                        kb = 2 * kp + i
                        o = Sp.t[:, i * n:(i + 1) * n] if n == 256 else Sp.t[:, i * 512:(i + 1) * 512]
                        ins = e.matmul(o, lhsT=kt.t[:, kb * 128:(kb + 1) * 128], rhs=qn.t[:, q0:q0 + n],
                                       start=True, stop=(not mla))
                        if mla:
                            ins = e.matmul(o, lhsT=krs.t[:, kb * 128:(kb + 1) * 128], rhs=qr.t[:, q0:q0 + n],
                                           start=False, stop=True)
                    return ins
                rd = [kt.b, qn.b] + ([krs.b, qr.b] if mla else [])
                P.op("pe", fn, reads=rd, writes=[Sp.b])
                return Sp

            def pv_pair(kp, Sp, vt=vt, n=n, Ops=Ops, Dps=Dps, scale=scale):
                pt = pts.next()
                w = 2 * n
                P.op("act", lambda e: e.activation(out=pt.t[:, 0:w], in_=Sp.t[:, 0:w], func=AF.Exp, scale=scale),
                     reads=[Sp.b], writes=[pt.b])

                def fn(e):
                    ins = None
                    for i in range(2):
                        kb = 2 * kp + i
                        first = (kb == 0)
                        last = (kb == 63)
                        e.matmul(Ops.t[:, 0:n], lhsT=vt.t[:, kb, :], rhs=pt.t[:, i * n:(i + 1) * n],
                                 start=first, stop=last)
                        ins = e.matmul(Dps.t[:, 0:n], lhsT=ones.t[:, :], rhs=pt.t[:, i * n:(i + 1) * n],
                                       start=first, stop=last)
                    return ins
                P.op("pe", fn, reads=[vt.b, pt.b, ones.b], writes=[Ops.b, Dps.b])

            prev = s_pair(0)
            for kp in range(32):
                nxt = s_pair(kp + 1) if kp + 1 < 32 else None
                pv_pair(kp, prev)
                prev = nxt
                if kp == 4:
                    flush_pend()
            rd = rds.next()
            P.op("dve", lambda e, rd=rd, Dps=Dps, n=n: e.reciprocal(out=rd.t[:, 0:n], in_=Dps.t[:, 0:n]),
                 reads=[Dps.b], writes=[rd.b])
            o = ost.next()
            P.op("dve", lambda e, o=o, rd=rd, Ops=Ops, n=n: e.tensor_tensor(out=o.t[:, 0:n], in0=Ops.t[:, 0:n],
                                                                           in1=rd.t[:, 0:n], op=ALU.mult),
                 reads=[Ops.b, rd.b], writes=[o.b])
            pend.append((o, mg_d[job * 128:(job + 1) * 128, q0:q0 + n], n))
    flush_pend()
    C.close()

    def tail_phase(tag, layer):
        C = Ctx(nc, sems, tag)
        P = C.P
        K = Common(C, gcols, with_bd=(layer == 0))
        slabs = Rot([C.sb([128, 8192], BF16, "slab") for _ in range(4)])
        xt = C.sb([128, 16, 512], F32, "xt")
        hT = C.sb([128, 16, 512], BF16, "hT")
        hid = C.sb([128, 32, 512], BF16, "hid")
        rls = Rot([C.sb([128, 512], F32, "rl") for _ in range(3)])
        ssq = Rot([C.ps([128, 512], "ssq") for _ in range(2)])
        prot = Rot([C.ps([128, 512], "pr") for _ in range(6)])
        if layer == 0:
            pm = C.sb([128, 128], BF16, "pm")
            dma_in(P, "pool", pm, perms[2])
            tsw = Rot([C.sb([128, 2, 512], F32, "tsw") for _ in range(2)])
            qst = Rot([C.sb([128, 512], BF16, "qst") for _ in range(3)])
            vst = Rot([C.sb([128, 4, 256], BF16, "vst") for _ in range(1)])
            ntiles = 5
            wo, mgsrc, xsrc = wb_o0, mg_d, None
        else:
            ntiles = 4
            wo, mgsrc = wb_o1, m1_d
        outtoks = []

        def load_slab(src_ap):
            sl = slabs.next()
            dma_in(P, "sp", sl, src_ap)
            return sl

        mg_v = mgsrc.rearrange("(k p) t -> p k t", p=128)
        for t in range(ntiles):
            if layer == 0:
                c0 = t * 512
                n = min(512, EXT - c0)
                dma_in(P, "pool", hT, mg_v[:, :, c0:c0 + n], hT.t[:, :, 0:n])
                if t == 0:
                    dma_in(P, "pool", xt, xT_v[:, :, c0:c0 + n], xt.t[:, :, 0:n])
            else:
                c0 = t * 512
                n = 512
                if t == 0:
                    dma_in(P, "pool", hT, mg_v[:, :, c0:c0 + n])
                dma_in(P, "pool", xt, x1_d.rearrange("(k p) t -> p k t", p=128)[:, :, 128 + c0:128 + c0 + n])
            for s in range(4):
                sl = load_slab(wo[s])
                slv = sl.t[:].rearrange("p (k c) -> p k c", k=16)
                for j in range(4):
                    c = 4 * s + j
                    pp = prot.next()
                    mm_group(P, pp.t[:, 0:n], pp.b,
                             [(slv[:, kc, j * 128:(j + 1) * 128], hT.t[:, kc, 0:n]) for kc in range(16)], [sl.b, hT.b])
                    P.op("dve", lambda e, c=c, pp=pp, n=n: e.tensor_tensor(out=xt.t[:, c, 0:n], in0=pp.t[:, 0:n],
                                                                           in1=xt.t[:, c, 0:n], op=ALU.add),
                         reads=[pp.b, xt.b], writes=[xt.b])
            xnorm(P, K, xt, hT, 16 + 32 * layer, n, ssq)
            for hf in range(2):
                for s in range(8):
                    sl = load_slab(wb_up[layer, hf * 8 + s])
                    slv = sl.t[:].rearrange("p (k c) -> p k c", k=16)
                    for j in range(4):
                        hc = s * 4 + j
                        pp = prot.next()
                        mm_group(P, pp.t[:, 0:n], pp.b,
                                 [(slv[:, kc, j * 128:(j + 1) * 128], hT.t[:, kc, 0:n]) for kc in range(16)],
                                 [sl.b, hT.b])
                        rl = rls.next()
                        P.op("act", lambda e, rl=rl, pp=pp, n=n: e.activation(out=rl.t[:, 0:n], in_=pp.t[:, 0:n],
                                                                              func=AF.Relu),
                             reads=[pp.b], writes=[rl.b])
                        P.op("dve", lambda e, rl=rl, hc=hc, n=n: e.tensor_tensor(out=hid.t[:, hc, 0:n], in0=rl.t[:, 0:n],
                                                                              in1=rl.t[:, 0:n], op=ALU.mult),
                             reads=[rl.b], writes=[hid.b])
                if layer == 1 and hf == 1 and t + 1 < ntiles:
                    dma_in(P, "pool", hT, mg_v[:, :, c0 + 512:c0 + 1024])
                for s in range(8):
                    sl = load_slab(wb_dn[layer, hf, s])
                    slv = sl.t[:].rearrange("p (k c) -> p k c", k=32)
                    for j in range(2):
                        c = 2 * s + j
                        pp = prot.next()
                        mm_group(P, pp.t[:, 0:n], pp.b,
                                 [(slv[:, kc, j * 128:(j + 1) * 128], hid.t[:, kc, 0:n]) for kc in range(32)],
                                 [sl.b, hid.b])
                        P.op("dve", lambda e, c=c, pp=pp, n=n: e.tensor_tensor(out=xt.t[:, c, 0:n], in0=pp.t[:, 0:n],
                                                                               in1=xt.t[:, c, 0:n], op=ALU.add),
                             reads=[pp.b, xt.b], writes=[xt.b])
            if layer == 1:
                outtoks.append(dma_out(P, xt, yT.rearrange("(k p) t -> p k t", p=128)[:, :, c0:c0 + n], xt.t[:, :, 0:n]))
                continue
            dma_out(P, xt, x1_d.rearrange("(k p) t -> p k t", p=128)[:, :, c0:c0 + n], xt.t[:, :, 0:n])
            ts_ = tsw.next()
            dma_in(P, "pool", ts_, t_swa[:, :, c0:c0 + n].rearrange("a p t -> p a t"), ts_.t[:, :, 0:n])
            xnorm(P, K, xt, hT, 32, n, ssq)
            if t + 1 < ntiles:
                c1 = (t + 1) * 512
                n1 = min(512, EXT - c1)
                dma_in(P, "pool", xt, xT_v[:, :, c1:c1 + n1], xt.t[:, :, 0:n1])
            nb = n // 128
            b0 = c0 // 128
            for s in range(5):
                sl = load_slab(wb_qkv1[s])
                slv = sl.t[:].rearrange("p (k c) -> p k c", k=16)
                nch = 4 if s < 4 else 2
                for j in range(nch):
                    pq = prot.next()
                    mm_group(P, pq.t[:, 0:n], pq.b,
                             [(slv[:, kc, j * 128:(j + 1) * 128], hT.t[:, kc, 0:n]) for kc in range(16)], [sl.b, hT.b])
                    ssq_ = sq_accum(P, K, [(pq.t[:, 0:n], pq.b, 128)], ssq, n, lhs=K.bd)
                    r = rstd_of(P, K, ssq_, 64.0, n)
                    psw = prot.next()
                    qs = qst.next()
                    pmtb = pm
                    apply_rope(P, K, pq, n, 128, pmtb, 80 if s < 4 else 82,
                               (ts_.t[:, 0, 0:n], ts_.b), (ts_.t[:, 1, 0:n], ts_.b), r, qs.t[:, 0:n], qs.b, psw, eng2="dve")
                    if s < 4:
                        c = 4 * s + j
                        for u in range(2):
                            h = 2 * c + u
                            kvh, hh = h // 8, h % 8
                            dma_out(P, qs, q1_d[kvh, :, b0:b0 + nb, hh, :],
                                    qs.t[u * 64:(u + 1) * 64, 0:n].rearrange("p (b q) -> p b q", q=128))
                    else:
                        for u in range(2):
                            dma_out(P, qs, k1_d[2 * j + u, :, c0:c0 + n], qs.t[u * 64:(u + 1) * 64, 0:n])
                if s == 4:
                    vs = vst.next()
                    for sb_ in range(nb):
                        pv = prot.next()
                        mm_group(P, pv.t[:, 0:256], pv.b,
                                 [(hT.t[:, kc, sb_ * 128:(sb_ + 1) * 128], slv[:, kc, 256:512]) for kc in range(16)],
                                 [sl.b, hT.b])
                        P.op("act", lambda e, vs=vs, sb_=sb_, pv=pv: e.activation(out=vs.t[:, sb_, :], in_=pv.t[:, 0:256],
                                                                                  func=AF.Copy),
                             reads=[pv.b], writes=[vs.b])
                    dma_out(P, vs, v1_d[:, b0:b0 + nb, :], vs.t[:, 0:nb, :])
        C.close()
        return outtoks

    tail_phase("p3", 0)

    C = Ctx(nc, sems, "p4")
    P = C.P
    ones = C.sb([128, 128], BF16, "ones")
    P.op("dve", lambda e: e.memset(ones.t[:], 1.0), writes=[ones.b])
    k1s = C.sb([64, 4, EXT], BF16, "k1s")
    dma_in(P, "sp", k1s, k1_d.rearrange("k p t -> p k t"))
    v1s = C.sb([128, 18, 256], BF16, "v1s")
    dma_in(P, "sp", v1s, v1_d)
    mks = C.sb([128, 4, 512], BF16, "mks")
    dma_in(P, "pool", mks, masks.rearrange("a p t -> p a t"))
    esk = C.sb([64, 32], F32, "esk")
    dma_in(P, "sp", esk, sinkrep)
    P.op("act", lambda e: e.activation(out=esk.t[:], in_=esk.t[:], func=AF.Exp), reads=[esk.b], writes=[esk.b])
    eskx = C.sb([64, 32, 128], F32, "eskx")
    P.op("dve", lambda e: e.memset(eskx.t[:], 0.0), writes=[eskx.b])
    for h in range(32):
        P.op("dve", lambda e, h=h: e.tensor_scalar(out=eskx.t[:, h, :], in0=eskx.t[:, h, :], scalar1=esk.t[:, h:h + 1],
                                                   scalar2=None, op0=ALU.add), reads=[esk.b, eskx.b], writes=[eskx.b])
    q1s = Rot([C.sb([64, 18, 8, 128], BF16, "q1s") for _ in range(2)])
    pts = Rot([C.sb([128, 1536], BF16, "pt") for _ in range(3)])
    dns = Rot([C.sb([64, 512], F32, "dn") for _ in range(2)])
    dls = Rot([C.sb([64, 512], F32, "dl") for _ in range(2)])
    ost = Rot([C.sb([64, 512], BF16, "ost") for _ in range(2)])
    Sp = C.ps([128, 1536], "S")
    opsr = Rot([C.ps([128, 512], "O") for _ in range(2)])
    dpsr = Rot([C.ps([128, 512], "Dn") for _ in range(2)])
    sc1 = 64.0 ** -0.5
    q1_of = {}
    its = []
    for kv in range(4):
        for jq in range(16):
            for hg in range(2):
                its.append((kv, jq, hg))

    def s_mm(it):
        kv, jq, hg = it
        if kv not in q1_of:
            q1 = q1s.next()
            dma_in(P, "sp", q1, q1_d[kv])
            q1_of[kv] = q1
        q1 = q1_of[kv]

        def fnS(e):
            ins = None
            for kb in range(3):
                ins = e.matmul(Sp.t[:, kb * 512:(kb + 1) * 512],
                               lhsT=k1s.t[:, kv, (jq + kb) * 128:(jq + kb + 1) * 128],
                               rhs=q1.t[:, jq + 1, hg * 4:(hg + 1) * 4, :].rearrange("p h q -> p (h q)"),
                               start=True, stop=True)
            return ins
        P.op("pe", fnS, reads=[k1s.b, q1.b], writes=[Sp.b])

    def exp_mask(it):
        kv, jq, hg = it
        pt = pts.next()
        P.op("act", lambda e: e.activation(out=pt.t[:, :], in_=Sp.t[:, :], func=AF.Exp, scale=sc1),
             reads=[Sp.b], writes=[pt.b])
        mP = 2 if jq == 0 else 0
        mN = 3 if jq == 15 else 1
        P.op("dve", lambda e: e.tensor_tensor(out=pt.t[:, 0:512], in0=pt.t[:, 0:512], in1=mks.t[:, mP, :],
                                              op=ALU.mult), reads=[pt.b, mks.b], writes=[pt.b])
        P.op("pool", lambda e: e.tensor_tensor(out=pt.t[:, 1024:1536], in0=pt.t[:, 1024:1536], in1=mks.t[:, mN, :],
                                               op=ALU.mult), reads=[pt.b, mks.b], writes=[pt.b])
        return pt

    def pv_norm(it, pt):
        kv, jq, hg = it
        Ops = opsr.next()
        Dps = dpsr.next()

        def fnP(e):
            ins = None
            for kb in range(3):
                e.matmul(Ops.t[0:64, :], lhsT=v1s.t[:, jq + kb, kv * 64:(kv + 1) * 64],
                         rhs=pt.t[:, kb * 512:(kb + 1) * 512], start=(kb == 0), stop=(kb == 2))
            for kb in range(3):
                ins = e.matmul(Dps.t[0:64, :], lhsT=ones.t[:, 0:64], rhs=pt.t[:, kb * 512:(kb + 1) * 512],
                               start=(kb == 0), stop=(kb == 2))
            return ins
        P.op("pe", fnP, reads=[v1s.b, pt.b, ones.b], writes=[Ops.b, Dps.b])
        dn = dns.next()
        h0 = kv * 8 + hg * 4
        P.op("dve", lambda e: e.tensor_tensor(
            out=dn.t[:, :], in0=Dps.t[0:64, :], in1=eskx.t[:, h0:h0 + 4, :].rearrange("p h q -> p (h q)"),
            op=ALU.add), reads=[Dps.b, eskx.b], writes=[dn.b])
        dl = dls.next()
        P.op("act", lambda e: e.activation(out=dl.t[:, :], in_=dn.t[:, :], func=AF.Ln),
             reads=[dn.b], writes=[dl.b])
        P.op("act", lambda e: e.activation(out=dn.t[:, :], in_=dl.t[:, :], func=AF.Exp, scale=-1.0),
             reads=[dl.b], writes=[dn.b])
        o = ost.next()
        P.op("dve", lambda e: e.tensor_tensor(out=o.t[:, :], in0=Ops.t[0:64, :], in1=dn.t[:, :], op=ALU.mult),
             reads=[Ops.b, dn.b], writes=[o.b])
        r0 = (kv * 8 + hg * 4) * 64
        dma_out(P, o, m1_d[r0:r0 + 256, jq * 128:(jq + 1) * 128].rearrange("(h d) q -> d h q", d=64),
                o.t[:, :].rearrange("p (h q) -> p h q", q=128), q="sp")

    s_mm(its[0])
    pt_cur = exp_mask(its[0])
    for i, it in enumerate(its):
        if i + 1 < len(its):
            s_mm(its[i + 1])
            pt_nxt = exp_mask(its[i + 1])
        else:
            pt_nxt = None
        pv_norm(it, pt_cur)
        pt_cur = pt_nxt
    C.close()

    tail_phase("p5", 1)
    es.close()
    return nc


def _slab(W, cols=None):
    W = np.asarray(W, np.float32)
    if cols is not None:
        W = W[:, cols]
    Kd, n = W.shape
    return np.ascontiguousarray(W.reshape(Kd // 128, 128, n).transpose(1, 0, 2).reshape(128, (Kd // 128) * n))


def _rope_tab(pos, dim, theta):
    inv = np.float32(theta) ** (-np.arange(0, dim, 2, dtype=np.float32) / np.float32(dim))
    ang = pos.astype(np.float32)[:, None] * inv[None, :].astype(np.float32)
    ang = ang.astype(np.float32).astype(np.float64)
    return np.cos(ang).astype(np.float32).T, np.sin(ang).astype(np.float32).T


def _perm_lhsT(perm):
    n = len(perm)
    m = np.zeros((128, 128), np.float32)
    for o in range(n):
        m[perm[o], o] = 1.0
    return m


_NC_CACHE = {}


def kernel(x, even_norm, even_w_in, mla_q_lat_norm, mla_kv_lat_norm, mla_w_uq, mla_w_ukv,
           mla_q_norm, mla_k_nope_norm, mla_k_rope_norm, gqa_q_norm, gqa_k_norm, even_w_out,
           odd_norm, odd_w_qkv, swa_q_norm, swa_k_norm, swa_sink, odd_w_out,
           mlp_norm, mlp_w_up, mlp_w_down):
    f = lambda a: np.asarray(a, np.float32)
    x = f(x)
    p_mla = np.concatenate([np.arange(32, 64), np.arange(0, 32)])
    p_gqa = np.concatenate([p_mla, 64 + p_mla])
    p_sw64 = np.concatenate([np.arange(8, 16), np.arange(0, 8), np.arange(16, 64)])
    p_swa = np.concatenate([p_sw64, 64 + p_sw64])
    perms = np.stack([_perm_lhsT(p_mla), _perm_lhsT(p_gqa), _perm_lhsT(p_swa)])
    g = np.ones((128, 84), np.float32)
    g[:, 0:16] = f(even_norm)[0].reshape(16, 128).T
    g[:, 16:32] = f(mlp_norm)[0].reshape(16, 128).T
    g[:, 32:48] = f(odd_norm)[0].reshape(16, 128).T
    g[:, 48:64] = f(mlp_norm)[1].reshape(16, 128).T
    g[:, 64:68] = f(mla_q_lat_norm)[0].reshape(4, 128).T
    g[:, 68:70] = f(mla_kv_lat_norm)[0].reshape(2, 128).T
    qn_ = f(mla_q_norm)[0]
    g[:, 70] = qn_[0:128]
    g[0:64, 71] = qn_[128:192]
    g[0:64, 72] = qn_[128:192][p_mla]
    g[:, 73] = f(mla_k_nope_norm)[0]
    kr_ = f(mla_k_rope_norm)[0]
    g[0:64, 74] = kr_
    g[0:64, 75] = kr_[p_mla]
    gq_ = f(gqa_q_norm)[0]
    g[:, 76] = gq_
    g[:, 77] = gq_[p_gqa]
    gk_ = f(gqa_k_norm)[0]
    g[:, 78] = gk_
    g[:, 79] = gk_[p_gqa]
    sq_ = np.tile(f(swa_q_norm)[0], 2)
    g[:, 80] = sq_
    g[:, 81] = sq_[p_swa]
    sk_ = np.tile(f(swa_k_norm)[0], 2)
    g[:, 82] = sk_
    g[:, 83] = sk_[p_swa]
    w_in = f(even_w_in)[0]
    cols_kv = np.concatenate([np.arange(512, 768), np.arange(768, 832), np.arange(1856, 2112), np.arange(2112, 2368)])
    cols_q = np.concatenate([np.arange(0, 512), np.arange(832, 1856)])
    wkv = _slab(w_in, cols_kv)
    wq0 = _slab(w_in, cols_q)
    wuq = _slab(f(mla_w_uq)[0])
    ukv = f(mla_w_ukv)[0]
    ck = np.concatenate([np.arange(h * 256, h * 256 + 128) for h in range(8)])
    cv = np.concatenate([np.arange(h * 256 + 128, h * 256 + 256) for h in range(8)])
    wuk = _slab(ukv, ck)
    wuv = _slab(ukv, cv)
    wo0 = np.stack([_slab(f(even_w_out)[0][:, s * 512:(s + 1) * 512]) for s in range(4)])
    wo1 = np.stack([_slab(f(odd_w_out)[0][:, s * 512:(s + 1) * 512]) for s in range(4)])
    wqkv1 = np.stack([_slab(f(odd_w_qkv)[0][:, s * 512:(s + 1) * 512]) for s in range(5)])
    up = f(mlp_w_up)
    wup = np.stack([np.stack([_slab(up[l][:, s * 512:(s + 1) * 512]) for s in range(16)]) for l in range(2)])
    dn = f(mlp_w_down)
    wdn = np.stack([np.stack([np.stack([_slab(dn[l][hf * 4096:(hf + 1) * 4096, s * 256:(s + 1) * 256])
                                        for s in range(8)]) for hf in range(2)]) for l in range(2)])
    sinkrep = np.ascontiguousarray(np.broadcast_to(f(swa_sink)[0][None, :], (64, 32)))
    kk = np.arange(128)[:, None]
    qq = np.arange(128)[None, :]
    mP = np.tile((kk >= qq).astype(np.float32), (1, 4))
    mN = np.tile((kk <= qq).astype(np.float32), (1, 4))

    in_maps = []
    for c in range(8):
        b, ch = c // 4, c % 4
        ext = list(range(16 * ch - 1, 16 * ch + 17))
        if ch == 0:
            ext[0] = 17
        if ch == 3:
            ext[17] = 46
        rest = [i for i in range(64) if i not in ext]
        order = np.array(ext + rest)
        tok = (order[:, None] * 128 + np.arange(128)[None, :]).reshape(-1)
        xTc = np.ascontiguousarray(x[b][tok].T)
        cm, sm = _rope_tab(tok, 64, 500000.0)
        t_mla = np.stack([np.concatenate([cm, cm]), np.concatenate([-sm, sm])])
        cr, sr = _rope_tab(tok // 64, 64, 10000.0)
        cc, sc = _rope_tab(tok % 64, 64, 10000.0)
        t_gqa = np.stack([np.concatenate([cr, cr, cc, cc]), np.concatenate([-sr, sr, -sc, sc])])
        te = tok[:EXT]
        cs, ss = _rope_tab(te, 16, 500000.0)
        one = np.ones((48, EXT), np.float32)
        zero = np.zeros((48, EXT), np.float32)
        c64 = np.concatenate([cs, cs, one])
        s64 = np.concatenate([-ss, ss, zero])
        t_swa = np.stack([np.concatenate([c64, c64]), np.concatenate([s64, s64])])
        masks = np.stack([mP, mN, mP * (0.0 if ch == 0 else 1.0), mN * (0.0 if ch == 3 else 1.0)])
        in_maps.append({
            "xT": xTc, "gcols": g, "w_kv": wkv, "w_q0": wq0, "w_uq": wuq, "w_uk": wuk, "w_uv": wuv,
            "w_o0": wo0, "w_up": wup, "w_dn": wdn, "w_qkv1": wqkv1, "w_o1": wo1,
            "t_mla": np.ascontiguousarray(t_mla, np.float32), "t_gqa": np.ascontiguousarray(t_gqa, np.float32),
            "t_swa": np.ascontiguousarray(t_swa, np.float32), "perms": perms,
            "masks": np.ascontiguousarray(masks, np.float32), "sinkrep": sinkrep,
        })
    if "nc" not in _NC_CACHE:
        _NC_CACHE["nc"] = build_program()
    nc = _NC_CACHE["nc"]
    res = run_bass_kernel_spmd(nc, in_maps, core_ids=list(range(8)))
    out = np.empty((2, S, D), np.float32)
    for c in range(8):
        b, ch = c // 4, c % 4
        out[b, ch * 2048:(ch + 1) * 2048, :] = res.results[c]["yT"].T
    return out
```

```python
import numpy as np
from contextlib import ExitStack
import concourse.bass as bass
import concourse.mybir as mybir
from concourse.bass_utils import run_bass_kernel_spmd

F32 = mybir.dt.float32
BF16 = mybir.dt.bfloat16
AF = mybir.ActivationFunctionType
ALU = mybir.AluOpType

D = 2048
S = 8192
EXT = 2304
OWN = 2048
EPS = 1e-6
ENGS = ("pe", "act", "dve", "pool", "sp")
ENGATTR = {"pe": "tensor", "act": "scalar", "dve": "vector", "pool": "gpsimd", "sp": "sync"}


class Buf:
    __slots__ = ("name", "last_w", "readers", "ndma", "sem", "id")
    _n = 0

    def __init__(self, name="b"):
        self.name = name
        self.last_w = None
        self.readers = []
        self.ndma = 0
        self.sem = None
        Buf._n += 1
        self.id = Buf._n


class TB:
    __slots__ = ("t", "b")

    def __init__(self, t, name="b"):
        self.t = t
        self.b = Buf(name)


class Rot:
    def __init__(self, items):
        self.items = items
        self.i = 0

    def next(self):
        it = self.items[self.i % len(self.items)]
        self.i += 1
        return it


class Sems:
    def __init__(self, nc, es):
        self.nc = nc
        self.es = es
        self.eng = {e: [es.enter_context(nc.semaphore("prog_" + e)), 0] for e in ENGS}
        self.free = []
        self.n = 0

    def get_dma(self):
        if self.free:
            return self.free.pop()
        self.n += 1
        return [self.es.enter_context(self.nc.semaphore("dsem%d" % self.n)), 0]


class Op:
    __slots__ = ("eng", "fn", "waits", "signal", "dma_buf", "idx")

    def __init__(self, eng, fn):
        self.eng = eng
        self.fn = fn
        self.waits = []
        self.signal = False
        self.dma_buf = None
        self.idx = None


class Prog:
    def __init__(self, nc, sems):
        self.nc = nc
        self.sems = sems
        self.ops = {e: [] for e in ENGS}
        self.waited = {e: {} for e in ENGS}
        self.dma_bufs = []

    def _need(self, op, tok):
        if tok is None:
            return
        w = self.waited[op.eng]
        if tok[0] == "c":
            if tok[1] == op.eng and op.dma_buf is None:
                return
            key = tok[1]
            if w.get(key, -1) >= tok[2]:
                return
            w[key] = tok[2]
            op.waits.append(tok)
            self.ops[tok[1]][tok[2]].signal = True
        else:
            key = ("d", tok[1].id)
            if w.get(key, -1) >= tok[2]:
                return
            w[key] = tok[2]
            op.waits.append(tok)

    def op(self, eng, fn, reads=(), writes=(), dma=None):
        o = Op(eng, fn)
        o.idx = len(self.ops[eng])
        o.dma_buf = dma
        for b in reads:
            self._need(o, b.last_w)
        for b in writes:
            self._need(o, b.last_w)
            for t in b.readers:
                self._need(o, t)
        if dma is not None:
            if dma.ndma == 0 and dma.sem is None:
                self.dma_bufs.append(dma)
            dma.ndma += 1
            tok = ("d", dma, dma.ndma)
        else:
            tok = ("c", eng, o.idx)
        for b in reads:
            b.readers.append(tok)
        for b in writes:
            b.last_w = tok
            b.readers = []
        self.ops[eng].append(o)
        return tok

    def emit(self):
        nc = self.nc
        sems = self.sems
        for b in self.dma_bufs:
            b.sem = sems.get_dma()
        for e in ENGS:
            for o in reversed(self.ops[e]):
                if o.dma_buf is None:
                    o.signal = True
                    break
        sigval = {}
        final = {}
        for e in ENGS:
            c = sems.eng[e][1]
            for o in self.ops[e]:
                if o.signal and o.dma_buf is None:
                    c += 1
                    sigval[(e, o.idx)] = c
            final[e] = c

        def lower(tok):
            if tok[0] == "c":
                return sems.eng[tok[1]][0], sigval[(tok[1], tok[2])]
            b = tok[1]
            return b.sem[0], 16 * (b.sem[1] + tok[2])

        sp_final = final["sp"] + 1
        with nc.Block() as block:
            def make(e):
                def body(eng):
                    for o in self.ops[e]:
                        for t in o.waits:
                            s, v = lower(t)
                            eng.wait_ge(s, v)
                        ins = o.fn(eng)
                        if o.dma_buf is not None:
                            ins.then_inc(o.dma_buf.sem[0], 16)
                        elif o.signal:
                            ins.then_inc(sems.eng[e][0], 1)
                    if e == "sp":
                        for e2 in ENGS:
                            if e2 != "sp" and final[e2] > 0:
                                eng.wait_ge(sems.eng[e2][0], final[e2])
                        for b in self.dma_bufs:
                            eng.wait_ge(b.sem[0], 16 * (b.sem[1] + b.ndma))
                        eng.sem_inc(sems.eng["sp"][0], 1)
                    else:
                        eng.wait_ge(sems.eng["sp"][0], sp_final)
                return body
            for e in ENGS:
                getattr(block, ENGATTR[e])(make(e))
        for e in ENGS:
            sems.eng[e][1] = final[e]
        sems.eng["sp"][1] = sp_final
        for b in self.dma_bufs:
            b.sem[1] += b.ndma
            sems.free.append(b.sem)
            b.sem = None


class Ctx:
    def __init__(self, nc, sems, tag):
        self.nc = nc
        self.es = ExitStack()
        self.P = Prog(nc, sems)
        self.tag = tag
        self.k = 0

    def sb(self, shape, dt, name="t"):
        self.k += 1
        t = self.es.enter_context(self.nc.sbuf_tensor("%s_%s%d" % (self.tag, name, self.k), list(shape), dt))
        return TB(t, name)

    def ps(self, shape, name="p"):
        self.k += 1
        t = self.es.enter_context(self.nc.psum_tensor("%s_%s%d" % (self.tag, name, self.k), list(shape), F32))
        return TB(t, name)

    def close(self):
        self.P.emit()
        self.es.close()


def dma_in(P, q, dst, src_ap, dst_ap=None):
    d = dst.t[:] if dst_ap is None else dst_ap
    P.op(q, lambda e, d=d, s=src_ap: e.dma_start(out=d, in_=s), writes=[dst.b], dma=dst.b)


def dma_out(P, src, dst_ap, src_ap, dram_bufs=(), q="pool"):
    return P.op(q, lambda e, d=dst_ap, s=src_ap: e.dma_start(out=d, in_=s),
                reads=[src.b], writes=list(dram_bufs), dma=src.b)


def mm_group(P, out_ap, out_b, pairs, reads):
    n = len(pairs)

    def fn(e, pairs=pairs, out_ap=out_ap, n=n):
        ins = None
        for i, (l, r) in enumerate(pairs):
            ins = e.matmul(out_ap, lhsT=l, rhs=r, start=(i == 0), stop=(i == n - 1))
        return ins
    P.op("pe", fn, reads=reads, writes=[out_b])


class Common:
    def __init__(self, C, gcols_d, with_bd=False, nf32=6):
        P = C.P
        self.C = C
        self.ones = C.sb([128, 128], BF16, "ones")
        P.op("dve", lambda e: e.memset(self.ones.t[:], 1.0), writes=[self.ones.b])
        self.eps = C.sb([128, 1], F32, "eps")
        P.op("dve", lambda e: e.memset(self.eps.t[:], EPS), writes=[self.eps.b])
        self.g = C.sb([128, 84], F32, "g")
        dma_in(P, "sp", self.g, gcols_d)
        self.sq = Rot([C.sb([128, 512], BF16, "sq") for _ in range(3)])
        self.f32 = Rot([C.sb([128, 512], F32, "f") for _ in range(nf32)])
        self.rstd_mode = "dve"
        self.rawb = Rot([C.sb([128, 512], BF16, "rawb") for _ in range(2)])
        if with_bd:
            self.bd = C.sb([128, 128], BF16, "bd")
            P.op("dve", lambda e: e.memset(self.bd.t[:], 0.0), writes=[self.bd.b])
            P.op("dve", lambda e: e.memset(self.bd.t[0:64, 0:64], 1.0), writes=[self.bd.b])
            P.op("dve", lambda e: e.memset(self.bd.t[64:128, 64:128], 1.0), writes=[self.bd.b])


def sq_accum(P, K, srcs, ssqs, n, mrows=128, lhs=None):
    ssq = ssqs.next()
    lhs = K.ones if lhs is None else lhs
    m = len(srcs)
    for i, (ap, b, rows) in enumerate(srcs):
        sq = K.sq.next()
        P.op("act", lambda e, o=sq.t[0:rows, 0:n], a=ap: e.activation(out=o, in_=a, func=AF.Square),
             reads=[b], writes=[sq.b])
        P.op("pe", lambda e, o=ssq.t[0:mrows, 0:n], l=lhs.t[0:rows, 0:mrows], r=sq.t[0:rows, 0:n], i=i:
             e.matmul(o, lhsT=l, rhs=r, start=(i == 0), stop=(i == m - 1)),
             reads=[sq.b, lhs.b], writes=[ssq.b])
    return ssq


def rstd_a(P, K, ssq, d, n, rows=128):
    v = K.f32.next()
    if K.rstd_mode == "dve":
        P.op("act", lambda e, o=v.t[0:rows, 0:n], a=ssq.t[0:rows, 0:n]:
             e.activation(out=o, in_=a, func=AF.Sqrt, bias=K.eps.t[0:rows, 0:1], scale=1.0 / d),
             reads=[ssq.b, K.eps.b], writes=[v.b])
    else:
        P.op("dve", lambda e: e.tensor_scalar(out=v.t[0:rows, 0:n], in0=ssq.t[0:rows, 0:n], scalar1=1.0 / d,
                                              scalar2=EPS, op0=ALU.mult, op1=ALU.add), reads=[ssq.b], writes=[v.b])
    return v


def rstd_b(P, K, v, n, rows=128):
    if K.rstd_mode == "dve":
        P.op("dve", lambda e, o=v.t[0:rows, 0:n]: e.reciprocal(out=o, in_=o), reads=[v.b], writes=[v.b])
        return v
    t = K.f32.next()
    P.op("act", lambda e: e.activation(out=t.t[0:rows, 0:n], in_=v.t[0:rows, 0:n], func=AF.Ln),
         reads=[v.b], writes=[t.b])
    r = K.f32.next()
    P.op("act", lambda e: e.activation(out=r.t[0:rows, 0:n], in_=t.t[0:rows, 0:n], func=AF.Exp, scale=-0.5),
         reads=[t.b], writes=[r.b])
    return r


def rstd_of(P, K, ssq, d, n, rows=128):
    return rstd_b(P, K, rstd_a(P, K, ssq, d, n, rows), n, rows)


def apply_plain(P, K, src_ap, src_b, gcol, r, out_ap, out_b, n, rows=128, eng="dve"):
    P.op(eng, lambda e: e.scalar_tensor_tensor(out=out_ap, in0=src_ap, scalar=K.g.t[0:rows, gcol:gcol + 1],
                                               in1=r.t[0:rows, 0:n], op0=ALU.mult, op1=ALU.mult),
         reads=[src_b, K.g.b, r.b], writes=[out_b])


def apply_rope(P, K, src, n, rows, perm, gcol, Ctab, Stab, r, out_ap, out_b, psw, eng2="pool"):
    raw = K.rawb.next()
    P.op("act", lambda e: e.activation(out=raw.t[0:rows, 0:n], in_=src.t[0:rows, 0:n], func=AF.Copy),
         reads=[src.b], writes=[raw.b])
    P.op("pe", lambda e: e.matmul(psw.t[0:rows, 0:n], lhsT=perm.t[0:rows, 0:rows], rhs=raw.t[0:rows, 0:n],
                                  start=True, stop=True), reads=[raw.b, perm.b], writes=[psw.b])
    t1 = K.f32.next()
    P.op("dve", lambda e: e.scalar_tensor_tensor(out=t1.t[0:rows, 0:n], in0=src.t[0:rows, 0:n],
                                                 scalar=K.g.t[0:rows, gcol:gcol + 1], in1=Ctab[0],
                                                 op0=ALU.mult, op1=ALU.mult),
         reads=[src.b, K.g.b, Ctab[1]], writes=[t1.b])
    t2 = K.f32.next()
    P.op("dve", lambda e: e.scalar_tensor_tensor(out=t2.t[0:rows, 0:n], in0=psw.t[0:rows, 0:n],
                                                 scalar=K.g.t[0:rows, gcol + 1:gcol + 2], in1=Stab[0],
                                                 op0=ALU.mult, op1=ALU.mult),
         reads=[psw.b, K.g.b, Stab[1]], writes=[t2.b])
    P.op(eng2, lambda e: e.tensor_tensor(out=t1.t[0:rows, 0:n], in0=t1.t[0:rows, 0:n], in1=t2.t[0:rows, 0:n],
                                         op=ALU.add), reads=[t1.b, t2.b], writes=[t1.b])
    P.op(eng2, lambda e: e.tensor_tensor(out=out_ap, in0=t1.t[0:rows, 0:n], in1=r.t[0:rows, 0:n],
                                         op=ALU.mult), reads=[t1.b, r.b], writes=[out_b])


def xnorm(P, K, xt, hT, gbase, n, ssq):
    ssq_ = sq_accum(P, K, [(xt.t[:, k, 0:n], xt.b, 128) for k in range(16)], ssq, n)
    r = rstd_of(P, K, ssq_, float(D), n)
    for k in range(16):
        apply_plain(P, K, xt.t[:, k, 0:n], xt.b, gbase + k, r, hT.t[:, k, 0:n], hT.b, n,
                    eng="dve")


def build_program():
    nc = bass.Bass("TRN2", target_bir_lowering=False)

    def din(name, shape):
        return nc.dram_tensor(name, list(shape), F32, kind="ExternalInput").ap()

    def dscr(name, shape, dt):
        return nc.dram_tensor(name, list(shape), dt, kind="Internal").ap()

    xT = din("xT", [D, S])
    gcols = din("gcols", [128, 84])
    w_kv = din("w_kv", [128, 16 * 832])
    w_q0 = din("w_q0", [128, 16 * 1536])
    w_uq = din("w_uq", [128, 4 * 1536])
    w_uk = din("w_uk", [128, 2 * 1024])
    w_uv = din("w_uv", [128, 2 * 1024])
    w_o0 = din("w_o0", [4, 128, 8192])
    w_up = din("w_up", [2, 16, 128, 8192])
    w_dn = din("w_dn", [2, 2, 8, 128, 8192])
    w_qkv1 = din("w_qkv1", [5, 128, 8192])
    w_o1 = din("w_o1", [4, 128, 8192])
    t_mla = din("t_mla", [2, 64, S])
    t_gqa = din("t_gqa", [2, 128, S])
    t_swa = din("t_swa", [2, 128, EXT])
    perms = din("perms", [3, 128, 128])
    masks = din("masks", [4, 128, 512])
    sinkrep = din("sinkrep", [64, 32])
    yT = nc.dram_tensor("yT", [D, OWN], F32, kind="ExternalOutput").ap()

    kn_d = dscr("kn_d", [8, 128, S], BF16)
    kr_d = dscr("kr_d", [64, S], BF16)
    kg_d = dscr("kg_d", [2, 128, S], BF16)
    va_d = dscr("va_d", [8, 128, 64, 128], BF16)
    vg_d = dscr("vg_d", [2, 128, 64, 128], BF16)
    qn_d = dscr("qn_d", [8, 128, EXT], BF16)
    qr_d = dscr("qr_d", [8, 64, EXT], BF16)
    qg_d = dscr("qg_d", [8, 128, EXT], BF16)
    mg_d = dscr("mg_d", [D, EXT], BF16)
    x1_d = dscr("x1_d", [D, EXT], F32)
    q1_d = dscr("q1_d", [4, 64, 18, 8, 128], BF16)
    k1_d = dscr("k1_d", [4, 64, EXT], BF16)
    v1_d = dscr("v1_d", [128, 18, 256], BF16)
    m1_d = dscr("m1_d", [D, OWN], BF16)
    h0_d = dscr("h0_d", [D, EXT], BF16)
    wb_o0 = dscr("wb_o0", [4, 128, 8192], BF16)
    wb_up = dscr("wb_up", [2, 16, 128, 8192], BF16)
    wb_dn = dscr("wb_dn", [2, 2, 8, 128, 8192], BF16)
    wb_qkv1 = dscr("wb_qkv1", [5, 128, 8192], BF16)
    wb_o1 = dscr("wb_o1", [4, 128, 8192], BF16)

    es = ExitStack()
    sems = Sems(nc, es)
    xT_v = xT.rearrange("(k p) t -> p k t", p=128)

    C = Ctx(nc, sems, "p0")
    P = C.P
    K = Common(C, gcols, nf32=10)
    K.rstd_mode = "act"
    wkv = C.sb([128, 16, 832], BF16, "wkv")
    dma_in(P, "pool", wkv, w_kv.rearrange("p (k c) -> p k c", k=16))
    wuk = C.sb([128, 2, 1024], BF16, "wuk")
    dma_in(P, "pool", wuk, w_uk.rearrange("p (k c) -> p k c", k=2))
    wuv = C.sb([128, 2, 1024], BF16, "wuv")
    dma_in(P, "pool", wuv, w_uv.rearrange("p (k c) -> p k c", k=2))
    pm = C.sb([128, 3, 128], BF16, "pm")
    dma_in(P, "pool", pm, perms.rearrange("a p m -> p a m"))
    pm_mla = TB(pm.t[:, 0, :]); pm_mla.b = pm.b
    pm_gqa = TB(pm.t[:, 1, :]); pm_gqa.b = pm.b
    xts = Rot([C.sb([128, 16, 512], F32, "xt") for _ in range(2)])
    hT = C.sb([128, 16, 512], BF16, "hT")
    tabs = Rot([(C.sb([64, 2, 512], F32, "tm"), C.sb([128, 2, 512], F32, "tg")) for _ in range(2)])
    ckvn = C.sb([128, 2, 512], BF16, "ckvn")
    kst = Rot([(C.sb([128, 8, 512], BF16, "knst"), C.sb([64, 512], BF16, "krst"),
                C.sb([128, 2, 512], BF16, "kgst"), C.sb([128, 4, 1024], BF16, "vast"),
                C.sb([128, 4, 256], BF16, "vgst")) for _ in range(2)])
    ssq = Rot([C.ps([128, 512], "ssq") for _ in range(2)])
    prot = Rot([C.ps([128, 512], "pr") for _ in range(6)])
    for t in range(16):
        c0 = t * 512
        xt = xts.next()
        dma_in(P, "sp", xt, xT_v[:, :, c0:c0 + 512])
        tm, tg = tabs.next()
        dma_in(P, "sp", tm, t_mla[:, :, c0:c0 + 512].rearrange("a p t -> p a t"))
        dma_in(P, "sp", tg, t_gqa[:, :, c0:c0 + 512].rearrange("a p t -> p a t"))
        xnorm(P, K, xt, hT, 0, 512, ssq)
        if c0 < EXT:
            n_ = min(512, EXT - c0)
            dma_out(P, hT, h0_d.rearrange("(k p) t -> p k t", p=128)[:, :, c0:c0 + n_], hT.t[:, :, 0:n_])
        knst, krst, kgst, vast, vgst = kst.next()
        pck = [prot.next(), prot.next()]
        for j in range(2):
            mm_group(P, pck[j].t[:, :], pck[j].b,
                     [(wkv.t[:, kc, j * 128:(j + 1) * 128], hT.t[:, kc, :]) for kc in range(16)], [wkv.b, hT.b])
        ssq_ = sq_accum(P, K, [(pck[j].t[:, :], pck[j].b, 128) for j in range(2)], ssq, 512)
        r = rstd_of(P, K, ssq_, 256.0, 512)
        for j in range(2):
            apply_plain(P, K, pck[j].t[:, :], pck[j].b, 68 + j, r, ckvn.t[:, j, :], ckvn.b, 512)
        def kn_a(h):
            pk = prot.next()
            mm_group(P, pk.t[:, :], pk.b,
                     [(wuk.t[:, kc, h * 128:(h + 1) * 128], ckvn.t[:, kc, :]) for kc in range(2)], [wuk.b, ckvn.b])
            ssq_ = sq_accum(P, K, [(pk.t[:, :], pk.b, 128)], ssq, 512)
            return pk, rstd_a(P, K, ssq_, 128.0, 512)

        def kn_b(h, pk, v):
            r = rstd_b(P, K, v, 512)
            apply_plain(P, K, pk.t[:, :], pk.b, 73, r, knst.t[:, h, :], knst.b, 512)
        cur = kn_a(0)
        for h in range(8):
            nxt = kn_a(h + 1) if h + 1 < 8 else None
            kn_b(h, *cur)
            cur = nxt
        for s in range(4):
            for hf in range(2):
                pv = prot.next()
                mm_group(P, pv.t[:, :], pv.b,
                         [(ckvn.t[:, kc, s * 128:(s + 1) * 128], wuv.t[:, kc, hf * 512:(hf + 1) * 512]) for kc in range(2)],
                         [wuv.b, ckvn.b])
                eng = "act" if (s * 2 + hf) % 2 == 0 else "dve"
                if eng == "act":
                    P.op("act", lambda e, o=vast.t[:, s, hf * 512:(hf + 1) * 512], a=pv.t[:, :]:
                         e.activation(out=o, in_=a, func=AF.Copy), reads=[pv.b], writes=[vast.b])
                else:
                    P.op("dve", lambda e, o=vast.t[:, s, hf * 512:(hf + 1) * 512], a=pv.t[:, :]:
                         e.tensor_copy(out=o, in_=a), reads=[pv.b], writes=[vast.b])
        pk = prot.next()
        mm_group(P, pk.t[0:64, :], pk.b, [(wkv.t[:, kc, 256:320], hT.t[:, kc, :]) for kc in range(16)], [wkv.b, hT.b])
        ssq_ = sq_accum(P, K, [(pk.t[0:64, :], pk.b, 64)], ssq, 512, mrows=64)
        r = rstd_of(P, K, ssq_, 64.0, 512, rows=64)
        psw = prot.next()
        apply_rope(P, K, pk, 512, 64, pm_mla, 74, (tm.t[:, 0, :], tm.b), (tm.t[:, 1, :], tm.b), r,
                   krst.t[:, :], krst.b, psw)
        for j in range(2):
            pk = prot.next()
            mm_group(P, pk.t[:, :], pk.b,
                     [(wkv.t[:, kc, 320 + j * 128:320 + (j + 1) * 128], hT.t[:, kc, :]) for kc in range(16)],
                     [wkv.b, hT.b])
            ssq_ = sq_accum(P, K, [(pk.t[:, :], pk.b, 128)], ssq, 512)
            r = rstd_of(P, K, ssq_, 128.0, 512)
            psw = prot.next()
            apply_rope(P, K, pk, 512, 128, pm_gqa, 78, (tg.t[:, 0, :], tg.b), (tg.t[:, 1, :], tg.b), r,
                       kgst.t[:, j, :], kgst.b, psw)
        for s in range(4):
            pv = prot.next()
            mm_group(P, pv.t[:, 0:256], pv.b,
                     [(hT.t[:, kc, s * 128:(s + 1) * 128], wkv.t[:, kc, 576:832]) for kc in range(16)], [wkv.b, hT.b])
            P.op("act", lambda e, o=vgst.t[:, s, :], a=pv.t[:, 0:256]: e.activation(out=o, in_=a, func=AF.Copy),
                 reads=[pv.b], writes=[vgst.b])
        dma_out(P, knst, kn_d[:, :, c0:c0 + 512].rearrange("h p t -> p h t"), knst.t[:])
        dma_out(P, krst, kr_d[:, c0:c0 + 512], krst.t[:])
        dma_out(P, kgst, kg_d[:, :, c0:c0 + 512].rearrange("h p t -> p h t"), kgst.t[:])
        for s in range(4):
            dma_out(P, vast, va_d[:, :, 4 * t + s, :].rearrange("h p d -> p h d"),
                    vast.t[:, s, :].rearrange("p (h d) -> p h d", h=8))
        for s in range(4):
            dma_out(P, vgst, vg_d[:, :, 4 * t + s, :].rearrange("h p d -> p h d"),
                    vgst.t[:, s, :].rearrange("p (h d) -> p h d", h=2))
    C.close()

    C = Ctx(nc, sems, "p1")
    P = C.P
    K = Common(C, gcols)
    wq = C.sb([128, 16, 1536], BF16, "wq")
    dma_in(P, "pool", wq, w_q0.rearrange("p (k c) -> p k c", k=16))
    wuq = C.sb([128, 4, 1536], BF16, "wuq")
    dma_in(P, "pool", wuq, w_uq.rearrange("p (k c) -> p k c", k=4))
    pm = C.sb([128, 3, 128], BF16, "pm")
    dma_in(P, "pool", pm, perms.rearrange("a p m -> p a m"))
    pm_mla = TB(pm.t[:, 0, :]); pm_mla.b = pm.b
    pm_gqa = TB(pm.t[:, 1, :]); pm_gqa.b = pm.b
    hTs = Rot([C.sb([128, 16, 512], BF16, "hT") for _ in range(2)])
    tabs = Rot([(C.sb([64, 2, 512], F32, "tm"), C.sb([128, 2, 512], F32, "tg")) for _ in range(2)])
    cqn = C.sb([128, 4, 512], BF16, "cqn")
    qst = Rot([(C.sb([128, 8, 512], BF16, "qnst"), C.sb([64, 8, 512], BF16, "qrst"),
                C.sb([128, 8, 512], BF16, "qgst")) for _ in range(1)])
    ssq = Rot([C.ps([128, 512], "ssq") for _ in range(2)])
    prot = Rot([C.ps([128, 512], "pr") for _ in range(6)])
    for t in range(5):
        c0 = t * 512
        n = min(512, EXT - c0)
        hT = hTs.next()
        dma_in(P, "sp", hT, h0_d.rearrange("(k p) t -> p k t", p=128)[:, :, c0:c0 + n], hT.t[:, :, 0:n])
        tm, tg = tabs.next()
        dma_in(P, "sp", tm, t_mla[:, :, c0:c0 + n].rearrange("a p t -> p a t"), tm.t[:, :, 0:n])
        dma_in(P, "sp", tg, t_gqa[:, :, c0:c0 + n].rearrange("a p t -> p a t"), tg.t[:, :, 0:n])
        qnst, qrst, qgst = qst.next()
        pcq = [prot.next() for _ in range(4)]
        for j in range(4):
            mm_group(P, pcq[j].t[:, 0:n], pcq[j].b,
                     [(wq.t[:, kc, j * 128:(j + 1) * 128], hT.t[:, kc, 0:n]) for kc in range(16)], [wq.b, hT.b])
        ssq_ = sq_accum(P, K, [(pcq[j].t[:, 0:n], pcq[j].b, 128) for j in range(4)], ssq, n)
        r = rstd_of(P, K, ssq_, 512.0, n)
        for j in range(4):
            apply_plain(P, K, pcq[j].t[:, 0:n], pcq[j].b, 64 + j, r, cqn.t[:, j, 0:n], cqn.b, n)
        for h in range(8):
            pn = prot.next()
            mm_group(P, pn.t[:, 0:n], pn.b,
                     [(wuq.t[:, kc, h * 192:h * 192 + 128], cqn.t[:, kc, 0:n]) for kc in range(4)], [wuq.b, cqn.b])
            pr = prot.next()
            mm_group(P, pr.t[0:64, 0:n], pr.b,
                     [(wuq.t[:, kc, h * 192 + 128:h * 192 + 192], cqn.t[:, kc, 0:n]) for kc in range(4)], [wuq.b, cqn.b])
            ssq_ = sq_accum(P, K, [(pn.t[:, 0:n], pn.b, 128), (pr.t[0:64, 0:n], pr.b, 64)], ssq, n)
            r = rstd_of(P, K, ssq_, 192.0, n)
            apply_plain(P, K, pn.t[:, 0:n], pn.b, 70, r, qnst.t[:, h, 0:n], qnst.b, n)
            psw = prot.next()
            apply_rope(P, K, pr, n, 64, pm_mla, 71, (tm.t[:, 0, 0:n], tm.b), (tm.t[:, 1, 0:n], tm.b), r,
                       qrst.t[:, h, 0:n], qrst.b, psw)
        for h in range(8):
            pg = prot.next()
            mm_group(P, pg.t[:, 0:n], pg.b,
                     [(wq.t[:, kc, 512 + h * 128:512 + (h + 1) * 128], hT.t[:, kc, 0:n]) for kc in range(16)],
                     [wq.b, hT.b])
            ssq_ = sq_accum(P, K, [(pg.t[:, 0:n], pg.b, 128)], ssq, n)
            r = rstd_of(P, K, ssq_, 128.0, n)
            psw = prot.next()
            apply_rope(P, K, pg, n, 128, pm_gqa, 76, (tg.t[:, 0, 0:n], tg.b), (tg.t[:, 1, 0:n], tg.b), r,
                       qgst.t[:, h, 0:n], qgst.b, psw)
        dma_out(P, qnst, qn_d[:, :, c0:c0 + n].rearrange("h p t -> p h t"), qnst.t[:, :, 0:n])
        dma_out(P, qrst, qr_d[:, :, c0:c0 + n].rearrange("h p t -> p h t"), qrst.t[:, :, 0:n])
        dma_out(P, qgst, qg_d[:, :, c0:c0 + n].rearrange("h p t -> p h t"), qgst.t[:, :, 0:n])
    C.close()

    C = Ctx(nc, sems, "p2")
    P = C.P
    ones = C.sb([128, 128], BF16, "ones")
    P.op("dve", lambda e: e.memset(ones.t[:], 1.0), writes=[ones.b])
    krs = C.sb([128, S], BF16, "krs")
    P.op("dve", lambda e: e.memset(krs.t[64:128, :], 0.0), writes=[krs.b])
    dma_in(P, "sp", krs, kr_d, krs.t[0:64, :])
    cvb = Buf("cv")
    cvl = [(wb_o0[i], w_o0[i]) for i in range(4)]
    cvl += [(wb_up[0, i], w_up[0, i]) for i in range(16)]
    cvl += [(wb_dn[0, hf, i], w_dn[0, hf, i]) for hf in range(2) for i in range(8)]
    cvl += [(wb_qkv1[i], w_qkv1[i]) for i in range(5)]
    cvl += [(wb_o1[i], w_o1[i]) for i in range(4)]
    cvl += [(wb_up[1, i], w_up[1, i]) for i in range(16)]
    cvl += [(wb_dn[1, hf, i], w_dn[1, hf, i]) for hf in range(2) for i in range(8)]
    for (dst_, src_) in cvl:
        P.op("pool", lambda e, d=dst_, s_=src_: e.dma_start(out=d, in_=s_), writes=[cvb], dma=cvb)
    kts = Rot([C.sb([128, S], BF16, "kt") for _ in range(2)])
    vts = Rot([C.sb([128, 64, 128], BF16, "vt") for _ in range(2)])
    qns = Rot([C.sb([128, EXT], BF16, "qn") for _ in range(2)])
    qrs = Rot([C.sb([128, EXT], BF16, "qr") for _ in range(2)])
    for _q in qrs.items:
        P.op("dve", lambda e, _q=_q: e.memset(_q.t[64:128, :], 0.0), writes=[_q.b])
    pts = Rot([C.sb([128, 1024], BF16, "pt") for _ in range(4)])
    ones32 = C.sb([128, 128], F32, "ones32")
    P.op("dve", lambda e: e.memset(ones32.t[:], 1.0), writes=[ones32.b])
    accAs = Rot([C.sb([128, 2, 512], F32, "accA") for _ in range(2)])
    accBs = Rot([C.sb([128, 2, 512], F32, "accB") for _ in range(1)])
    hls = Rot([C.sb([128, 2, 512], BF16, "hl") for _ in range(2)])
    rds = Rot([C.sb([128, 512], F32, "rd") for _ in range(2)])
    ost = Rot([C.sb([128, 512], BF16, "ost") for _ in range(2)])
    sps = Rot([C.ps([128, 1024], "S") for _ in range(2)])
    ops_ = Rot([C.ps([128, 512], "O") for _ in range(2)])
    dps = Rot([C.ps([128, 512], "Dn") for _ in range(2)])
    qtiles = [(0, 512), (512, 512), (1024, 512), (1536, 512), (2048, 256)]
    kt = vt = None
    pend = []

    def flush_pend():
        while pend:
            o_, dst_, n_ = pend.pop(0)
            dma_out(P, o_, dst_, o_.t[:, 0:n_], q="act")
    for job in range(16):
        mla = job < 8
        if mla:
            kt = kts.next()
            dma_in(P, "sp", kt, kn_d[job])
            vt = vts.next()
            dma_in(P, "sp", vt, va_d[job])
            qn = qns.next()
            dma_in(P, "sp", qn, qn_d[job])
            qr = qrs.next()
            dma_in(P, "sp", qr, qr_d[job], qr.t[0:64, :])
            scale = 192.0 ** -0.5
        else:
            j = job - 8
            if j % 4 == 0:
                kt = kts.next()
                dma_in(P, "sp", kt, kg_d[j // 4])
                vt = vts.next()
                dma_in(P, "sp", vt, vg_d[j // 4])
            qn = qns.next()
            dma_in(P, "sp", qn, qg_d[j])
            scale = 128.0 ** -0.5
        for (q0, n) in qtiles:
            Ops = ops_.next()
            Dps = dps.next()

            def s_pair(kp, mla=mla, kt=kt, qn=qn, qr=(qr if mla else None), q0=q0, n=n):
                Sp = sps.next()

                def fn(e, Sp=Sp):
                    ins = None
                    for i in range(2):
                        kb = 2 * kp + i
                        o = Sp.t[:, i * n:(i + 1) * n] if n == 256 else Sp.t[:, i * 512:(i + 1) * 512]
                        ins = e.matmul(o, lhsT=kt.t[:, kb * 128:(kb + 1) * 128], rhs=qn.t[:, q0:q0 + n],
                                       start=True, stop=(not mla))
                        if mla:
                            ins = e.matmul(o, lhsT=krs.t[:, kb * 128:(kb + 1) * 128], rhs=qr.t[:, q0:q0 + n],
                                           start=False, stop=True)
                    return ins
                rd = [kt.b, qn.b] + ([krs.b, qr.b] if mla else [])
                P.op("pe", fn, reads=rd, writes=[Sp.b])
                return Sp

            def pv_pair(kp, Sp, vt=vt, n=n, Ops=Ops, Dps=Dps, scale=scale):
                pt = pts.next()
                w = 2 * n
                P.op("act", lambda e: e.activation(out=pt.t[:, 0:w], in_=Sp.t[:, 0:w], func=AF.Exp, scale=scale),
                     reads=[Sp.b], writes=[pt.b])

                def fn(e):
                    ins = None
                    for i in range(2):
                        kb = 2 * kp + i
                        first = (kb == 0)
                        last = (kb == 63)
                        e.matmul(Ops.t[:, 0:n], lhsT=vt.t[:, kb, :], rhs=pt.t[:, i * n:(i + 1) * n],
                                 start=first, stop=last)
                        ins = e.matmul(Dps.t[:, 0:n], lhsT=ones.t[:, :], rhs=pt.t[:, i * n:(i + 1) * n],
                                       start=first, stop=last)
                    return ins
                P.op("pe", fn, reads=[vt.b, pt.b, ones.b], writes=[Ops.b, Dps.b])

            prev = s_pair(0)
            for kp in range(32):
                nxt = s_pair(kp + 1) if kp + 1 < 32 else None
                pv_pair(kp, prev)
                prev = nxt
                if kp == 4:
                    flush_pend()
            rd = rds.next()
            P.op("dve", lambda e, rd=rd, Dps=Dps, n=n: e.reciprocal(out=rd.t[:, 0:n], in_=Dps.t[:, 0:n]),
                 reads=[Dps.b], writes=[rd.b])
            o = ost.next()
            P.op("dve", lambda e, o=o, rd=rd, Ops=Ops, n=n: e.tensor_tensor(out=o.t[:, 0:n], in0=Ops.t[:, 0:n],
                                                                           in1=rd.t[:, 0:n], op=ALU.mult),
                 reads=[Ops.b, rd.b], writes=[o.b])
            pend.append((o, mg_d[job * 128:(job + 1) * 128, q0:q0 + n], n))
    flush_pend()
    C.close()

    def tail_phase(tag, layer):
        C = Ctx(nc, sems, tag)
        P = C.P
        K = Common(C, gcols, with_bd=(layer == 0))
        slabs = Rot([C.sb([128, 8192], BF16, "slab") for _ in range(4)])
        xt = C.sb([128, 16, 512], F32, "xt")
        hT = C.sb([128, 16, 512], BF16, "hT")
        hid = C.sb([128, 32, 512], BF16, "hid")
        rls = Rot([C.sb([128, 512], F32, "rl") for _ in range(3)])
        ssq = Rot([C.ps([128, 512], "ssq") for _ in range(2)])
        prot = Rot([C.ps([128, 512], "pr") for _ in range(6)])
        if layer == 0:
            pm = C.sb([128, 128], BF16, "pm")
            dma_in(P, "pool", pm, perms[2])
            tsw = Rot([C.sb([128, 2, 512], F32, "tsw") for _ in range(2)])
            qst = Rot([C.sb([128, 512], BF16, "qst") for _ in range(3)])
            vst = Rot([C.sb([128, 4, 256], BF16, "vst") for _ in range(1)])
            ntiles = 5
            wo, mgsrc, xsrc = wb_o0, mg_d, None
        else:
            ntiles = 4
            wo, mgsrc = wb_o1, m1_d
        outtoks = []

        def load_slab(src_ap):
            sl = slabs.next()
            dma_in(P, "sp", sl, src_ap)
            return sl

        mg_v = mgsrc.rearrange("(k p) t -> p k t", p=128)
        for t in range(ntiles):
            if layer == 0:
                c0 = t * 512
                n = min(512, EXT - c0)
                dma_in(P, "pool", hT, mg_v[:, :, c0:c0 + n], hT.t[:, :, 0:n])
                if t == 0:
                    dma_in(P, "pool", xt, xT_v[:, :, c0:c0 + n], xt.t[:, :, 0:n])
            else:
                c0 = t * 512
                n = 512
                if t == 0:
                    dma_in(P, "pool", hT, mg_v[:, :, c0:c0 + n])
                dma_in(P, "pool", xt, x1_d.rearrange("(k p) t -> p k t", p=128)[:, :, 128 + c0:128 + c0 + n])
            for s in range(4):
                sl = load_slab(wo[s])
                slv = sl.t[:].rearrange("p (k c) -> p k c", k=16)
                for j in range(4):
                    c = 4 * s + j
                    pp = prot.next()
                    mm_group(P, pp.t[:, 0:n], pp.b,
                             [(slv[:, kc, j * 128:(j + 1) * 128], hT.t[:, kc, 0:n]) for kc in range(16)], [sl.b, hT.b])
                    P.op("dve", lambda e, c=c, pp=pp, n=n: e.tensor_tensor(out=xt.t[:, c, 0:n], in0=pp.t[:, 0:n],
                                                                           in1=xt.t[:, c, 0:n], op=ALU.add),
                         reads=[pp.b, xt.b], writes=[xt.b])
            xnorm(P, K, xt, hT, 16 + 32 * layer, n, ssq)
            for hf in range(2):
                for s in range(8):
                    sl = load_slab(wb_up[layer, hf * 8 + s])
                    slv = sl.t[:].rearrange("p (k c) -> p k c", k=16)
                    for j in range(4):
                        hc = s * 4 + j
                        pp = prot.next()
                        mm_group(P, pp.t[:, 0:n], pp.b,
                                 [(slv[:, kc, j * 128:(j + 1) * 128], hT.t[:, kc, 0:n]) for kc in range(16)],
                                 [sl.b, hT.b])
                        rl = rls.next()
                        P.op("act", lambda e, rl=rl, pp=pp, n=n: e.activation(out=rl.t[:, 0:n], in_=pp.t[:, 0:n],
                                                                              func=AF.Relu),
                             reads=[pp.b], writes=[rl.b])
                        P.op("dve", lambda e, rl=rl, hc=hc, n=n: e.tensor_tensor(out=hid.t[:, hc, 0:n], in0=rl.t[:, 0:n],
                                                                              in1=rl.t[:, 0:n], op=ALU.mult),
                             reads=[rl.b], writes=[hid.b])
                if layer == 1 and hf == 1 and t + 1 < ntiles:
                    dma_in(P, "pool", hT, mg_v[:, :, c0 + 512:c0 + 1024])
                for s in range(8):
                    sl = load_slab(wb_dn[layer, hf, s])
                    slv = sl.t[:].rearrange("p (k c) -> p k c", k=32)
                    for j in range(2):
                        c = 2 * s + j
                        pp = prot.next()
                        mm_group(P, pp.t[:, 0:n], pp.b,
                                 [(slv[:, kc, j * 128:(j + 1) * 128], hid.t[:, kc, 0:n]) for kc in range(32)],
                                 [sl.b, hid.b])
                        P.op("dve", lambda e, c=c, pp=pp, n=n: e.tensor_tensor(out=xt.t[:, c, 0:n], in0=pp.t[:, 0:n],
                                                                               in1=xt.t[:, c, 0:n], op=ALU.add),
                             reads=[pp.b, xt.b], writes=[xt.b])
            if layer == 1:
                outtoks.append(dma_out(P, xt, yT.rearrange("(k p) t -> p k t", p=128)[:, :, c0:c0 + n], xt.t[:, :, 0:n]))
                continue
            dma_out(P, xt, x1_d.rearrange("(k p) t -> p k t", p=128)[:, :, c0:c0 + n], xt.t[:, :, 0:n])
            ts_ = tsw.next()
            dma_in(P, "pool", ts_, t_swa[:, :, c0:c0 + n].rearrange("a p t -> p a t"), ts_.t[:, :, 0:n])
            xnorm(P, K, xt, hT, 32, n, ssq)
            if t + 1 < ntiles:
                c1 = (t + 1) * 512
                n1 = min(512, EXT - c1)
                dma_in(P, "pool", xt, xT_v[:, :, c1:c1 + n1], xt.t[:, :, 0:n1])
            nb = n // 128
            b0 = c0 // 128
            for s in range(5):
                sl = load_slab(wb_qkv1[s])
                slv = sl.t[:].rearrange("p (k c) -> p k c", k=16)
                nch = 4 if s < 4 else 2
                for j in range(nch):
                    pq = prot.next()
                    mm_group(P, pq.t[:, 0:n], pq.b,
                             [(slv[:, kc, j * 128:(j + 1) * 128], hT.t[:, kc, 0:n]) for kc in range(16)], [sl.b, hT.b])
                    ssq_ = sq_accum(P, K, [(pq.t[:, 0:n], pq.b, 128)], ssq, n, lhs=K.bd)
                    r = rstd_of(P, K, ssq_, 64.0, n)
                    psw = prot.next()
                    qs = qst.next()
                    pmtb = pm
                    apply_rope(P, K, pq, n, 128, pmtb, 80 if s < 4 else 82,
                               (ts_.t[:, 0, 0:n], ts_.b), (ts_.t[:, 1, 0:n], ts_.b), r, qs.t[:, 0:n], qs.b, psw, eng2="dve")
                    if s < 4:
                        c = 4 * s + j
                        for u in range(2):
                            h = 2 * c + u
                            kvh, hh = h // 8, h % 8
                            dma_out(P, qs, q1_d[kvh, :, b0:b0 + nb, hh, :],
                                    qs.t[u * 64:(u + 1) * 64, 0:n].rearrange("p (b q) -> p b q", q=128))
                    else:
                        for u in range(2):
                            dma_out(P, qs, k1_d[2 * j + u, :, c0:c0 + n], qs.t[u * 64:(u + 1) * 64, 0:n])
                if s == 4:
                    vs = vst.next()
                    for sb_ in range(nb):
                        pv = prot.next()
                        mm_group(P, pv.t[:, 0:256], pv.b,
                                 [(hT.t[:, kc, sb_ * 128:(sb_ + 1) * 128], slv[:, kc, 256:512]) for kc in range(16)],
                                 [sl.b, hT.b])
                        P.op("act", lambda e, vs=vs, sb_=sb_, pv=pv: e.activation(out=vs.t[:, sb_, :], in_=pv.t[:, 0:256],
                                                                                  func=AF.Copy),
                             reads=[pv.b], writes=[vs.b])
                    dma_out(P, vs, v1_d[:, b0:b0 + nb, :], vs.t[:, 0:nb, :])
        C.close()
        return outtoks

    tail_phase("p3", 0)

    C = Ctx(nc, sems, "p4")
    P = C.P
    ones = C.sb([128, 128], BF16, "ones")
    P.op("dve", lambda e: e.memset(ones.t[:], 1.0), writes=[ones.b])
    k1s = C.sb([64, 4, EXT], BF16, "k1s")
    dma_in(P, "sp", k1s, k1_d.rearrange("k p t -> p k t"))
    v1s = C.sb([128, 18, 256], BF16, "v1s")
    dma_in(P, "sp", v1s, v1_d)
    mks = C.sb([128, 4, 512], BF16, "mks")
    dma_in(P, "pool", mks, masks.rearrange("a p t -> p a t"))
    esk = C.sb([64, 32], F32, "esk")
    dma_in(P, "sp", esk, sinkrep)
    P.op("act", lambda e: e.activation(out=esk.t[:], in_=esk.t[:], func=AF.Exp), reads=[esk.b], writes=[esk.b])
    eskx = C.sb([64, 32, 128], F32, "eskx")
    P.op("dve", lambda e: e.memset(eskx.t[:], 0.0), writes=[eskx.b])
    for h in range(32):
        P.op("dve", lambda e, h=h: e.tensor_scalar(out=eskx.t[:, h, :], in0=eskx.t[:, h, :], scalar1=esk.t[:, h:h + 1],
                                                   scalar2=None, op0=ALU.add), reads=[esk.b, eskx.b], writes=[eskx.b])
    q1s = Rot([C.sb([64, 18, 8, 128], BF16, "q1s") for _ in range(2)])
    pts = Rot([C.sb([128, 1536], BF16, "pt") for _ in range(3)])
    dns = Rot([C.sb([64, 512], F32, "dn") for _ in range(2)])
    dls = Rot([C.sb([64, 512], F32, "dl") for _ in range(2)])
    ost = Rot([C.sb([64, 512], BF16, "ost") for _ in range(2)])
    Sp = C.ps([128, 1536], "S")
    opsr = Rot([C.ps([128, 512], "O") for _ in range(2)])
    dpsr = Rot([C.ps([128, 512], "Dn") for _ in range(2)])
    sc1 = 64.0 ** -0.5
    q1_of = {}
    its = []
    for kv in range(4):
        for jq in range(16):
            for hg in range(2):
                its.append((kv, jq, hg))

    def s_mm(it):
        kv, jq, hg = it
        if kv not in q1_of:
            q1 = q1s.next()
            dma_in(P, "sp", q1, q1_d[kv])
            q1_of[kv] = q1
        q1 = q1_of[kv]

        def fnS(e):
            ins = None
            for kb in range(3):
                ins = e.matmul(Sp.t[:, kb * 512:(kb + 1) * 512],
                               lhsT=k1s.t[:, kv, (jq + kb) * 128:(jq + kb + 1) * 128],
                               rhs=q1.t[:, jq + 1, hg * 4:(hg + 1) * 4, :].rearrange("p h q -> p (h q)"),
                               start=True, stop=True)
            return ins
        P.op("pe", fnS, reads=[k1s.b, q1.b], writes=[Sp.b])

    def exp_mask(it):
        kv, jq, hg = it
        pt = pts.next()
        P.op("act", lambda e: e.activation(out=pt.t[:, :], in_=Sp.t[:, :], func=AF.Exp, scale=sc1),
             reads=[Sp.b], writes=[pt.b])
        mP = 2 if jq == 0 else 0
        mN = 3 if jq == 15 else 1
        P.op("dve", lambda e: e.tensor_tensor(out=pt.t[:, 0:512], in0=pt.t[:, 0:512], in1=mks.t[:, mP, :],
                                              op=ALU.mult), reads=[pt.b, mks.b], writes=[pt.b])
        P.op("pool", lambda e: e.tensor_tensor(out=pt.t[:, 1024:1536], in0=pt.t[:, 1024:1536], in1=mks.t[:, mN, :],
                                               op=ALU.mult), reads=[pt.b, mks.b], writes=[pt.b])
        return pt

    def pv_norm(it, pt):
        kv, jq, hg = it
        Ops = opsr.next()
        Dps = dpsr.next()

        def fnP(e):
            ins = None
            for kb in range(3):
                e.matmul(Ops.t[0:64, :], lhsT=v1s.t[:, jq + kb, kv * 64:(kv + 1) * 64],
                         rhs=pt.t[:, kb * 512:(kb + 1) * 512], start=(kb == 0), stop=(kb == 2))
            for kb in range(3):
                ins = e.matmul(Dps.t[0:64, :], lhsT=ones.t[:, 0:64], rhs=pt.t[:, kb * 512:(kb + 1) * 512],
                               start=(kb == 0), stop=(kb == 2))
            return ins
        P.op("pe", fnP, reads=[v1s.b, pt.b, ones.b], writes=[Ops.b, Dps.b])
        dn = dns.next()
        h0 = kv * 8 + hg * 4
        P.op("dve", lambda e: e.tensor_tensor(
            out=dn.t[:, :], in0=Dps.t[0:64, :], in1=eskx.t[:, h0:h0 + 4, :].rearrange("p h q -> p (h q)"),
            op=ALU.add), reads=[Dps.b, eskx.b], writes=[dn.b])
        dl = dls.next()
        P.op("act", lambda e: e.activation(out=dl.t[:, :], in_=dn.t[:, :], func=AF.Ln),
             reads=[dn.b], writes=[dl.b])
        P.op("act", lambda e: e.activation(out=dn.t[:, :], in_=dl.t[:, :], func=AF.Exp, scale=-1.0),
             reads=[dl.b], writes=[dn.b])
        o = ost.next()
        P.op("dve", lambda e: e.tensor_tensor(out=o.t[:, :], in0=Ops.t[0:64, :], in1=dn.t[:, :], op=ALU.mult),
             reads=[Ops.b, dn.b], writes=[o.b])
        r0 = (kv * 8 + hg * 4) * 64
        dma_out(P, o, m1_d[r0:r0 + 256, jq * 128:(jq + 1) * 128].rearrange("(h d) q -> d h q", d=64),
                o.t[:, :].rearrange("p (h q) -> p h q", q=128), q="sp")

    s_mm(its[0])
    pt_cur = exp_mask(its[0])
    for i, it in enumerate(its):
        if i + 1 < len(its):
            s_mm(its[i + 1])
            pt_nxt = exp_mask(its[i + 1])
        else:
            pt_nxt = None
        pv_norm(it, pt_cur)
        pt_cur = pt_nxt
    C.close()

    tail_phase("p5", 1)
    es.close()
    return nc


def _slab(W, cols=None):
    W = np.asarray(W, np.float32)
    if cols is not None:
        W = W[:, cols]
    Kd, n = W.shape
    return np.ascontiguousarray(W.reshape(Kd // 128, 128, n).transpose(1, 0, 2).reshape(128, (Kd // 128) * n))


def _rope_tab(pos, dim, theta):
    inv = np.float32(theta) ** (-np.arange(0, dim, 2, dtype=np.float32) / np.float32(dim))
    ang = pos.astype(np.float32)[:, None] * inv[None, :].astype(np.float32)
    ang = ang.astype(np.float32).astype(np.float64)
    return np.cos(ang).astype(np.float32).T, np.sin(ang).astype(np.float32).T


def _perm_lhsT(perm):
    n = len(perm)
    m = np.zeros((128, 128), np.float32)
    for o in range(n):
        m[perm[o], o] = 1.0
    return m


_NC_CACHE = {}


def kernel(x, even_norm, even_w_in, mla_q_lat_norm, mla_kv_lat_norm, mla_w_uq, mla_w_ukv,
           mla_q_norm, mla_k_nope_norm, mla_k_rope_norm, gqa_q_norm, gqa_k_norm, even_w_out,
           odd_norm, odd_w_qkv, swa_q_norm, swa_k_norm, swa_sink, odd_w_out,
           mlp_norm, mlp_w_up, mlp_w_down):
    f = lambda a: np.asarray(a, np.float32)
    x = f(x)
    p_mla = np.concatenate([np.arange(32, 64), np.arange(0, 32)])
    p_gqa = np.concatenate([p_mla, 64 + p_mla])
    p_sw64 = np.concatenate([np.arange(8, 16), np.arange(0, 8), np.arange(16, 64)])
    p_swa = np.concatenate([p_sw64, 64 + p_sw64])
    perms = np.stack([_perm_lhsT(p_mla), _perm_lhsT(p_gqa), _perm_lhsT(p_swa)])
    g = np.ones((128, 84), np.float32)
    g[:, 0:16] = f(even_norm)[0].reshape(16, 128).T
    g[:, 16:32] = f(mlp_norm)[0].reshape(16, 128).T
    g[:, 32:48] = f(odd_norm)[0].reshape(16, 128).T
    g[:, 48:64] = f(mlp_norm)[1].reshape(16, 128).T
    g[:, 64:68] = f(mla_q_lat_norm)[0].reshape(4, 128).T
    g[:, 68:70] = f(mla_kv_lat_norm)[0].reshape(2, 128).T
    qn_ = f(mla_q_norm)[0]
    g[:, 70] = qn_[0:128]
    g[0:64, 71] = qn_[128:192]
    g[0:64, 72] = qn_[128:192][p_mla]
    g[:, 73] = f(mla_k_nope_norm)[0]
    kr_ = f(mla_k_rope_norm)[0]
    g[0:64, 74] = kr_
    g[0:64, 75] = kr_[p_mla]
    gq_ = f(gqa_q_norm)[0]
    g[:, 76] = gq_
    g[:, 77] = gq_[p_gqa]
    gk_ = f(gqa_k_norm)[0]
    g[:, 78] = gk_
    g[:, 79] = gk_[p_gqa]
    sq_ = np.tile(f(swa_q_norm)[0], 2)
    g[:, 80] = sq_
    g[:, 81] = sq_[p_swa]
    sk_ = np.tile(f(swa_k_norm)[0], 2)
    g[:, 82] = sk_
    g[:, 83] = sk_[p_swa]
    w_in = f(even_w_in)[0]
    cols_kv = np.concatenate([np.arange(512, 768), np.arange(768, 832), np.arange(1856, 2112), np.arange(2112, 2368)])
    cols_q = np.concatenate([np.arange(0, 512), np.arange(832, 1856)])
    wkv = _slab(w_in, cols_kv)
    wq0 = _slab(w_in, cols_q)
    wuq = _slab(f(mla_w_uq)[0])
    ukv = f(mla_w_ukv)[0]
    ck = np.concatenate([np.arange(h * 256, h * 256 + 128) for h in range(8)])
    cv = np.concatenate([np.arange(h * 256 + 128, h * 256 + 256) for h in range(8)])
    wuk = _slab(ukv, ck)
    wuv = _slab(ukv, cv)
    wo0 = np.stack([_slab(f(even_w_out)[0][:, s * 512:(s + 1) * 512]) for s in range(4)])
    wo1 = np.stack([_slab(f(odd_w_out)[0][:, s * 512:(s + 1) * 512]) for s in range(4)])
    wqkv1 = np.stack([_slab(f(odd_w_qkv)[0][:, s * 512:(s + 1) * 512]) for s in range(5)])
    up = f(mlp_w_up)
    wup = np.stack([np.stack([_slab(up[l][:, s * 512:(s + 1) * 512]) for s in range(16)]) for l in range(2)])
    dn = f(mlp_w_down)
    wdn = np.stack([np.stack([np.stack([_slab(dn[l][hf * 4096:(hf + 1) * 4096, s * 256:(s + 1) * 256])
                                        for s in range(8)]) for hf in range(2)]) for l in range(2)])
    sinkrep = np.ascontiguousarray(np.broadcast_to(f(swa_sink)[0][None, :], (64, 32)))
    kk = np.arange(128)[:, None]
    qq = np.arange(128)[None, :]
    mP = np.tile((kk >= qq).astype(np.float32), (1, 4))
    mN = np.tile((kk <= qq).astype(np.float32), (1, 4))

    in_maps = []
    for c in range(8):
        b, ch = c // 4, c % 4
        ext = list(range(16 * ch - 1, 16 * ch + 17))
        if ch == 0:
            ext[0] = 17
        if ch == 3:
            ext[17] = 46
        rest = [i for i in range(64) if i not in ext]
        order = np.array(ext + rest)
        tok = (order[:, None] * 128 + np.arange(128)[None, :]).reshape(-1)
        xTc = np.ascontiguousarray(x[b][tok].T)
        cm, sm = _rope_tab(tok, 64, 500000.0)
        t_mla = np.stack([np.concatenate([cm, cm]), np.concatenate([-sm, sm])])
        cr, sr = _rope_tab(tok // 64, 64, 10000.0)
        cc, sc = _rope_tab(tok % 64, 64, 10000.0)
        t_gqa = np.stack([np.concatenate([cr, cr, cc, cc]), np.concatenate([-sr, sr, -sc, sc])])
        te = tok[:EXT]
        cs, ss = _rope_tab(te, 16, 500000.0)
        one = np.ones((48, EXT), np.float32)
        zero = np.zeros((48, EXT), np.float32)
        c64 = np.concatenate([cs, cs, one])
        s64 = np.concatenate([-ss, ss, zero])
        t_swa = np.stack([np.concatenate([c64, c64]), np.concatenate([s64, s64])])
        masks = np.stack([mP, mN, mP * (0.0 if ch == 0 else 1.0), mN * (0.0 if ch == 3 else 1.0)])
        in_maps.append({
            "xT": xTc, "gcols": g, "w_kv": wkv, "w_q0": wq0, "w_uq": wuq, "w_uk": wuk, "w_uv": wuv,
            "w_o0": wo0, "w_up": wup, "w_dn": wdn, "w_qkv1": wqkv1, "w_o1": wo1,
            "t_mla": np.ascontiguousarray(t_mla, np.float32), "t_gqa": np.ascontiguousarray(t_gqa, np.float32),
            "t_swa": np.ascontiguousarray(t_swa, np.float32), "perms": perms,
            "masks": np.ascontiguousarray(masks, np.float32), "sinkrep": sinkrep,
        })
    if "nc" not in _NC_CACHE:
        _NC_CACHE["nc"] = build_program()
    nc = _NC_CACHE["nc"]
    res = run_bass_kernel_spmd(nc, in_maps, core_ids=list(range(8)))
    out = np.empty((2, S, D), np.float32)
    for c in range(8):
        b, ch = c // 4, c % 4
        out[b, ch * 2048:(ch + 1) * 2048, :] = res.results[c]["yT"].T
    return out
```

```python
import numpy as np
from contextlib import ExitStack
import concourse.bass as bass
import concourse.mybir as mybir
from concourse.bass_utils import run_bass_kernel_spmd

F32 = mybir.dt.float32
BF16 = mybir.dt.bfloat16
AF = mybir.ActivationFunctionType
ALU = mybir.AluOpType

D = 2048
S = 8192
EXT = 2304
OWN = 2048
EPS = 1e-6
ENGS = ("pe", "act", "dve", "pool", "sp")
ENGATTR = {"pe": "tensor", "act": "scalar", "dve": "vector", "pool": "gpsimd", "sp": "sync"}


class Buf:
    __slots__ = ("name", "last_w", "readers", "ndma", "sem", "id")
    _n = 0

    def __init__(self, name="b"):
        self.name = name
        self.last_w = None
        self.readers = []
        self.ndma = 0
        self.sem = None
        Buf._n += 1
        self.id = Buf._n


class TB:
    __slots__ = ("t", "b")

    def __init__(self, t, name="b"):
        self.t = t
        self.b = Buf(name)


class Rot:
    def __init__(self, items):
        self.items = items
        self.i = 0

    def next(self):
        it = self.items[self.i % len(self.items)]
        self.i += 1
        return it


class Sems:
    def __init__(self, nc, es):
        self.nc = nc
        self.es = es
        self.eng = {e: [es.enter_context(nc.semaphore("prog_" + e)), 0] for e in ENGS}
        self.free = []
        self.n = 0

    def get_dma(self):
        if self.free:
            return self.free.pop()
        self.n += 1
        return [self.es.enter_context(self.nc.semaphore("dsem%d" % self.n)), 0]


class Op:
    __slots__ = ("eng", "fn", "waits", "signal", "dma_buf", "idx")

    def __init__(self, eng, fn):
        self.eng = eng
        self.fn = fn
        self.waits = []
        self.signal = False
        self.dma_buf = None
        self.idx = None


class Prog:
    def __init__(self, nc, sems):
        self.nc = nc
        self.sems = sems
        self.ops = {e: [] for e in ENGS}
        self.waited = {e: {} for e in ENGS}
        self.dma_bufs = []

    def _need(self, op, tok):
        if tok is None:
            return
        w = self.waited[op.eng]
        if tok[0] == "c":
            if tok[1] == op.eng and op.dma_buf is None:
                return
            key = tok[1]
            if w.get(key, -1) >= tok[2]:
                return
            w[key] = tok[2]
            op.waits.append(tok)
            self.ops[tok[1]][tok[2]].signal = True
        else:
            key = ("d", tok[1].id)
            if w.get(key, -1) >= tok[2]:
                return
            w[key] = tok[2]
            op.waits.append(tok)

    def op(self, eng, fn, reads=(), writes=(), dma=None):
        o = Op(eng, fn)
        o.idx = len(self.ops[eng])
        o.dma_buf = dma
        for b in reads:
            self._need(o, b.last_w)
        for b in writes:
            self._need(o, b.last_w)
            for t in b.readers:
                self._need(o, t)
        if dma is not None:
            if dma.ndma == 0 and dma.sem is None:
                self.dma_bufs.append(dma)
            dma.ndma += 1
            tok = ("d", dma, dma.ndma)
        else:
            tok = ("c", eng, o.idx)
        for b in reads:
            b.readers.append(tok)
        for b in writes:
            b.last_w = tok
            b.readers = []
        self.ops[eng].append(o)
        return tok

    def emit(self):
        nc = self.nc
        sems = self.sems
        for b in self.dma_bufs:
            b.sem = sems.get_dma()
        for e in ENGS:
            for o in reversed(self.ops[e]):
                if o.dma_buf is None:
                    o.signal = True
                    break
        sigval = {}
        final = {}
        for e in ENGS:
            c = sems.eng[e][1]
            for o in self.ops[e]:
                if o.signal and o.dma_buf is None:
                    c += 1
                    sigval[(e, o.idx)] = c
            final[e] = c

        def lower(tok):
            if tok[0] == "c":
                return sems.eng[tok[1]][0], sigval[(tok[1], tok[2])]
            b = tok[1]
            return b.sem[0], 16 * (b.sem[1] + tok[2])

        sp_final = final["sp"] + 1
        with nc.Block() as block:
            def make(e):
                def body(eng):
                    for o in self.ops[e]:
                        for t in o.waits:
                            s, v = lower(t)
                            eng.wait_ge(s, v)
                        ins = o.fn(eng)
                        if o.dma_buf is not None:
                            ins.then_inc(o.dma_buf.sem[0], 16)
                        elif o.signal:
                            ins.then_inc(sems.eng[e][0], 1)
                    if e == "sp":
                        for e2 in ENGS:
                            if e2 != "sp" and final[e2] > 0:
                                eng.wait_ge(sems.eng[e2][0], final[e2])
                        for b in self.dma_bufs:
                            eng.wait_ge(b.sem[0], 16 * (b.sem[1] + b.ndma))
                        eng.sem_inc(sems.eng["sp"][0], 1)
                    else:
                        eng.wait_ge(sems.eng["sp"][0], sp_final)
                return body
            for e in ENGS:
                getattr(block, ENGATTR[e])(make(e))
        for e in ENGS:
            sems.eng[e][1] = final[e]
        sems.eng["sp"][1] = sp_final
        for b in self.dma_bufs:
            b.sem[1] += b.ndma
            sems.free.append(b.sem)
            b.sem = None


class Ctx:
    def __init__(self, nc, sems, tag):
        self.nc = nc
        self.es = ExitStack()
        self.P = Prog(nc, sems)
        self.tag = tag
        self.k = 0

    def sb(self, shape, dt, name="t"):
        self.k += 1
        t = self.es.enter_context(self.nc.sbuf_tensor("%s_%s%d" % (self.tag, name, self.k), list(shape), dt))
        return TB(t, name)

    def ps(self, shape, name="p"):
        self.k += 1
        t = self.es.enter_context(self.nc.psum_tensor("%s_%s%d" % (self.tag, name, self.k), list(shape), F32))
        return TB(t, name)

    def close(self):
        self.P.emit()
        self.es.close()


def dma_in(P, q, dst, src_ap, dst_ap=None):
    d = dst.t[:] if dst_ap is None else dst_ap
    P.op(q, lambda e, d=d, s=src_ap: e.dma_start(out=d, in_=s), writes=[dst.b], dma=dst.b)


def dma_out(P, src, dst_ap, src_ap, dram_bufs=(), q="pool"):
    return P.op(q, lambda e, d=dst_ap, s=src_ap: e.dma_start(out=d, in_=s),
                reads=[src.b], writes=list(dram_bufs), dma=src.b)


def mm_group(P, out_ap, out_b, pairs, reads):
    n = len(pairs)

    def fn(e, pairs=pairs, out_ap=out_ap, n=n):
        ins = None
        for i, (l, r) in enumerate(pairs):
            ins = e.matmul(out_ap, lhsT=l, rhs=r, start=(i == 0), stop=(i == n - 1))
        return ins
    P.op("pe", fn, reads=reads, writes=[out_b])


class Common:
    def __init__(self, C, gcols_d, with_bd=False, nf32=6):
        P = C.P
        self.C = C
        self.ones = C.sb([128, 128], BF16, "ones")
        P.op("dve", lambda e: e.memset(self.ones.t[:], 1.0), writes=[self.ones.b])
        self.eps = C.sb([128, 1], F32, "eps")
        P.op("dve", lambda e: e.memset(self.eps.t[:], EPS), writes=[self.eps.b])
        self.g = C.sb([128, 84], F32, "g")
        dma_in(P, "sp", self.g, gcols_d)
        self.sq = Rot([C.sb([128, 512], BF16, "sq") for _ in range(3)])
        self.f32 = Rot([C.sb([128, 512], F32, "f") for _ in range(nf32)])
        self.rstd_mode = "dve"
        self.rawb = Rot([C.sb([128, 512], BF16, "rawb") for _ in range(2)])
        if with_bd:
            self.bd = C.sb([128, 128], BF16, "bd")
            P.op("dve", lambda e: e.memset(self.bd.t[:], 0.0), writes=[self.bd.b])
            P.op("dve", lambda e: e.memset(self.bd.t[0:64, 0:64], 1.0), writes=[self.bd.b])
            P.op("dve", lambda e: e.memset(self.bd.t[64:128, 64:128], 1.0), writes=[self.bd.b])


def sq_accum(P, K, srcs, ssqs, n, mrows=128, lhs=None):
    ssq = ssqs.next()
    lhs = K.ones if lhs is None else lhs
    m = len(srcs)
    for i, (ap, b, rows) in enumerate(srcs):
        sq = K.sq.next()
        P.op("act", lambda e, o=sq.t[0:rows, 0:n], a=ap: e.activation(out=o, in_=a, func=AF.Square),
             reads=[b], writes=[sq.b])
        P.op("pe", lambda e, o=ssq.t[0:mrows, 0:n], l=lhs.t[0:rows, 0:mrows], r=sq.t[0:rows, 0:n], i=i:
             e.matmul(o, lhsT=l, rhs=r, start=(i == 0), stop=(i == m - 1)),
             reads=[sq.b, lhs.b], writes=[ssq.b])
    return ssq


def rstd_a(P, K, ssq, d, n, rows=128):
    v = K.f32.next()
    if K.rstd_mode == "dve":
        P.op("act", lambda e, o=v.t[0:rows, 0:n], a=ssq.t[0:rows, 0:n]:
             e.activation(out=o, in_=a, func=AF.Sqrt, bias=K.eps.t[0:rows, 0:1], scale=1.0 / d),
             reads=[ssq.b, K.eps.b], writes=[v.b])
    else:
        P.op("dve", lambda e: e.tensor_scalar(out=v.t[0:rows, 0:n], in0=ssq.t[0:rows, 0:n], scalar1=1.0 / d,
                                              scalar2=EPS, op0=ALU.mult, op1=ALU.add), reads=[ssq.b], writes=[v.b])
    return v


def rstd_b(P, K, v, n, rows=128):
    if K.rstd_mode == "dve":
        P.op("dve", lambda e, o=v.t[0:rows, 0:n]: e.reciprocal(out=o, in_=o), reads=[v.b], writes=[v.b])
        return v
    t = K.f32.next()
    P.op("act", lambda e: e.activation(out=t.t[0:rows, 0:n], in_=v.t[0:rows, 0:n], func=AF.Ln),
         reads=[v.b], writes=[t.b])
    r = K.f32.next()
    P.op("act", lambda e: e.activation(out=r.t[0:rows, 0:n], in_=t.t[0:rows, 0:n], func=AF.Exp, scale=-0.5),
         reads=[t.b], writes=[r.b])
    return r


def rstd_of(P, K, ssq, d, n, rows=128):
    return rstd_b(P, K, rstd_a(P, K, ssq, d, n, rows), n, rows)


def apply_plain(P, K, src_ap, src_b, gcol, r, out_ap, out_b, n, rows=128, eng="dve"):
    P.op(eng, lambda e: e.scalar_tensor_tensor(out=out_ap, in0=src_ap, scalar=K.g.t[0:rows, gcol:gcol + 1],
                                               in1=r.t[0:rows, 0:n], op0=ALU.mult, op1=ALU.mult),
         reads=[src_b, K.g.b, r.b], writes=[out_b])


def apply_rope(P, K, src, n, rows, perm, gcol, Ctab, Stab, r, out_ap, out_b, psw, eng2="pool"):
    raw = K.rawb.next()
    P.op("act", lambda e: e.activation(out=raw.t[0:rows, 0:n], in_=src.t[0:rows, 0:n], func=AF.Copy),
         reads=[src.b], writes=[raw.b])
    P.op("pe", lambda e: e.matmul(psw.t[0:rows, 0:n], lhsT=perm.t[0:rows, 0:rows], rhs=raw.t[0:rows, 0:n],
                                  start=True, stop=True), reads=[raw.b, perm.b], writes=[psw.b])
    t1 = K.f32.next()
    P.op("dve", lambda e: e.scalar_tensor_tensor(out=t1.t[0:rows, 0:n], in0=src.t[0:rows, 0:n],
                                                 scalar=K.g.t[0:rows, gcol:gcol + 1], in1=Ctab[0],
                                                 op0=ALU.mult, op1=ALU.mult),
         reads=[src.b, K.g.b, Ctab[1]], writes=[t1.b])
    t2 = K.f32.next()
    P.op("dve", lambda e: e.scalar_tensor_tensor(out=t2.t[0:rows, 0:n], in0=psw.t[0:rows, 0:n],
                                                 scalar=K.g.t[0:rows, gcol + 1:gcol + 2], in1=Stab[0],
                                                 op0=ALU.mult, op1=ALU.mult),
         reads=[psw.b, K.g.b, Stab[1]], writes=[t2.b])
    P.op(eng2, lambda e: e.tensor_tensor(out=t1.t[0:rows, 0:n], in0=t1.t[0:rows, 0:n], in1=t2.t[0:rows, 0:n],
                                         op=ALU.add), reads=[t1.b, t2.b], writes=[t1.b])
    P.op(eng2, lambda e: e.tensor_tensor(out=out_ap, in0=t1.t[0:rows, 0:n], in1=r.t[0:rows, 0:n],
                                         op=ALU.mult), reads=[t1.b, r.b], writes=[out_b])


def xnorm(P, K, xt, hT, gbase, n, ssq):
    ssq_ = sq_accum(P, K, [(xt.t[:, k, 0:n], xt.b, 128) for k in range(16)], ssq, n)
    r = rstd_of(P, K, ssq_, float(D), n)
    for k in range(16):
        apply_plain(P, K, xt.t[:, k, 0:n], xt.b, gbase + k, r, hT.t[:, k, 0:n], hT.b, n,
                    eng="dve")


def build_program():
    nc = bass.Bass("TRN2", target_bir_lowering=False)

    def din(name, shape):
        return nc.dram_tensor(name, list(shape), F32, kind="ExternalInput").ap()

    def dscr(name, shape, dt):
        return nc.dram_tensor(name, list(shape), dt, kind="Internal").ap()

    xT = din("xT", [D, S])
    gcols = din("gcols", [128, 84])
    w_kv = din("w_kv", [128, 16 * 832])
    w_q0 = din("w_q0", [128, 16 * 1536])
    w_uq = din("w_uq", [128, 4 * 1536])
    w_uk = din("w_uk", [128, 2 * 1024])
    w_uv = din("w_uv", [128, 2 * 1024])
    w_o0 = din("w_o0", [4, 128, 8192])
    w_up = din("w_up", [2, 16, 128, 8192])
    w_dn = din("w_dn", [2, 2, 8, 128, 8192])
    w_qkv1 = din("w_qkv1", [5, 128, 8192])
    w_o1 = din("w_o1", [4, 128, 8192])
    t_mla = din("t_mla", [2, 64, S])
    t_gqa = din("t_gqa", [2, 128, S])
    t_swa = din("t_swa", [2, 128, EXT])
    perms = din("perms", [3, 128, 128])
    masks = din("masks", [4, 128, 512])
    sinkrep = din("sinkrep", [64, 32])
    yT = nc.dram_tensor("yT", [D, OWN], F32, kind="ExternalOutput").ap()

    kn_d = dscr("kn_d", [8, 128, S], BF16)
    kr_d = dscr("kr_d", [64, S], BF16)
    kg_d = dscr("kg_d", [2, 128, S], BF16)
    va_d = dscr("va_d", [8, 128, 64, 128], BF16)
    vg_d = dscr("vg_d", [2, 128, 64, 128], BF16)
    qn_d = dscr("qn_d", [8, 128, EXT], BF16)
    qr_d = dscr("qr_d", [8, 64, EXT], BF16)
    qg_d = dscr("qg_d", [8, 128, EXT], BF16)
    mg_d = dscr("mg_d", [D, EXT], BF16)
    x1_d = dscr("x1_d", [D, EXT], F32)
    q1_d = dscr("q1_d", [4, 64, 18, 8, 128], BF16)
    k1_d = dscr("k1_d", [4, 64, EXT], BF16)
    v1_d = dscr("v1_d", [128, 18, 256], BF16)
    m1_d = dscr("m1_d", [D, OWN], BF16)
    h0_d = dscr("h0_d", [D, EXT], BF16)
    wb_o0 = dscr("wb_o0", [4, 128, 8192], BF16)
    wb_up = dscr("wb_up", [2, 16, 128, 8192], BF16)
    wb_dn = dscr("wb_dn", [2, 2, 8, 128, 8192], BF16)
    wb_qkv1 = dscr("wb_qkv1", [5, 128, 8192], BF16)
    wb_o1 = dscr("wb_o1", [4, 128, 8192], BF16)

    es = ExitStack()
    sems = Sems(nc, es)
    xT_v = xT.rearrange("(k p) t -> p k t", p=128)

    C = Ctx(nc, sems, "p0")
    P = C.P
    K = Common(C, gcols, nf32=10)
    K.rstd_mode = "act"
    wkv = C.sb([128, 16, 832], BF16, "wkv")
    dma_in(P, "pool", wkv, w_kv.rearrange("p (k c) -> p k c", k=16))
    wuk = C.sb([128, 2, 1024], BF16, "wuk")
    dma_in(P, "pool", wuk, w_uk.rearrange("p (k c) -> p k c", k=2))
    wuv = C.sb([128, 2, 1024], BF16, "wuv")
    dma_in(P, "pool", wuv, w_uv.rearrange("p (k c) -> p k c", k=2))
    pm = C.sb([128, 3, 128], BF16, "pm")
    dma_in(P, "pool", pm, perms.rearrange("a p m -> p a m"))
    pm_mla = TB(pm.t[:, 0, :]); pm_mla.b = pm.b
    pm_gqa = TB(pm.t[:, 1, :]); pm_gqa.b = pm.b
    xts = Rot([C.sb([128, 16, 512], F32, "xt") for _ in range(2)])
    hT = C.sb([128, 16, 512], BF16, "hT")
    tabs = Rot([(C.sb([64, 2, 512], F32, "tm"), C.sb([128, 2, 512], F32, "tg")) for _ in range(2)])
    ckvn = C.sb([128, 2, 512], BF16, "ckvn")
    kst = Rot([(C.sb([128, 8, 512], BF16, "knst"), C.sb([64, 512], BF16, "krst"),
                C.sb([128, 2, 512], BF16, "kgst"), C.sb([128, 4, 1024], BF16, "vast"),
                C.sb([128, 4, 256], BF16, "vgst")) for _ in range(2)])
    ssq = Rot([C.ps([128, 512], "ssq") for _ in range(2)])
    prot = Rot([C.ps([128, 512], "pr") for _ in range(6)])
    for t in range(16):
        c0 = t * 512
        xt = xts.next()
        dma_in(P, "sp", xt, xT_v[:, :, c0:c0 + 512])
        tm, tg = tabs.next()
        dma_in(P, "sp", tm, t_mla[:, :, c0:c0 + 512].rearrange("a p t -> p a t"))
        dma_in(P, "sp", tg, t_gqa[:, :, c0:c0 + 512].rearrange("a p t -> p a t"))
        xnorm(P, K, xt, hT, 0, 512, ssq)
        if c0 < EXT:
            n_ = min(512, EXT - c0)
            dma_out(P, hT, h0_d.rearrange("(k p) t -> p k t", p=128)[:, :, c0:c0 + n_], hT.t[:, :, 0:n_])
        knst, krst, kgst, vast, vgst = kst.next()
        pck = [prot.next(), prot.next()]
        for j in range(2):
            mm_group(P, pck[j].t[:, :], pck[j].b,
                     [(wkv.t[:, kc, j * 128:(j + 1) * 128], hT.t[:, kc, :]) for kc in range(16)], [wkv.b, hT.b])
        ssq_ = sq_accum(P, K, [(pck[j].t[:, :], pck[j].b, 128) for j in range(2)], ssq, 512)
        r = rstd_of(P, K, ssq_, 256.0, 512)
        for j in range(2):
            apply_plain(P, K, pck[j].t[:, :], pck[j].b, 68 + j, r, ckvn.t[:, j, :], ckvn.b, 512)
        def kn_a(h):
            pk = prot.next()
            mm_group(P, pk.t[:, :], pk.b,
                     [(wuk.t[:, kc, h * 128:(h + 1) * 128], ckvn.t[:, kc, :]) for kc in range(2)], [wuk.b, ckvn.b])
            ssq_ = sq_accum(P, K, [(pk.t[:, :], pk.b, 128)], ssq, 512)
            return pk, rstd_a(P, K, ssq_, 128.0, 512)

        def kn_b(h, pk, v):
            r = rstd_b(P, K, v, 512)
            apply_plain(P, K, pk.t[:, :], pk.b, 73, r, knst.t[:, h, :], knst.b, 512)
        cur = kn_a(0)
        for h in range(8):
            nxt = kn_a(h + 1) if h + 1 < 8 else None
            kn_b(h, *cur)
            cur = nxt
        for s in range(4):
            for hf in range(2):
                pv = prot.next()
                mm_group(P, pv.t[:, :], pv.b,
                         [(ckvn.t[:, kc, s * 128:(s + 1) * 128], wuv.t[:, kc, hf * 512:(hf + 1) * 512]) for kc in range(2)],
                         [wuv.b, ckvn.b])
                eng = "act" if (s * 2 + hf) % 2 == 0 else "dve"
                if eng == "act":
                    P.op("act", lambda e, o=vast.t[:, s, hf * 512:(hf + 1) * 512], a=pv.t[:, :]:
                         e.activation(out=o, in_=a, func=AF.Copy), reads=[pv.b], writes=[vast.b])
                else:
                    P.op("dve", lambda e, o=vast.t[:, s, hf * 512:(hf + 1) * 512], a=pv.t[:, :]:
                         e.tensor_copy(out=o, in_=a), reads=[pv.b], writes=[vast.b])
        pk = prot.next()
        mm_group(P, pk.t[0:64, :], pk.b, [(wkv.t[:, kc, 256:320], hT.t[:, kc, :]) for kc in range(16)], [wkv.b, hT.b])
        ssq_ = sq_accum(P, K, [(pk.t[0:64, :], pk.b, 64)], ssq, 512, mrows=64)
        r = rstd_of(P, K, ssq_, 64.0, 512, rows=64)
        psw = prot.next()
        apply_rope(P, K, pk, 512, 64, pm_mla, 74, (tm.t[:, 0, :], tm.b), (tm.t[:, 1, :], tm.b), r,
                   krst.t[:, :], krst.b, psw)
        for j in range(2):
            pk = prot.next()
            mm_group(P, pk.t[:, :], pk.b,
                     [(wkv.t[:, kc, 320 + j * 128:320 + (j + 1) * 128], hT.t[:, kc, :]) for kc in range(16)],
                     [wkv.b, hT.b])
            ssq_ = sq_accum(P, K, [(pk.t[:, :], pk.b, 128)], ssq, 512)
            r = rstd_of(P, K, ssq_, 128.0, 512)
            psw = prot.next()
            apply_rope(P, K, pk, 512, 128, pm_gqa, 78, (tg.t[:, 0, :], tg.b), (tg.t[:, 1, :], tg.b), r,
                       kgst.t[:, j, :], kgst.b, psw)
        for s in range(4):
            pv = prot.next()
            mm_group(P, pv.t[:, 0:256], pv.b,
                     [(hT.t[:, kc, s * 128:(s + 1) * 128], wkv.t[:, kc, 576:832]) for kc in range(16)], [wkv.b, hT.b])
            P.op("act", lambda e, o=vgst.t[:, s, :], a=pv.t[:, 0:256]: e.activation(out=o, in_=a, func=AF.Copy),
                 reads=[pv.b], writes=[vgst.b])
        dma_out(P, knst, kn_d[:, :, c0:c0 + 512].rearrange("h p t -> p h t"), knst.t[:])
        dma_out(P, krst, kr_d[:, c0:c0 + 512], krst.t[:])
        dma_out(P, kgst, kg_d[:, :, c0:c0 + 512].rearrange("h p t -> p h t"), kgst.t[:])
        for s in range(4):
            dma_out(P, vast, va_d[:, :, 4 * t + s, :].rearrange("h p d -> p h d"),
                    vast.t[:, s, :].rearrange("p (h d) -> p h d", h=8))
        for s in range(4):
            dma_out(P, vgst, vg_d[:, :, 4 * t + s, :].rearrange("h p d -> p h d"),
                    vgst.t[:, s, :].rearrange("p (h d) -> p h d", h=2))
    C.close()

    C = Ctx(nc, sems, "p1")
    P = C.P
    K = Common(C, gcols)
    wq0v = w_q0.rearrange("p (k c) -> p k c", k=16)
    wqa = C.sb([128, 16, 512], BF16, "wqa")
    dma_in(P, "pool", wqa, wq0v[:, :, 0:512])
    wuq = C.sb([128, 4, 1536], BF16, "wuq")
    dma_in(P, "pool", wuq, w_uq.rearrange("p (k c) -> p k c", k=4))
    wqb = C.sb([128, 16, 1024], BF16, "wqb")
    dma_in(P, "pool", wqb, wq0v[:, :, 512:1536])
    pm = C.sb([128, 3, 128], BF16, "pm")
    dma_in(P, "pool", pm, perms.rearrange("a p m -> p a m"))
    pm_mla = TB(pm.t[:, 0, :]); pm_mla.b = pm.b
    pm_gqa = TB(pm.t[:, 1, :]); pm_gqa.b = pm.b
    hTs = Rot([C.sb([128, 16, 512], BF16, "hT") for _ in range(2)])
    tabs = Rot([(C.sb([64, 2, 512], F32, "tm"), C.sb([128, 2, 512], F32, "tg")) for _ in range(2)])
    cqn = C.sb([128, 4, 512], BF16, "cqn")
    qst = Rot([(C.sb([128, 8, 512], BF16, "qnst"), C.sb([64, 8, 512], BF16, "qrst"),
                C.sb([128, 8, 512], BF16, "qgst")) for _ in range(1)])
    ssq = Rot([C.ps([128, 512], "ssq") for _ in range(2)])
    prot = Rot([C.ps([128, 512], "pr") for _ in range(6)])
    for t in range(5):
        c0 = t * 512
        n = min(512, EXT - c0)
        hT = hTs.next()
        dma_in(P, "sp", hT, h0_d.rearrange("(k p) t -> p k t", p=128)[:, :, c0:c0 + n], hT.t[:, :, 0:n])
        tm, tg = tabs.next()
        dma_in(P, "sp", tm, t_mla[:, :, c0:c0 + n].rearrange("a p t -> p a t"), tm.t[:, :, 0:n])
        dma_in(P, "sp", tg, t_gqa[:, :, c0:c0 + n].rearrange("a p t -> p a t"), tg.t[:, :, 0:n])
        qnst, qrst, qgst = qst.next()
        pcq = [prot.next() for _ in range(4)]
        for j in range(4):
            mm_group(P, pcq[j].t[:, 0:n], pcq[j].b,
                     [(wqa.t[:, kc, j * 128:(j + 1) * 128], hT.t[:, kc, 0:n]) for kc in range(16)], [wqa.b, hT.b])
        ssq_ = sq_accum(P, K, [(pcq[j].t[:, 0:n], pcq[j].b, 128) for j in range(4)], ssq, n)
        r = rstd_of(P, K, ssq_, 512.0, n)
        for j in range(4):
            apply_plain(P, K, pcq[j].t[:, 0:n], pcq[j].b, 64 + j, r, cqn.t[:, j, 0:n], cqn.b, n)
        for h in range(8):
            pn = prot.next()
            mm_group(P, pn.t[:, 0:n], pn.b,
                     [(wuq.t[:, kc, h * 192:h * 192 + 128], cqn.t[:, kc, 0:n]) for kc in range(4)], [wuq.b, cqn.b])
            pr = prot.next()
            mm_group(P, pr.t[0:64, 0:n], pr.b,
                     [(wuq.t[:, kc, h * 192 + 128:h * 192 + 192], cqn.t[:, kc, 0:n]) for kc in range(4)], [wuq.b, cqn.b])
            ssq_ = sq_accum(P, K, [(pn.t[:, 0:n], pn.b, 128), (pr.t[0:64, 0:n], pr.b, 64)], ssq, n)
            r = rstd_of(P, K, ssq_, 192.0, n)
            apply_plain(P, K, pn.t[:, 0:n], pn.b, 70, r, qnst.t[:, h, 0:n], qnst.b, n)
            psw = prot.next()
            apply_rope(P, K, pr, n, 64, pm_mla, 71, (tm.t[:, 0, 0:n], tm.b), (tm.t[:, 1, 0:n], tm.b), r,
                       qrst.t[:, h, 0:n], qrst.b, psw)
        for h in range(8):
            pg = prot.next()
            mm_group(P, pg.t[:, 0:n], pg.b,
                     [(wqb.t[:, kc, h * 128:(h + 1) * 128], hT.t[:, kc, 0:n]) for kc in range(16)],
                     [wqb.b, hT.b])
            ssq_ = sq_accum(P, K, [(pg.t[:, 0:n], pg.b, 128)], ssq, n)
            r = rstd_of(P, K, ssq_, 128.0, n)
            psw = prot.next()
            apply_rope(P, K, pg, n, 128, pm_gqa, 76, (tg.t[:, 0, 0:n], tg.b), (tg.t[:, 1, 0:n], tg.b), r,
                       qgst.t[:, h, 0:n], qgst.b, psw)
        dma_out(P, qnst, qn_d[:, :, c0:c0 + n].rearrange("h p t -> p h t"), qnst.t[:, :, 0:n])
        dma_out(P, qrst, qr_d[:, :, c0:c0 + n].rearrange("h p t -> p h t"), qrst.t[:, :, 0:n])
        dma_out(P, qgst, qg_d[:, :, c0:c0 + n].rearrange("h p t -> p h t"), qgst.t[:, :, 0:n])
    C.close()

    C = Ctx(nc, sems, "p2")
    P = C.P
    ones = C.sb([128, 128], BF16, "ones")
    P.op("dve", lambda e: e.memset(ones.t[:], 1.0), writes=[ones.b])
    krs = C.sb([128, S], BF16, "krs")
    P.op("dve", lambda e: e.memset(krs.t[64:128, :], 0.0), writes=[krs.b])
    dma_in(P, "sp", krs, kr_d, krs.t[0:64, :])
    cvb = Buf("cv")
    cvl = [(wb_o0[i], w_o0[i]) for i in range(4)]
    cvl += [(wb_up[0, i], w_up[0, i]) for i in range(16)]
    cvl += [(wb_dn[0, hf, i], w_dn[0, hf, i]) for hf in range(2) for i in range(8)]
    cvl += [(wb_qkv1[i], w_qkv1[i]) for i in range(5)]
    cvl += [(wb_o1[i], w_o1[i]) for i in range(4)]
    cvl += [(wb_up[1, i], w_up[1, i]) for i in range(16)]
    cvl += [(wb_dn[1, hf, i], w_dn[1, hf, i]) for hf in range(2) for i in range(8)]
    for (dst_, src_) in cvl:
        P.op("pool", lambda e, d=dst_, s_=src_: e.dma_start(out=d, in_=s_), writes=[cvb], dma=cvb)
    kts = Rot([C.sb([128, S], BF16, "kt") for _ in range(2)])
    vts = Rot([C.sb([128, 64, 128], BF16, "vt") for _ in range(2)])
    qns = Rot([C.sb([128, EXT], BF16, "qn") for _ in range(2)])
    qrs = Rot([C.sb([128, EXT], BF16, "qr") for _ in range(2)])
    for _q in qrs.items:
        P.op("dve", lambda e, _q=_q: e.memset(_q.t[64:128, :], 0.0), writes=[_q.b])
    pts = Rot([C.sb([128, 1024], BF16, "pt") for _ in range(4)])
    ones32 = C.sb([128, 128], F32, "ones32")
    P.op("dve", lambda e: e.memset(ones32.t[:], 1.0), writes=[ones32.b])
    accAs = Rot([C.sb([128, 2, 512], F32, "accA") for _ in range(2)])
    accBs = Rot([C.sb([128, 2, 512], F32, "accB") for _ in range(1)])
    hls = Rot([C.sb([128, 2, 512], BF16, "hl") for _ in range(2)])
    rds = Rot([C.sb([128, 512], F32, "rd") for _ in range(2)])
    ost = Rot([C.sb([128, 512], BF16, "ost") for _ in range(2)])
    sps = Rot([C.ps([128, 1024], "S") for _ in range(2)])
    ops_ = Rot([C.ps([128, 512], "O") for _ in range(2)])
    dps = Rot([C.ps([128, 512], "Dn") for _ in range(2)])
    qtiles = [(0, 512), (512, 512), (1024, 512), (1536, 512), (2048, 256)]
    kt = vt = None
    pend = []

    def flush_pend():
        while pend:
            o_, dst_, n_ = pend.pop(0)
            dma_out(P, o_, dst_, o_.t[:, 0:n_], q="act")
    for job in range(16):
        mla = job < 8
        if mla:
            kt = kts.next()
            dma_in(P, "sp", kt, kn_d[job])
            qn = qns.next()
            dma_in(P, "sp", qn, qn_d[job])
            qr = qrs.next()
            dma_in(P, "sp", qr, qr_d[job], qr.t[0:64, :])
            vt = vts.next()
            dma_in(P, "sp", vt, va_d[job])
            scale = 192.0 ** -0.5
        else:
            j = job - 8
            if j % 4 == 0:
                kt = kts.next()
                dma_in(P, "sp", kt, kg_d[j // 4])
                vt = vts.next()
                dma_in(P, "sp", vt, vg_d[j // 4])
            qn = qns.next()
            dma_in(P, "sp", qn, qg_d[j])
            scale = 128.0 ** -0.5
        for (q0, n) in qtiles:
            Ops = ops_.next()
            Dps = dps.next()

            def s_pair(kp, mla=mla, kt=kt, qn=qn, qr=(qr if mla else None), q0=q0, n=n):
                Sp = sps.next()

                def fn(e, Sp=Sp):
                    ins = None
                    for i in range(2):
                        kb = 2 * kp + i
                        o = Sp.t[:, i * n:(i + 1) * n] if n == 256 else Sp.t[:, i * 512:(i + 1) * 512]
                        ins = e.matmul(o, lhsT=kt.t[:, kb * 128:(kb + 1) * 128], rhs=qn.t[:, q0:q0 + n],
                                       start=True, stop=(not mla))
                        if mla:
                            ins = e.matmul(o, lhsT=krs.t[:, kb * 128:(kb + 1) * 128], rhs=qr.t[:, q0:q0 + n],
                                           start=False, stop=True)
                    return ins
                rd = [kt.b, qn.b] + ([krs.b, qr.b] if mla else [])
                P.op("pe", fn, reads=rd, writes=[Sp.b])
                return Sp

            def pv_pair(kp, Sp, vt=vt, n=n, Ops=Ops, Dps=Dps, scale=scale):
                pt = pts.next()
                w = 2 * n
                P.op("act", lambda e: e.activation(out=pt.t[:, 0:w], in_=Sp.t[:, 0:w], func=AF.Exp, scale=scale),
                     reads=[Sp.b], writes=[pt.b])

                def fn(e):
                    ins = None
                    for i in range(2):
                        kb = 2 * kp + i
                        first = (kb == 0)
                        last = (kb == 63)
                        e.matmul(Ops.t[:, 0:n], lhsT=vt.t[:, kb, :], rhs=pt.t[:, i * n:(i + 1) * n],
                                 start=first, stop=last)
                        ins = e.matmul(Dps.t[:, 0:n], lhsT=ones.t[:, :], rhs=pt.t[:, i * n:(i + 1) * n],
                                       start=first, stop=last)
                    return ins
                P.op("pe", fn, reads=[vt.b, pt.b, ones.b], writes=[Ops.b, Dps.b])

            prev = s_pair(0)
            for kp in range(32):
                nxt = s_pair(kp + 1) if kp + 1 < 32 else None
                pv_pair(kp, prev)
                prev = nxt
                if kp == 4:
                    flush_pend()
            rd = rds.next()
            P.op("dve", lambda e, rd=rd, Dps=Dps, n=n: e.reciprocal(out=rd.t[:, 0:n], in_=Dps.t[:, 0:n]),
                 reads=[Dps.b], writes=[rd.b])
            o = ost.next()
            P.op("dve", lambda e, o=o, rd=rd, Ops=Ops, n=n: e.tensor_tensor(out=o.t[:, 0:n], in0=Ops.t[:, 0:n],
                                                                           in1=rd.t[:, 0:n], op=ALU.mult),
                 reads=[Ops.b, rd.b], writes=[o.b])
            pend.append((o, mg_d[job * 128:(job + 1) * 128, q0:q0 + n], n))
    flush_pend()
    C.close()

    def tail_phase(tag, layer):
        C = Ctx(nc, sems, tag)
        P = C.P
        K = Common(C, gcols, with_bd=(layer == 0))
        slabs = Rot([C.sb([128, 8192], BF16, "slab") for _ in range(4)])
        xt = C.sb([128, 16, 512], F32, "xt")
        hT = C.sb([128, 16, 512], BF16, "hT")
        hid = C.sb([128, 32, 512], BF16, "hid")
        rls = Rot([C.sb([128, 512], F32, "rl") for _ in range(3)])
        ssq = Rot([C.ps([128, 512], "ssq") for _ in range(2)])
        prot = Rot([C.ps([128, 512], "pr") for _ in range(6)])
        if layer == 0:
            pm = C.sb([128, 128], BF16, "pm")
            dma_in(P, "pool", pm, perms[2])
            tsw = Rot([C.sb([128, 2, 512], F32, "tsw") for _ in range(2)])
            qst = Rot([C.sb([128, 512], BF16, "qst") for _ in range(3)])
            vst = Rot([C.sb([128, 4, 256], BF16, "vst") for _ in range(1)])
            ntiles = 5
            wo, mgsrc, xsrc = wb_o0, mg_d, None
        else:
            ntiles = 4
            wo, mgsrc = wb_o1, m1_d
        outtoks = []

        def load_slab(src_ap):
            sl = slabs.next()
            dma_in(P, "sp", sl, src_ap)
            return sl

        mg_v = mgsrc.rearrange("(k p) t -> p k t", p=128)
        for t in range(ntiles):
            if layer == 0:
                c0 = t * 512
                n = min(512, EXT - c0)
                dma_in(P, "pool", hT, mg_v[:, :, c0:c0 + n], hT.t[:, :, 0:n])
                if t == 0:
                    dma_in(P, "pool", xt, xT_v[:, :, c0:c0 + n], xt.t[:, :, 0:n])
            else:
                c0 = t * 512
                n = 512
                if t == 0:
                    dma_in(P, "pool", hT, mg_v[:, :, c0:c0 + n])
                dma_in(P, "pool", xt, x1_d.rearrange("(k p) t -> p k t", p=128)[:, :, 128 + c0:128 + c0 + n])
            for s in range(4):
                sl = load_slab(wo[s])
                slv = sl.t[:].rearrange("p (k c) -> p k c", k=16)
                for j in range(4):
                    c = 4 * s + j
                    pp = prot.next()
                    mm_group(P, pp.t[:, 0:n], pp.b,
                             [(slv[:, kc, j * 128:(j + 1) * 128], hT.t[:, kc, 0:n]) for kc in range(16)], [sl.b, hT.b])
                    P.op("dve", lambda e, c=c, pp=pp, n=n: e.tensor_tensor(out=xt.t[:, c, 0:n], in0=pp.t[:, 0:n],
                                                                           in1=xt.t[:, c, 0:n], op=ALU.add),
                         reads=[pp.b, xt.b], writes=[xt.b])
            xnorm(P, K, xt, hT, 16 + 32 * layer, n, ssq)
            for hf in range(2):
                for s in range(8):
                    sl = load_slab(wb_up[layer, hf * 8 + s])
                    slv = sl.t[:].rearrange("p (k c) -> p k c", k=16)
                    for j in range(4):
                        hc = s * 4 + j
                        pp = prot.next()
                        mm_group(P, pp.t[:, 0:n], pp.b,
                                 [(slv[:, kc, j * 128:(j + 1) * 128], hT.t[:, kc, 0:n]) for kc in range(16)],
                                 [sl.b, hT.b])
                        rl = rls.next()
                        P.op("act", lambda e, rl=rl, pp=pp, n=n: e.activation(out=rl.t[:, 0:n], in_=pp.t[:, 0:n],
                                                                              func=AF.Relu),
                             reads=[pp.b], writes=[rl.b])
                        P.op("dve", lambda e, rl=rl, hc=hc, n=n: e.tensor_tensor(out=hid.t[:, hc, 0:n], in0=rl.t[:, 0:n],
                                                                              in1=rl.t[:, 0:n], op=ALU.mult),
                             reads=[rl.b], writes=[hid.b])
                if layer == 1 and hf == 1 and t + 1 < ntiles:
                    dma_in(P, "pool", hT, mg_v[:, :, c0 + 512:c0 + 1024])
                for s in range(8):
                    sl = load_slab(wb_dn[layer, hf, s])
                    slv = sl.t[:].rearrange("p (k c) -> p k c", k=32)
                    for j in range(2):
                        c = 2 * s + j
                        pp = prot.next()
                        mm_group(P, pp.t[:, 0:n], pp.b,
                                 [(slv[:, kc, j * 128:(j + 1) * 128], hid.t[:, kc, 0:n]) for kc in range(32)],
                                 [sl.b, hid.b])
                        P.op("dve", lambda e, c=c, pp=pp, n=n: e.tensor_tensor(out=xt.t[:, c, 0:n], in0=pp.t[:, 0:n],
                                                                               in1=xt.t[:, c, 0:n], op=ALU.add),
                             reads=[pp.b, xt.b], writes=[xt.b])
            if layer == 1:
                outtoks.append(dma_out(P, xt, yT.rearrange("(k p) t -> p k t", p=128)[:, :, c0:c0 + n], xt.t[:, :, 0:n]))
                continue
            dma_out(P, xt, x1_d.rearrange("(k p) t -> p k t", p=128)[:, :, c0:c0 + n], xt.t[:, :, 0:n])
            ts_ = tsw.next()
            dma_in(P, "pool", ts_, t_swa[:, :, c0:c0 + n].rearrange("a p t -> p a t"), ts_.t[:, :, 0:n])
            xnorm(P, K, xt, hT, 32, n, ssq)
            if t + 1 < ntiles:
                c1 = (t + 1) * 512
                n1 = min(512, EXT - c1)
                dma_in(P, "pool", xt, xT_v[:, :, c1:c1 + n1], xt.t[:, :, 0:n1])
            nb = n // 128
            b0 = c0 // 128
            for s in range(5):
                sl = load_slab(wb_qkv1[s])
                slv = sl.t[:].rearrange("p (k c) -> p k c", k=16)
                nch = 4 if s < 4 else 2
                for j in range(nch):
                    pq = prot.next()
                    mm_group(P, pq.t[:, 0:n], pq.b,
                             [(slv[:, kc, j * 128:(j + 1) * 128], hT.t[:, kc, 0:n]) for kc in range(16)], [sl.b, hT.b])
                    ssq_ = sq_accum(P, K, [(pq.t[:, 0:n], pq.b, 128)], ssq, n, lhs=K.bd)
                    r = rstd_of(P, K, ssq_, 64.0, n)
                    psw = prot.next()
                    qs = qst.next()
                    pmtb = pm
                    apply_rope(P, K, pq, n, 128, pmtb, 80 if s < 4 else 82,
                               (ts_.t[:, 0, 0:n], ts_.b), (ts_.t[:, 1, 0:n], ts_.b), r, qs.t[:, 0:n], qs.b, psw, eng2="dve")
                    if s < 4:
                        c = 4 * s + j
                        for u in range(2):
                            h = 2 * c + u
                            kvh, hh = h // 8, h % 8
                            dma_out(P, qs, q1_d[kvh, :, b0:b0 + nb, hh, :],
                                    qs.t[u * 64:(u + 1) * 64, 0:n].rearrange("p (b q) -> p b q", q=128))
                    else:
                        for u in range(2):
                            dma_out(P, qs, k1_d[2 * j + u, :, c0:c0 + n], qs.t[u * 64:(u + 1) * 64, 0:n])
                if s == 4:
                    vs = vst.next()
                    for sb_ in range(nb):
                        pv = prot.next()
                        mm_group(P, pv.t[:, 0:256], pv.b,
                                 [(hT.t[:, kc, sb_ * 128:(sb_ + 1) * 128], slv[:, kc, 256:512]) for kc in range(16)],
                                 [sl.b, hT.b])
                        P.op("act", lambda e, vs=vs, sb_=sb_, pv=pv: e.activation(out=vs.t[:, sb_, :], in_=pv.t[:, 0:256],
                                                                                  func=AF.Copy),
                             reads=[pv.b], writes=[vs.b])
                    dma_out(P, vs, v1_d[:, b0:b0 + nb, :], vs.t[:, 0:nb, :])
        C.close()
        return outtoks

    tail_phase("p3", 0)

    C = Ctx(nc, sems, "p4")
    P = C.P
    ones = C.sb([128, 128], BF16, "ones")
    P.op("dve", lambda e: e.memset(ones.t[:], 1.0), writes=[ones.b])
    k1s = C.sb([64, 4, EXT], BF16, "k1s")
    dma_in(P, "sp", k1s, k1_d.rearrange("k p t -> p k t"))
    v1s = C.sb([128, 18, 256], BF16, "v1s")
    dma_in(P, "sp", v1s, v1_d)
    mks = C.sb([128, 4, 512], BF16, "mks")
    dma_in(P, "pool", mks, masks.rearrange("a p t -> p a t"))
    esk = C.sb([64, 32], F32, "esk")
    dma_in(P, "sp", esk, sinkrep)
    P.op("act", lambda e: e.activation(out=esk.t[:], in_=esk.t[:], func=AF.Exp), reads=[esk.b], writes=[esk.b])
    eskx = C.sb([64, 32, 128], F32, "eskx")
    P.op("dve", lambda e: e.memset(eskx.t[:], 0.0), writes=[eskx.b])
    for h in range(32):
        P.op("dve", lambda e, h=h: e.tensor_scalar(out=eskx.t[:, h, :], in0=eskx.t[:, h, :], scalar1=esk.t[:, h:h + 1],
                                                   scalar2=None, op0=ALU.add), reads=[esk.b, eskx.b], writes=[eskx.b])
    q1s = Rot([C.sb([64, 18, 8, 128], BF16, "q1s") for _ in range(2)])
    pts = Rot([C.sb([128, 1536], BF16, "pt") for _ in range(3)])
    dns = Rot([C.sb([64, 512], F32, "dn") for _ in range(2)])
    dls = Rot([C.sb([64, 512], F32, "dl") for _ in range(2)])
    ost = Rot([C.sb([64, 512], BF16, "ost") for _ in range(2)])
    Sp = C.ps([128, 1536], "S")
    opsr = Rot([C.ps([128, 512], "O") for _ in range(2)])
    dpsr = Rot([C.ps([128, 512], "Dn") for _ in range(2)])
    sc1 = 64.0 ** -0.5
    q1_of = {}
    its = []
    for kv in range(4):
        for jq in range(16):
            for hg in range(2):
                its.append((kv, jq, hg))

    def s_mm(it):
        kv, jq, hg = it
        if kv not in q1_of:
            q1 = q1s.next()
            dma_in(P, "sp", q1, q1_d[kv])
            q1_of[kv] = q1
        q1 = q1_of[kv]

        def fnS(e):
            ins = None
            for kb in range(3):
                ins = e.matmul(Sp.t[:, kb * 512:(kb + 1) * 512],
                               lhsT=k1s.t[:, kv, (jq + kb) * 128:(jq + kb + 1) * 128],
                               rhs=q1.t[:, jq + 1, hg * 4:(hg + 1) * 4, :].rearrange("p h q -> p (h q)"),
                               start=True, stop=True)
            return ins
        P.op("pe", fnS, reads=[k1s.b, q1.b], writes=[Sp.b])

    def exp_mask(it):
        kv, jq, hg = it
        pt = pts.next()
        P.op("act", lambda e: e.activation(out=pt.t[:, :], in_=Sp.t[:, :], func=AF.Exp, scale=sc1),
             reads=[Sp.b], writes=[pt.b])
        mP = 2 if jq == 0 else 0
        mN = 3 if jq == 15 else 1
        P.op("dve", lambda e: e.tensor_tensor(out=pt.t[:, 0:512], in0=pt.t[:, 0:512], in1=mks.t[:, mP, :],
                                              op=ALU.mult), reads=[pt.b, mks.b], writes=[pt.b])
        P.op("pool", lambda e: e.tensor_tensor(out=pt.t[:, 1024:1536], in0=pt.t[:, 1024:1536], in1=mks.t[:, mN, :],
                                               op=ALU.mult), reads=[pt.b, mks.b], writes=[pt.b])
        return pt

    def pv_norm(it, pt):
        kv, jq, hg = it
        Ops = opsr.next()
        Dps = dpsr.next()

        def fnP(e):
            ins = None
            for kb in range(3):
                e.matmul(Ops.t[0:64, :], lhsT=v1s.t[:, jq + kb, kv * 64:(kv + 1) * 64],
                         rhs=pt.t[:, kb * 512:(kb + 1) * 512], start=(kb == 0), stop=(kb == 2))
            for kb in range(3):
                ins = e.matmul(Dps.t[0:64, :], lhsT=ones.t[:, 0:64], rhs=pt.t[:, kb * 512:(kb + 1) * 512],
                               start=(kb == 0), stop=(kb == 2))
            return ins
        P.op("pe", fnP, reads=[v1s.b, pt.b, ones.b], writes=[Ops.b, Dps.b])
        dn = dns.next()
        h0 = kv * 8 + hg * 4
        P.op("dve", lambda e: e.tensor_tensor(
            out=dn.t[:, :], in0=Dps.t[0:64, :], in1=eskx.t[:, h0:h0 + 4, :].rearrange("p h q -> p (h q)"),
            op=ALU.add), reads=[Dps.b, eskx.b], writes=[dn.b])
        dl = dls.next()
        P.op("act", lambda e: e.activation(out=dl.t[:, :], in_=dn.t[:, :], func=AF.Ln),
             reads=[dn.b], writes=[dl.b])
        P.op("act", lambda e: e.activation(out=dn.t[:, :], in_=dl.t[:, :], func=AF.Exp, scale=-1.0),
             reads=[dl.b], writes=[dn.b])
        o = ost.next()
        P.op("dve", lambda e: e.tensor_tensor(out=o.t[:, :], in0=Ops.t[0:64, :], in1=dn.t[:, :], op=ALU.mult),
             reads=[Ops.b, dn.b], writes=[o.b])
        r0 = (kv * 8 + hg * 4) * 64
        dma_out(P, o, m1_d[r0:r0 + 256, jq * 128:(jq + 1) * 128].rearrange("(h d) q -> d h q", d=64),
                o.t[:, :].rearrange("p (h q) -> p h q", q=128), q="sp")

    s_mm(its[0])
    pt_cur = exp_mask(its[0])
    for i, it in enumerate(its):
        if i + 1 < len(its):
            s_mm(its[i + 1])
            pt_nxt = exp_mask(its[i + 1])
        else:
            pt_nxt = None
        pv_norm(it, pt_cur)
        pt_cur = pt_nxt
    C.close()

    tail_phase("p5", 1)
    es.close()
    return nc


def _slab(W, cols=None):
    W = np.asarray(W, np.float32)
    if cols is not None:
        W = W[:, cols]
    Kd, n = W.shape
    return np.ascontiguousarray(W.reshape(Kd // 128, 128, n).transpose(1, 0, 2).reshape(128, (Kd // 128) * n))


def _rope_tab(pos, dim, theta):
    inv = np.float32(theta) ** (-np.arange(0, dim, 2, dtype=np.float32) / np.float32(dim))
    ang = pos.astype(np.float32)[:, None] * inv[None, :].astype(np.float32)
    ang = ang.astype(np.float32).astype(np.float64)
    return np.cos(ang).astype(np.float32).T, np.sin(ang).astype(np.float32).T


def _perm_lhsT(perm):
    n = len(perm)
    m = np.zeros((128, 128), np.float32)
    for o in range(n):
        m[perm[o], o] = 1.0
    return m


_NC_CACHE = {}


def kernel(x, even_norm, even_w_in, mla_q_lat_norm, mla_kv_lat_norm, mla_w_uq, mla_w_ukv,
           mla_q_norm, mla_k_nope_norm, mla_k_rope_norm, gqa_q_norm, gqa_k_norm, even_w_out,
           odd_norm, odd_w_qkv, swa_q_norm, swa_k_norm, swa_sink, odd_w_out,
           mlp_norm, mlp_w_up, mlp_w_down):
    f = lambda a: np.asarray(a, np.float32)
    x = f(x)
    p_mla = np.concatenate([np.arange(32, 64), np.arange(0, 32)])
    p_gqa = np.concatenate([p_mla, 64 + p_mla])
    p_sw64 = np.concatenate([np.arange(8, 16), np.arange(0, 8), np.arange(16, 64)])
    p_swa = np.concatenate([p_sw64, 64 + p_sw64])
    perms = np.stack([_perm_lhsT(p_mla), _perm_lhsT(p_gqa), _perm_lhsT(p_swa)])
    g = np.ones((128, 84), np.float32)
    g[:, 0:16] = f(even_norm)[0].reshape(16, 128).T
    g[:, 16:32] = f(mlp_norm)[0].reshape(16, 128).T
    g[:, 32:48] = f(odd_norm)[0].reshape(16, 128).T
    g[:, 48:64] = f(mlp_norm)[1].reshape(16, 128).T
    g[:, 64:68] = f(mla_q_lat_norm)[0].reshape(4, 128).T
    g[:, 68:70] = f(mla_kv_lat_norm)[0].reshape(2, 128).T
    qn_ = f(mla_q_norm)[0]
    g[:, 70] = qn_[0:128]
    g[0:64, 71] = qn_[128:192]
    g[0:64, 72] = qn_[128:192][p_mla]
    g[:, 73] = f(mla_k_nope_norm)[0]
    kr_ = f(mla_k_rope_norm)[0]
    g[0:64, 74] = kr_
    g[0:64, 75] = kr_[p_mla]
    gq_ = f(gqa_q_norm)[0]
    g[:, 76] = gq_
    g[:, 77] = gq_[p_gqa]
    gk_ = f(gqa_k_norm)[0]
    g[:, 78] = gk_
    g[:, 79] = gk_[p_gqa]
    sq_ = np.tile(f(swa_q_norm)[0], 2)
    g[:, 80] = sq_
    g[:, 81] = sq_[p_swa]
    sk_ = np.tile(f(swa_k_norm)[0], 2)
    g[:, 82] = sk_
    g[:, 83] = sk_[p_swa]
    w_in = f(even_w_in)[0]
    cols_kv = np.concatenate([np.arange(512, 768), np.arange(768, 832), np.arange(1856, 2112), np.arange(2112, 2368)])
    cols_q = np.concatenate([np.arange(0, 512), np.arange(832, 1856)])
    wkv = _slab(w_in, cols_kv)
    wq0 = _slab(w_in, cols_q)
    wuq = _slab(f(mla_w_uq)[0])
    ukv = f(mla_w_ukv)[0]
    ck = np.concatenate([np.arange(h * 256, h * 256 + 128) for h in range(8)])
    cv = np.concatenate([np.arange(h * 256 + 128, h * 256 + 256) for h in range(8)])
    wuk = _slab(ukv, ck)
    wuv = _slab(ukv, cv)
    wo0 = np.stack([_slab(f(even_w_out)[0][:, s * 512:(s + 1) * 512]) for s in range(4)])
    wo1 = np.stack([_slab(f(odd_w_out)[0][:, s * 512:(s + 1) * 512]) for s in range(4)])
    wqkv1 = np.stack([_slab(f(odd_w_qkv)[0][:, s * 512:(s + 1) * 512]) for s in range(5)])
    up = f(mlp_w_up)
    wup = np.stack([np.stack([_slab(up[l][:, s * 512:(s + 1) * 512]) for s in range(16)]) for l in range(2)])
    dn = f(mlp_w_down)
    wdn = np.stack([np.stack([np.stack([_slab(dn[l][hf * 4096:(hf + 1) * 4096, s * 256:(s + 1) * 256])
                                        for s in range(8)]) for hf in range(2)]) for l in range(2)])
    sinkrep = np.ascontiguousarray(np.broadcast_to(f(swa_sink)[0][None, :], (64, 32)))
    kk = np.arange(128)[:, None]
    qq = np.arange(128)[None, :]
    mP = np.tile((kk >= qq).astype(np.float32), (1, 4))
    mN = np.tile((kk <= qq).astype(np.float32), (1, 4))

    in_maps = []
    for c in range(8):
        b, ch = c // 4, c % 4
        ext = list(range(16 * ch - 1, 16 * ch + 17))
        if ch == 0:
            ext[0] = 17
        if ch == 3:
            ext[17] = 46
        rest = [i for i in range(64) if i not in ext]
        order = np.array(ext + rest)
        tok = (order[:, None] * 128 + np.arange(128)[None, :]).reshape(-1)
        xTc = np.ascontiguousarray(x[b][tok].T)
        cm, sm = _rope_tab(tok, 64, 500000.0)
        t_mla = np.stack([np.concatenate([cm, cm]), np.concatenate([-sm, sm])])
        cr, sr = _rope_tab(tok // 64, 64, 10000.0)
        cc, sc = _rope_tab(tok % 64, 64, 10000.0)
        t_gqa = np.stack([np.concatenate([cr, cr, cc, cc]), np.concatenate([-sr, sr, -sc, sc])])
        te = tok[:EXT]
        cs, ss = _rope_tab(te, 16, 500000.0)
        one = np.ones((48, EXT), np.float32)
        zero = np.zeros((48, EXT), np.float32)
        c64 = np.concatenate([cs, cs, one])
        s64 = np.concatenate([-ss, ss, zero])
        t_swa = np.stack([np.concatenate([c64, c64]), np.concatenate([s64, s64])])
        masks = np.stack([mP, mN, mP * (0.0 if ch == 0 else 1.0), mN * (0.0 if ch == 3 else 1.0)])
        in_maps.append({
            "xT": xTc, "gcols": g, "w_kv": wkv, "w_q0": wq0, "w_uq": wuq, "w_uk": wuk, "w_uv": wuv,
            "w_o0": wo0, "w_up": wup, "w_dn": wdn, "w_qkv1": wqkv1, "w_o1": wo1,
            "t_mla": np.ascontiguousarray(t_mla, np.float32), "t_gqa": np.ascontiguousarray(t_gqa, np.float32),
            "t_swa": np.ascontiguousarray(t_swa, np.float32), "perms": perms,
            "masks": np.ascontiguousarray(masks, np.float32), "sinkrep": sinkrep,
        })
    if "nc" not in _NC_CACHE:
        _NC_CACHE["nc"] = build_program()
    nc = _NC_CACHE["nc"]
    res = run_bass_kernel_spmd(nc, in_maps, core_ids=list(range(8)))
    out = np.empty((2, S, D), np.float32)
    for c in range(8):
        b, ch = c // 4, c % 4
        out[b, ch * 2048:(ch + 1) * 2048, :] = res.results[c]["yT"].T
    return out
```
